# Optimizing a Trainium2 kernel written in Bass

```python
import jax, jax.numpy as jnp
from jax import lax
import numpy as np

D_MODEL = 1024
BATCH = 4
SEQ = 4096
DEPTH = 1

GRID_W = 64
CTX_LEN = 256
MLA_HEADS = 8
QK_NOPE = 64
QK_ROPE = 32
V_HEAD = 64
Q_LORA = 384
KV_LORA = 256
MLA_WIDTH = MLA_HEADS * V_HEAD
AXIS_DIM = QK_ROPE // 2
ROPE_BASE = 10000.0
SM_SCALE = (QK_NOPE + QK_ROPE) ** -0.5
Q_BLOCK = 128
CONV_CH = D_MODEL - MLA_WIDTH
CONV_WIDTH = 31
CONV_PAD = CONV_WIDTH // 2
D_MIX = MLA_WIDTH + CONV_CH
IN_COLS = Q_LORA + KV_LORA + QK_ROPE + 2 * CONV_CH
D_FF = 4 * D_MODEL
EPS = 1e-6
ALPHA = (2 * DEPTH) ** 0.25
BETA = (8 * DEPTH) ** -0.25

kernel_name = 'hybrid_mla_conformer_dit_block'


def layer_norm(x, g, b):
    xf = x.astype(jnp.float32)
    mu = jnp.mean(xf, -1, keepdims=True)
    var = jnp.mean(jnp.square(xf - mu), -1, keepdims=True)
    return ((xf - mu) * lax.rsqrt(var + EPS) * g + b).astype(x.dtype)


def rms_norm(x, g):
    xf = x.astype(jnp.float32)
    return (xf * lax.rsqrt(jnp.mean(jnp.square(xf), -1, keepdims=True) + EPS) * g).astype(x.dtype)


def axial_angles(n):
    rows = n // GRID_W
    row_id = jnp.repeat(jnp.arange(rows, dtype=jnp.float32), GRID_W)
    col_id = jnp.tile(jnp.arange(GRID_W, dtype=jnp.float32), rows)
    freqs = ROPE_BASE ** (-jnp.arange(0, AXIS_DIM, 2, dtype=jnp.float32) / AXIS_DIM)
    return row_id[:, None] * freqs, col_id[:, None] * freqs


def rotate_axis(x, ang):
    x1, x2 = jnp.split(x, 2, -1)
    cos, sin = jnp.cos(ang), jnp.sin(ang)
    return jnp.concatenate([x1 * cos - x2 * sin, x2 * cos + x1 * sin], -1)


def axial_rope(x, ang_row, ang_col):
    xr, xc = jnp.split(x, 2, -1)
    out = jnp.concatenate([rotate_axis(xr, ang_row), rotate_axis(xc, ang_col)], -1)
    return out.astype(x.dtype)


def split_in(z):
    return jnp.split(z, [Q_LORA, Q_LORA + KV_LORA, Q_LORA + KV_LORA + QK_ROPE], -1)


def mla_queries(c_q, q_norm_g, w_uq):
    b, n, _ = c_q.shape
    q = (rms_norm(c_q, q_norm_g) @ w_uq).reshape(b, n, MLA_HEADS, QK_NOPE + QK_ROPE)
    return q[..., :QK_NOPE], q[..., QK_NOPE:]


def mla_keys_values(c_kv, kv_norm_g, w_uk, w_uv):
    b, n, _ = c_kv.shape
    ckv = rms_norm(c_kv, kv_norm_g)
    k_nope = (ckv @ w_uk).reshape(b, n, MLA_HEADS, QK_NOPE)
    v = (ckv @ w_uv).reshape(b, n, MLA_HEADS, V_HEAD)
    return k_nope, v


def context_attention(q_nope, q_rope, k_nope, k_rope, v):
    s = (jnp.einsum('bqhd,bkhd->bhqk', q_nope, k_nope, preferred_element_type=jnp.float32)
         + jnp.einsum('bqhd,bkd->bhqk', q_rope, k_rope, preferred_element_type=jnp.float32))
    p = jax.nn.softmax(s * SM_SCALE, -1).astype(v.dtype)
    o = jnp.einsum('bhqk,bkhd->bqhd', p, v)
    b, n = o.shape[:2]
    return o.reshape(b, n, MLA_WIDTH)


def latent_attention(q_nope, q_rope_rot, q_rope_raw, k_nope, k_rope_rot, v, k_nope_c, k_rope_c, v_c):
    b, n = q_nope.shape[:2]
    nb = n // Q_BLOCK

    def to_blocks(t):
        return t.reshape(b, nb, Q_BLOCK, *t.shape[2:]).swapaxes(0, 1)

    def block(qs):
        qn, qr, qw = qs
        s_lat = (jnp.einsum('bqhd,bkhd->bhqk', qn, k_nope, preferred_element_type=jnp.float32)
                 + jnp.einsum('bqhd,bkd->bhqk', qr, k_rope_rot, preferred_element_type=jnp.float32))
        s_ctx = (jnp.einsum('bqhd,bkhd->bhqk', qn, k_nope_c, preferred_element_type=jnp.float32)
                 + jnp.einsum('bqhd,bkd->bhqk', qw, k_rope_c, preferred_element_type=jnp.float32))
        s = jnp.concatenate([s_lat, s_ctx], -1) * SM_SCALE
        p = jax.nn.softmax(s, -1).astype(v.dtype)
        return (jnp.einsum('bhqk,bkhd->bqhd', p[..., :n], v)
                + jnp.einsum('bhqk,bkhd->bqhd', p[..., n:], v_c))

    o = lax.map(block, (to_blocks(q_nope), to_blocks(q_rope_rot), to_blocks(q_rope_raw)))
    return o.swapaxes(0, 1).reshape(b, n, MLA_WIDTH)


def conformer_conv(u, w_dw, b_dw, ln_g, ln_b):
    a, gate = jnp.split(u, 2, -1)
    y = a * jax.nn.sigmoid(gate)
    y = lax.conv_general_dilated(y, w_dw[:, None, :], window_strides=(1,),
                                 padding=[(CONV_PAD, CONV_PAD)],
                                 dimension_numbers=('NWC', 'WIO', 'NWC'),
                                 feature_group_count=CONV_CH) + b_dw
    return jax.nn.silu(layer_norm(y, ln_g, ln_b))


def token_mixer(h, hc, w_in, q_norm_g, kv_norm_g, w_uq, w_uk, w_uv, w_dw, b_dw,
                conv_ln_g, conv_ln_b, w_out, ang_row, ang_col, need_ctx_out):
    c_q, c_kv, k_rope, u = split_in(h @ w_in)
    c_qc, c_kvc, k_rope_c, uc = split_in(hc @ w_in)
    q_nope, q_rope = mla_queries(c_q, q_norm_g, w_uq)
    k_nope, v = mla_keys_values(c_kv, kv_norm_g, w_uk, w_uv)
    k_nope_c, v_c = mla_keys_values(c_kvc, kv_norm_g, w_uk, w_uv)
    q_rope_rot = axial_rope(q_rope, ang_row[:, None, :], ang_col[:, None, :])
    k_rope_rot = axial_rope(k_rope, ang_row, ang_col)
    attn = latent_attention(q_nope, q_rope_rot, q_rope, k_nope, k_rope_rot, v,
                            k_nope_c, k_rope_c, v_c)
    conv = conformer_conv(u, w_dw, b_dw, conv_ln_g, conv_ln_b)
    y = jnp.concatenate([attn, conv], -1) @ w_out
    if not need_ctx_out:
        return y, None
    q_nope_c, q_rope_c = mla_queries(c_qc, q_norm_g, w_uq)
    attn_c = context_attention(q_nope_c, q_rope_c, k_nope_c, k_rope_c, v_c)
    conv_c = conformer_conv(uc, w_dw, b_dw, conv_ln_g, conv_ln_b)
    yc = jnp.concatenate([attn_c, conv_c], -1) @ w_out
    return y, yc


def sq_relu_mlp(h, w1, w2):
    return jnp.square(jax.nn.relu(h @ w1)) @ w2


def setup_inputs(seed: int = 0) -> dict:
    key = jax.random.key(seed)
    ks = jax.random.split(key, 24)
    f32 = jnp.float32
    L = DEPTH

    def nrm(k, shape, scale):
        return jax.random.normal(k, shape, f32) * scale

    return {
        'x': nrm(ks[0], (BATCH, SEQ, D_MODEL), 1.0),
        'c': nrm(ks[1], (BATCH, D_MODEL), 1.0),
        'ctx': nrm(ks[2], (BATCH, CTX_LEN, D_MODEL), 1.0),
        'c_ctx': nrm(ks[3], (D_MODEL,), 1.0),
        'ada_w': nrm(ks[4], (L, D_MODEL, 6 * D_MODEL), 0.3 * D_MODEL ** -0.5),
        'ada_b': nrm(ks[5], (L, 6 * D_MODEL), 0.01),
        'w_in': nrm(ks[6], (L, D_MODEL, IN_COLS), D_MODEL ** -0.5),
        'q_norm_g': 1.0 + nrm(ks[7], (L, Q_LORA), 0.02),
        'kv_norm_g': 1.0 + nrm(ks[8], (L, KV_LORA), 0.02),
        'w_uq': nrm(ks[9], (L, Q_LORA, MLA_HEADS * (QK_NOPE + QK_ROPE)), Q_LORA ** -0.5),
        'w_uk': nrm(ks[10], (L, KV_LORA, MLA_HEADS * QK_NOPE), KV_LORA ** -0.5),
        'w_uv': nrm(ks[11], (L, KV_LORA, MLA_WIDTH), BETA * KV_LORA ** -0.5),
        'w_dw': nrm(ks[12], (L, CONV_WIDTH, CONV_CH), CONV_WIDTH ** -0.5),
        'b_dw': nrm(ks[13], (L, CONV_CH), 0.01),
        'conv_ln_g': 1.0 + nrm(ks[14], (L, CONV_CH), 0.02),
        'conv_ln_b': nrm(ks[15], (L, CONV_CH), 0.01),
        'w_out': nrm(ks[16], (L, D_MIX, D_MODEL), BETA * D_MIX ** -0.5),
        'ln1_g': 1.0 + nrm(ks[17], (L, D_MODEL), 0.02),
        'ln1_b': nrm(ks[18], (L, D_MODEL), 0.01),
        'w1': nrm(ks[19], (L, D_MODEL, D_FF), BETA * D_MODEL ** -0.5),
        'w2': nrm(ks[20], (L, D_FF, D_MODEL), BETA * D_FF ** -0.5),
        'ln2_g': 1.0 + nrm(ks[21], (L, D_MODEL), 0.02),
        'ln2_b': nrm(ks[22], (L, D_MODEL), 0.01),
    }


def reference(x, c, ctx, c_ctx, ada_w, ada_b, w_in, q_norm_g, kv_norm_g, w_uq, w_uk, w_uv,
              w_dw, b_dw, conv_ln_g, conv_ln_b, w_out, ln1_g, ln1_b, w1, w2, ln2_g, ln2_b):
    ang_row, ang_col = axial_angles(x.shape[1])
    for l in range(DEPTH):
        last = l == DEPTH - 1
        mod = jax.nn.silu(c) @ ada_w[l] + ada_b[l]
        mod_c = jax.nn.silu(c_ctx) @ ada_w[l] + ada_b[l]
        sh1, sc1, g1, sh2, sc2, g2 = jnp.split(mod[:, None, :], 6, -1)
        csh1, csc1, cg1, csh2, csc2, cg2 = jnp.split(mod_c, 6, -1)
        h = x * (1 + sc1) + sh1
        hc = ctx * (1 + csc1) + csh1
        y, yc = token_mixer(h, hc, w_in[l], q_norm_g[l], kv_norm_g[l], w_uq[l], w_uk[l], w_uv[l],
                            w_dw[l], b_dw[l], conv_ln_g[l], conv_ln_b[l], w_out[l],
                            ang_row, ang_col, not last)
        x = layer_norm(ALPHA * x + g1 * y, ln1_g[l], ln1_b[l])
        h = x * (1 + sc2) + sh2
        x = layer_norm(ALPHA * x + g2 * sq_relu_mlp(h, w1[l], w2[l]), ln2_g[l], ln2_b[l])
        if not last:
            ctx = layer_norm(ALPHA * ctx + cg1 * yc, ln1_g[l], ln1_b[l])
            hc = ctx * (1 + csc2) + csh2
            ctx = layer_norm(ALPHA * ctx + cg2 * sq_relu_mlp(hc, w1[l], w2[l]), ln2_g[l], ln2_b[l])
    return x
```

```python
import numpy as np
import concourse.bass as bass
import concourse.mybir as mybir
from concourse.bass_utils import run_bass_kernel_spmd

F32 = mybir.dt.float32
BF16 = mybir.dt.bfloat16
AF = mybir.ActivationFunctionType
ALU = mybir.AluOpType

D = 1024
SEQ = 4096
NOWN = 2048
NCTX = 256
NK = 4352
NKT = 34
H = 8
EPS = 1e-6
ALPHA = 2.0 ** 0.25
SM_SCALE = 96.0 ** -0.5
KB = 1024
SB_BASE = 16512


class Eng:
    def __init__(self, h, sem):
        self.h = h
        self.sem = sem
        self.n = 0
        self.waited = {}


class T:
    def __init__(self, t=None):
        self.t = t
        self.w = None
        self.r = {}

    def __getitem__(self, k):
        return self.t[k]


class Chan:
    def __init__(self, sem):
        self.sem = sem
        self.n = 0


class Ctx:
    def __init__(self, nc, es):
        self.nc = nc
        self.es = es
        self.nsem = 0
        self.pe = Eng(nc.tensor, self.sem())
        self.act = Eng(nc.scalar, self.sem())
        self.dve = Eng(nc.vector, self.sem())
        self.pool = Eng(nc.gpsimd, self.sem())
        self.sp = Eng(nc.sync, self.sem())
        self.engs = [self.pe, self.act, self.dve, self.pool, self.sp]
        self.chans = []

    def sem(self):
        self.nsem += 1
        return self.es.enter_context(self.nc.semaphore(f"s{self.nsem}"))

    def chan(self):
        c = Chan(self.sem())
        self.chans.append(c)
        return c

    def _waits(self, eng, reads, writes, extra=()):
        need = {}

        def add(ev, same_ok):
            if ev is None:
                return
            sem, val = ev
            if (not same_ok) and sem is eng.sem:
                return
            k = id(sem)
            if k not in need or need[k][1] < val:
                need[k] = (sem, val)

        same = eng.h is not self.nc.tensor
        for t in reads:
            add(t.w, True)
        for t in writes:
            add(t.w, same)
            for ev in t.r.values():
                add(ev, same)
        for ev in extra:
            add(ev, True)
        for k, (sem, val) in need.items():
            if eng.waited.get(k, 0) < val:
                eng.h.wait_ge(sem, val)
                eng.waited[k] = val

    def op(self, eng, fn, reads=(), writes=(), sig=True, extra=()):
        self._waits(eng, reads, writes, extra)
        ins = fn()
        if sig:
            eng.n += 1
            ins.then_inc(eng.sem, 1)
            ev = (eng.sem, eng.n)
        else:
            ev = (eng.sem, eng.n + 1)
        for t in reads:
            t.r[id(eng.sem)] = ev
        for t in writes:
            t.w = ev
            t.r = {}
        return ev

    def dma(self, eng, ch, out, in_, reads=(), writes=(), extra=(), **kw):
        self._waits(eng, reads, writes, extra)
        ins = eng.h.dma_start(out=out, in_=in_, **kw)
        ch.n += 16
        ins.then_inc(ch.sem, 16)
        ev = (ch.sem, ch.n)
        for t in reads:
            t.r[id(ch.sem)] = ev
        for t in writes:
            t.w = ev
            t.r = {}
        return ev

    def snapshot(self):
        evs = {}
        for e in self.engs:
            if e.n > 0:
                evs[id(e.sem)] = (e.sem, e.n)
        for c in self.chans:
            if c.n > 0 and not getattr(c, "nofence", False):
                evs[id(c.sem)] = (c.sem, c.n)
        return evs

    def alias(self, new, old):
        ev = {}
        for t in old:
            for e in ([t.w] if t.w else []) + list(t.r.values()):
                k = id(e[0])
                if k not in ev or ev[k][1] < e[1]:
                    ev[k] = e
        for t in new:
            t.w = None
            t.r = dict(ev)

    def fence(self, tiles):
        snap = self.snapshot()
        for t in tiles:
            t.w = None
            t.r = dict(snap)


def build_program():
    from contextlib import ExitStack

    nc = bass.Bass("TRN2", target_bir_lowering=False)

    def din(name, shape, dt=F32):
        return nc.dram_tensor(name, list(shape), dt, kind="ExternalInput").ap()

    xT_d = din("xT", [D, NK])
    xown_d = din("xown", [NOWN, D])
    xTh_d = din("xTh", [D, 32])
    hmask_d = din("hmask", [128, 32])
    cv_d = din("cv", [128, 16])
    adaw_d = din("ada_w", [D, 6 * D])
    adabT_d = din("adabT", [128, 48])
    win_d = din("w_in_e", [D, 1792])
    wuq_d = din("w_uq_e", [384, 2048])
    wuk_d = din("w_uk", [256, 512])
    wuv_d = din("w_uv", [256, 512])
    qg_d = din("qg", [128, 3])
    kvg_d = din("kvg", [128, 2])
    wdw_d = din("wdw", [128, 4 * 31])
    cpar_d = din("cpar", [128, 12])
    wout_d = din("w_out", [D, D])
    w1_d = din("w1", [D, 4 * D])
    w2_d = din("w2", [4 * D, D])
    lnp_d = din("lnp", [4, D])
    ctab_d = din("ctab", [128, SEQ])
    stab_d = din("stab", [128, SEQ])
    i2_d = din("i2", [2, 2])
    lnT_d = din("lnT", [128, 16])
    ident_d = din("ident", [128, 128])
    out_d = nc.dram_tensor("out", [NOWN, D], F32, kind="ExternalOutput").ap()
    x1s_d = nc.dram_tensor("x1s", [NOWN, D], F32, kind="Internal").ap()
    h2s_d = nc.dram_tensor("h2s", [D, NOWN], BF16, kind="Internal").ap()
    w1b_d = nc.dram_tensor("w1b", [D, 4 * D], BF16, kind="Internal").ap()
    w2b_d = nc.dram_tensor("w2b", [4 * D, D], BF16, kind="Internal").ap()
    woutb_d = nc.dram_tensor("woutb", [D, D], BF16, kind="Internal").ap()

    with ExitStack() as es:
        cx = Ctx(nc, es)
        PE, ACT, DVE, POOL, SP = cx.pe, cx.act, cx.dve, cx.pool, cx.sp
        op, dma = cx.op, cx.dma

        cnt = [0]

        def sb(off_kb, shape, dt):
            cnt[0] += 1
            return T(nc.alloc_sbuf_tensor_at(f"t{cnt[0]}", list(shape), dt, offset=SB_BASE + int(off_kb * KB)))

        pairs = [es.enter_context(nc.psum_tensor(f"pp{i}", [128, 1024], F32)) for i in range(4)]
        banks = [T(pairs[i // 2][:, (i % 2) * 512:(i % 2 + 1) * 512]) for i in range(8)]
        prr = [0]

        def ps(pool=range(8)):
            pool = list(pool)
            b = banks[pool[prr[0] % len(pool)]]
            prr[0] += 1
            return b

        o = 0.0

        def small(shape, dt, nbytes):
            nonlocal o
            t = sb(o, shape, dt)
            o += (-(-nbytes // 32) * 32) / KB
            return t

        cv = small([128, 16], F32, 64)
        scv = small([128, 16], F32, 64)
        modT = small([128, 96], F32, 384)
        wdw = small([128, 124], F32, 496)
        cpar = small([128, 12], F32, 48)
        qg = small([128, 3], F32, 12 + 4)
        kvg = small([128, 2], F32, 8)
        ones_b = small([128, 128], BF16, 256)
        ident = small([128, 128], F32, 512)
        ones_f = small([128, 128], F32, 512)
        dg = small([128, 128], F32, 512)
        i2 = small([2, 2], F32, 8 + 24)
        hmask = small([128, 32], F32, 128)
        adabT = small([128, 48], F32, 192)
        lnT = small([128, 16], F32, 64)
        ab2 = small([128, 16], F32, 64)
        stat = small([128, 32], F32, 128)
        assert o * KB <= 3968, o * KB
        wuk_b = sb(4, [128, 2, 512], BF16)
        wuv_b = sb(6, [128, 2, 512], BF16)
        ckvn = sb(8, [128, 2, NK], BF16)
        QT = sb(25.5, [128, H, NOWN], BF16)
        KT0 = sb(57.5, [128, NK], BF16)
        ypad = sb(66.5, [128, 4, 2080], BF16)
        b0 = 83.5
        win_b = sb(b0, [128, 8, 1792], BF16)
        wuq_b = sb(b0 + 28, [128, 3, 2048], BF16)
        xs = [sb(b0 + 40 + 8 * i, [128, 4, 512], F32) for i in range(2)] + [sb(196, [128, 4, 512], F32)]
        hT = [sb(b0 + 56 + 8 * i, [128, 8, 512], BF16) for i in range(2)]
        craw = sb(b0 + 72, [128, 5, 512], F32)
        sq = sb(b0 + 82, [128, 5, 512], BF16)
        cqn = sb(b0 + 87, [128, 3, 512], BF16)
        ctb = [sb(b0 + 90 + 4 * i, [128, 512], F32) for i in range(2)]
        stb = [sb(b0 + 92 + 4 * i, [128, 512], F32) for i in range(2)]
        t1 = sb(b0 + 98, [128, 512], F32)
        t2 = sb(b0 + 100, [128, 512], F32)
        tB = sb(b0 + 102, [128, 512], F32)
        sig = [sb(b0 + 104 + 2 * i, [128, 512], F32) for i in range(2)]
        rs = sb(b0 + 108, [128, 512], F32)
        sd = sb(b0 + 110, [128, 512], F32)
        adaw_s = [sb(25.5 + 16 * i, [128, 8, 512], F32) for i in range(2)]
        wst = [sb(8 + 7 * i, [128, 1792], F32) for i in range(2)]
        wuq_s = sb(b0 + 72, [128, 3, 2048], F32)
        wukv_s = sb(b0 + 96, [128, 2, 1024], F32)

        cch = cx.chan()
        cl = []

        def cload(t, dst, src, eng=SP, **kw):
            dma(eng, cch, dst, src, writes=[t], **kw)
            cl.append(t)

        cload(cv, cv[:, :], cv_d)
        cload(adabT, adabT[:, :], adabT_d)
        cload(i2, i2[:, :], i2_d)
        cload(wdw, wdw[:, :], wdw_d)
        cload(cpar, cpar[:, :], cpar_d)
        cload(qg, qg[:, :], qg_d)
        cload(kvg, kvg[:, :], kvg_d)
        cload(ident, ident[:, :], ident_d)
        cload(hmask, hmask[:, :], hmask_d)
        cload(lnT, lnT[:, :], lnT_d)
        fin = (cch.sem, cch.n)
        for t in cl:
            t.w = fin

        op(POOL, lambda: nc.gpsimd.memset(ones_b[:, :], 1.0), writes=[ones_b])
        op(POOL, lambda: nc.gpsimd.memset(ones_f[:, :], 1.0), writes=[ones_f])
        op(ACT, lambda: nc.scalar.activation(out=scv[:, :], in_=cv[:, :], func=AF.Silu), reads=[cv], writes=[scv])

        def mod_dest(sc):
            if sc < 16:
                return sc
            if sc < 24:
                return 32 + (sc - 16)
            if sc < 40:
                return 16 + (sc - 24)
            return 40 + (sc - 40)

        def mod_mm(st_, co, sc):
            pm = ps()
            for k in range(8):
                op(PE, lambda k=k: nc.tensor.matmul(pm[0:2, 0:128], lhsT=scv[:, 2 * k:2 * k + 2], rhs=st_[:, k, co:co + 128],
                                                    start=(k == 0), stop=(k == 7)), reads=[scv, st_], writes=[pm], sig=(k == 7))
            return pm

        def mod_fin(pm, sc):
            op(DVE, lambda: nc.vector.tensor_copy(out=dg[0:2, 0:128], in_=pm[0:2, 0:128]), reads=[pm], writes=[dg])
            pt_ = ps()
            op(PE, lambda: nc.tensor.matmul(pt_[:, 0:2], lhsT=dg[0:2, 0:128], rhs=i2[0:2, 0:2], start=True, stop=True),
               reads=[dg, i2], writes=[pt_])
            j = mod_dest(sc)
            one = 1.0 if (8 <= j < 16 or 24 <= j < 32) else 0.0
            op(DVE, lambda: nc.vector.tensor_scalar(out=modT[:, 2 * j:2 * j + 2], in0=pt_[:, 0:2], scalar1=adabT[:, sc:sc + 1],
                                                    scalar2=one, op0=ALU.add, op1=ALU.add), reads=[pt_, adabT], writes=[modT])

        ach = [cx.chan(), cx.chan()]
        wsch = [cx.chan(), cx.chan()]
        xch = [cx.chan(), cx.chan(), cx.chan()]
        tch = [cx.chan(), cx.chan()]
        xT_v = xT_d.rearrange("(k p) n -> p k n", p=128)
        pendA = []

        def modA_fin(pmA, c):
            op(DVE, lambda: nc.vector.tensor_copy(out=rs[0:2, :], in_=pmA[0:2, :]), reads=[pmA], writes=[rs])
            pt_ = ps()
            for q in range(4):
                op(PE, lambda q=q: nc.tensor.matmul(pt_[:, 2 * q:2 * q + 2], lhsT=rs[0:2, q * 128:(q + 1) * 128], rhs=i2[0:2, 0:2],
                                                    start=True, stop=True), reads=[rs, i2], writes=[pt_], sig=(q == 3))
            for q in range(4):
                sc = 4 * c + q
                j = mod_dest(sc)
                one = 1.0 if (8 <= j < 16 or 24 <= j < 32) else 0.0
                op(DVE, lambda q=q, sc=sc, j=j, one=one: nc.vector.tensor_scalar(
                    out=modT[:, 2 * j:2 * j + 2], in0=pt_[:, 2 * q:2 * q + 2], scalar1=adabT[:, sc:sc + 1], scalar2=one,
                    op0=ALU.add, op1=ALU.add), reads=[pt_, adabT], writes=[modT])
        adaw_v = adaw_d.rearrange("(k p) c -> p k c", p=128)
        for c in range(4):
            st_ = adaw_s[c % 2]
            for hh in range(2):
                dma(SP, ach[c % 2], st_[:, 4 * hh:4 * hh + 4, :], adaw_v[:, 4 * hh:4 * hh + 4, c * 512:(c + 1) * 512],
                    writes=[st_])
            if c == 0:
                for hh in range(2):
                    dma(SP, xch[hh], xs[hh][:, :, :], xT_v[:, 4 * hh:4 * hh + 4, 0:512], writes=[xs[hh]])
            for pc in (2 * c, 2 * c + 1):
                ws_ = wst[pc % 2]
                dma(SP, wsch[pc % 2], ws_[:, :], win_d[pc * 128:(pc + 1) * 128, :], writes=[ws_])
                op(DVE, lambda pc=pc, ws_=ws_: nc.vector.tensor_copy(out=win_b[:, pc, :], in_=ws_[:, :]),
                   reads=[ws_], writes=[win_b])
            pmA = ps()
            for k in range(8):
                op(PE, lambda k=k: nc.tensor.matmul(pmA[0:2, :], lhsT=scv[:, 2 * k:2 * k + 2], rhs=st_[:, k, :],
                                                    start=(k == 0), stop=(k == 7)), reads=[scv, st_], writes=[pmA], sig=(k == 7))
            if pendA:
                modA_fin(*pendA.pop(0))
            pendA.append((pmA, c))
        while pendA:
            modA_fin(*pendA.pop(0))
        cload2 = cx.chan()
        dma(SP, cload2, wuq_s[:, :, :], wuq_d.rearrange("(k p) c -> p k c", p=128), writes=[wuq_s])
        dma(SP, cload2, wukv_s[:, :, 0:512], wuk_d.rearrange("(k p) c -> p k c", p=128), writes=[wukv_s])
        dma(SP, cload2, wukv_s[:, :, 512:1024], wuv_d.rearrange("(k p) c -> p k c", p=128), writes=[wukv_s])
        wuq_s.w = wukv_s.w = (cload2.sem, cload2.n)

        for k in range(3):
            op(DVE, lambda k=k: nc.vector.tensor_scalar(out=wuq_b[:, k, :], in0=wuq_s[:, k, :],
                                                         scalar1=qg[:, k:k + 1], scalar2=None, op0=ALU.mult),
               reads=[wuq_s, qg], writes=[wuq_b])
        for k in range(2):
            op(DVE, lambda k=k: nc.vector.tensor_scalar(out=wuk_b[:, k, :], in0=wukv_s[:, k, 0:512],
                                                         scalar1=kvg[:, k:k + 1], scalar2=None, op0=ALU.mult),
               reads=[wukv_s, kvg], writes=[wuk_b])
            op(DVE, lambda k=k: nc.vector.tensor_scalar(out=wuv_b[:, k, :], in0=wukv_s[:, k, 512:1024],
                                                         scalar1=kvg[:, k:k + 1], scalar2=None, op0=ALU.mult),
               reads=[wukv_s, kvg], writes=[wuv_b])

        bst = [sb(196 + 4 * i, [128, 8, 128], F32) for i in range(2)]
        rowB = [sb(204 + 0.5 * i, [2, 128], F32) for i in range(2)]
        bch_ = [cx.chan(), cx.chan()]
        modB = list(range(16, 48))
        modB_pend = []

        modB_p = []

        def modB_one():
            prev = modB_p.pop(0) if modB_p else None
            if modB:
                sc = modB.pop(0)
                st_ = bst[sc % 2]
                dma(SP, bch_[sc % 2], st_[:, :, :], adaw_v[:, :, sc * 128:(sc + 1) * 128], writes=[st_])
                pm = banks[6 + (sc % 2)]
                for k in range(8):
                    op(PE, lambda k=k: nc.tensor.matmul(pm[0:2, 0:128], lhsT=scv[:, 2 * k:2 * k + 2], rhs=st_[:, k, 0:128],
                                                        start=(k == 0), stop=(k == 7)), reads=[scv, st_], writes=[pm], sig=(k == 7))
                op(DVE, lambda: nc.vector.tensor_copy(out=rowB[sc % 2][0:2, 0:128], in_=pm[0:2, 0:128]), reads=[pm], writes=[rowB[sc % 2]])
                modB_p.append(sc)
            if prev is not None:
                sc = prev
                pt_ = banks[6 + (sc % 2)]
                op(PE, lambda: nc.tensor.matmul(pt_[:, 0:2], lhsT=rowB[sc % 2][0:2, 0:128], rhs=i2[0:2, 0:2], start=True, stop=True),
                   reads=[rowB[sc % 2], i2], writes=[pt_])
                j = mod_dest(sc)
                one = 1.0 if (8 <= j < 16 or 24 <= j < 32) else 0.0
                op(DVE, lambda: nc.vector.tensor_scalar(out=modT[:, 2 * j:2 * j + 2], in0=pt_[:, 0:2], scalar1=adabT[:, sc:sc + 1],
                                                        scalar2=one, op0=ALU.add, op1=ALU.add), reads=[pt_, adabT], writes=[modT])

        def modB_batch(n):
            pend = []
            for _ in range(n):
                if not modB:
                    break
                sc = modB.pop(0)
                st_ = bst[sc % 2]
                dma(POOL, bch_[sc % 2], st_[:, :, :], adaw_v[:, :, sc * 128:(sc + 1) * 128], writes=[st_])
                pend.append((mod_mm(st_, 0, sc), sc))
                if len(pend) > 1:
                    mod_fin(*pend.pop(0))
            while pend:
                mod_fin(*pend.pop(0))

        def mod_ap(kind, k, r):
            base = {"sh1": 0, "sc1": 8, "sh2": 16, "sc2": 24}[kind]
            c = 2 * (base + k) + r
            return modT[:, c:c + 1]

        def emit_precast():
            jobs = [(woutb_d[:, :], wout_d[:, :], woutbT)]
            jobs += [(w1b_d[:, c * 512:(c + 1) * 512], w1_d[:, c * 512:(c + 1) * 512], w1bT) for c in range(8)]
            jobs += [(w2b_d[c * 512:(c + 1) * 512, :], w2_d[c * 512:(c + 1) * 512, :], w2bT) for c in range(8)]
            for i, (dst, src, tl) in enumerate(jobs):
                ch = pcw[i % 2]
                prev = [(ch.sem, ch.n)] if ch.n > 0 else []
                ev = dma(POOL, ch, dst, src, extra=prev)
                tl.evs = getattr(tl, "evs", {})
                tl.evs[id(ch.sem)] = ev

        w1bT, w2bT, woutbT = T(), T(), T()
        pcw = [cx.chan(), cx.chan()]
        for c_ in pcw:
            c_.nofence = True

        cx.fence([QT, KT0, ypad, ckvn, craw, sq, cqn, ctb[0], ctb[1], stb[0], stb[1], t1, t2, tB])
        op(POOL, lambda: nc.gpsimd.memset(KT0[64:128, :], 0.0), writes=[KT0])
        op(POOL, lambda: nc.gpsimd.memset(ypad[:, :, :], 0.0), writes=[ypad])


        def modulate(g, ntok, src_v, col0, lat):
            h_ = hT[g % 2]
            r = 0 if lat else 1
            for hh in range(2):
                x_ = xs[hh]
                dma(SP, xch[hh], x_[:, :, 0:ntok], src_v[:, 4 * hh:4 * hh + 4, col0:col0 + ntok], writes=[x_])
                for kk in range(4):
                    k = 4 * hh + kk
                    op(ACT, lambda k=k, kk=kk, x_=x_, h_=h_: nc.scalar.activation(
                        out=h_[:, k, 0:ntok], in_=x_[:, kk, 0:ntok], func=AF.Identity,
                        scale=mod_ap("sc1", k, r), bias=mod_ap("sh1", k, r)),
                       reads=[x_, modT], writes=[h_])
            return h_

        def proj(h_, m, ntok, pool=range(8)):
            p_ = ps(pool)
            for k in range(8):
                op(PE, lambda k=k, p_=p_: nc.tensor.matmul(
                    p_[:, 0:ntok], lhsT=win_b[:, k, m * 128:(m + 1) * 128], rhs=h_[:, k, 0:ntok],
                    start=(k == 0), stop=(k == 7)), reads=[win_b, h_], writes=[p_], sig=(k == 7))
            return p_

        def rms_chain(j0, nj, ntok, width):
            pss = ps()
            for j in range(nj):
                op(PE, lambda j=j: nc.tensor.matmul(pss[:, 0:ntok], lhsT=ones_b[:, :], rhs=sq[:, j0 + j, 0:ntok],
                                                    start=(j == 0), stop=(j == nj - 1)),
                   reads=[ones_b, sq], writes=[pss], sig=(j == nj - 1))
            op(ACT, lambda: nc.scalar.activation(out=sd[:, 0:ntok], in_=pss[:, 0:ntok], func=AF.Ln,
                                                 scale=1.0 / width, bias=eps_ap), reads=[pss, epsT], writes=[sd])
            op(ACT, lambda: nc.scalar.activation(out=rs[:, 0:ntok], in_=sd[:, 0:ntok], func=AF.Exp, scale=-0.5),
               reads=[sd], writes=[rs])

        epsT = T(nc.alloc_sbuf_tensor_at("epsT", [128, 2], F32, offset=SB_BASE + 3968))
        op(POOL, lambda: nc.gpsimd.memset(epsT[:, 0:1], EPS), writes=[epsT])
        op(POOL, lambda: nc.gpsimd.memset(epsT[:, 1:2], 1.0), writes=[epsT])
        eps_ap = epsT[:, 0:1]
        one_ap = epsT[:, 1:2]

        groups = [(g, 512, g * 512, True) for g in range(8)] + [(8, 256, 4096, False)]

        def mod_dma(gi):
            g, ntok, col0, lat = groups[gi]
            if gi > 0:
                for hh in range(2):
                    bi = (2 * gi + hh) % 3
                    x_ = xs[bi]
                    dma(SP, xch[bi], x_[:, :, 0:ntok], xT_v[:, 4 * hh:4 * hh + 4, col0:col0 + ntok], writes=[x_])

        def tab_dma(gi):
            g, ntok, col0, lat = groups[gi]
            if lat:
                dma(SP, tch[g % 2], ctb[g % 2][:, :], ctab_d[:, col0:col0 + 512], writes=[ctb[g % 2]])
                dma(SP, tch[g % 2], stb[g % 2][:, :], stab_d[:, col0:col0 + 512], writes=[stb[g % 2]])
                ctb[g % 2].w = stb[g % 2].w = (tch[g % 2].sem, tch[g % 2].n)

        def mod_act(gi):
            g, ntok, col0, lat = groups[gi]
            h_ = hT[g % 2]
            r = 0 if lat else 1
            for k in range(8):
                x_ = xs[(2 * gi + k // 4) % 3]
                op(ACT, lambda k=k, x_=x_: nc.scalar.activation(
                    out=h_[:, k, 0:ntok], in_=x_[:, k % 4, 0:ntok], func=AF.Identity,
                    scale=mod_ap("sc1", k, r), bias=mod_ap("sh1", k, r)), reads=[x_, modT], writes=[h_])

        def kv_chunks(h_, ntok):
            for j, m in enumerate((3, 4)):
                p_ = proj(h_, m, ntok)
                op(ACT, lambda j=j, p_=p_: nc.scalar.activation(out=craw[:, j, 0:ntok], in_=p_[:, 0:ntok], func=AF.Copy),
                   reads=[p_], writes=[craw])
                op(ACT, lambda j=j, p_=p_: nc.scalar.activation(out=sq[:, j, 0:ntok], in_=p_[:, 0:ntok], func=AF.Square),
                   reads=[p_], writes=[sq])

        def rope_chunk(h_, ntok, col0, lat, ct, st):
            pr = proj(h_, 5, ntok)
            if lat:
                op(ACT, lambda: nc.scalar.activation(out=tB[64:96, :], in_=pr[96:128, :], func=AF.Copy),
                   reads=[pr], writes=[tB])
                op(DVE, lambda: nc.vector.tensor_tensor(out=t1[64:96, :], in0=pr[64:96, :], in1=ct[64:96, :], op=ALU.mult),
                   reads=[pr, ct], writes=[t1])
                op(DVE, lambda: nc.vector.tensor_tensor(out=t2[64:96, :], in0=tB[64:96, :], in1=st[64:96, :], op=ALU.mult),
                   reads=[tB, st], writes=[t2])
                op(DVE, lambda: nc.vector.tensor_tensor(out=KT0[64:96, col0:col0 + 512], in0=t1[64:96, :], in1=t2[64:96, :],
                                                        op=ALU.add), reads=[t1, t2], writes=[KT0])
            else:
                op(ACT, lambda: nc.scalar.activation(out=KT0[96:128, col0:col0 + ntok], in_=pr[64:96, 0:ntok], func=AF.Copy),
                   reads=[pr], writes=[KT0])

        def kv_norm(ntok, col0):
            rms_chain(0, 2, ntok, 256.0)
            for j in range(2):
                op(DVE, lambda j=j: nc.vector.tensor_tensor(out=ckvn[:, j, col0:col0 + ntok], in0=craw[:, j, 0:ntok],
                                                            in1=rs[:, 0:ntok], op=ALU.mult),
                   reads=[craw, rs], writes=[ckvn])

        def q_chunks(h_):
            for j in range(3):
                p_ = proj(h_, j, 512)
                op(ACT, lambda j=j, p_=p_: nc.scalar.activation(out=craw[:, 2 + j, :], in_=p_[:, :], func=AF.Copy),
                   reads=[p_], writes=[craw])
                op(ACT, lambda j=j, p_=p_: nc.scalar.activation(out=sq[:, 2 + j, :], in_=p_[:, :], func=AF.Square),
                   reads=[p_], writes=[sq])

        def glu_chunk(h_, cc, col0):
            pa = proj(h_, 6 + cc, 512)
            pg = proj(h_, 10 + cc, 512)
            sg = sig[cc % 2]
            op(ACT, lambda: nc.scalar.activation(out=sg[:, :], in_=pg[:, :], func=AF.Sigmoid), reads=[pg], writes=[sg])
            op(DVE, lambda: nc.vector.tensor_tensor(out=ypad[:, cc, 15 + col0:15 + col0 + 512], in0=pa[:, :], in1=sg[:, :],
                                                    op=ALU.mult), reads=[pa, sg], writes=[ypad])

        def q_heads(col0, ct, st):
            for h in range(H):
                pA, pB = ps(), ps()
                for (p_, cidx) in ((pA, 2 * h), (pB, 2 * h + 1)):
                    for k in range(3):
                        op(PE, lambda k=k, p_=p_, cidx=cidx: nc.tensor.matmul(
                            p_[:, :], lhsT=wuq_b[:, k, cidx * 128:(cidx + 1) * 128], rhs=cqn[:, k, :],
                            start=(k == 0), stop=(k == 2)), reads=[wuq_b, cqn], writes=[p_], sig=(k == 2))
                op(ACT, lambda: nc.scalar.activation(out=QT[0:64, h, col0:col0 + 512], in_=pA[0:64, :], func=AF.Copy),
                   reads=[pA], writes=[QT])
                op(ACT, lambda: nc.scalar.activation(out=QT[96:128, h, col0:col0 + 512], in_=pA[96:128, :], func=AF.Copy),
                   reads=[pA], writes=[QT])
                op(DVE, lambda: nc.vector.tensor_tensor(out=t1[64:96, :], in0=pA[64:96, :], in1=ct[64:96, :], op=ALU.mult),
                   reads=[pA, ct], writes=[t1])
                op(DVE, lambda: nc.vector.tensor_tensor(out=t2[64:96, :], in0=pB[64:96, :], in1=st[64:96, :], op=ALU.mult),
                   reads=[pB, st], writes=[t2])
                op(DVE, lambda: nc.vector.tensor_tensor(out=QT[64:96, h, col0:col0 + 512], in0=t1[64:96, :], in1=t2[64:96, :],
                                                        op=ALU.add), reads=[t1, t2], writes=[QT])

        mod_dma(0)
        tab_dma(0)
        mod_act(0)
        mod_dma(1)
        for gi, (g, ntok, col0, lat) in enumerate(groups):
            own = g < 4
            h_ = hT[g % 2]
            ct, st = ctb[g % 2], stb[g % 2]
            if gi + 1 < len(groups):
                mod_act(gi + 1)
            if gi + 2 < len(groups):
                mod_dma(gi + 2)
            if gi + 1 < len(groups):
                tab_dma(gi + 1)
            kv_chunks(h_, ntok)
            rope_chunk(h_, ntok, col0, lat, ct, st)
            if own:
                q_chunks(h_)
                glu_chunk(h_, 0, col0)
                glu_chunk(h_, 1, col0)
                kv_norm(ntok, col0)
                glu_chunk(h_, 2, col0)
                rms_chain(2, 3, 512, 384.0)
                for j in range(3):
                    op(DVE, lambda j=j: nc.vector.tensor_tensor(out=cqn[:, j, :], in0=craw[:, 2 + j, :], in1=rs[:, :], op=ALU.mult),
                       reads=[craw, rs], writes=[cqn])
                glu_chunk(h_, 3, col0)
                q_heads(col0, ct, st)
            else:
                kv_norm(ntok, col0)
        xTh_v = xTh_d.rearrange("(k p) n -> p k n", p=128)
        h_ = modulate(9, 32, xTh_v, 0, True)
        for cc in range(4):
            pa = proj(h_, 6 + cc, 32)
            pg = proj(h_, 10 + cc, 32)
            sg = sig[cc % 2]
            op(ACT, lambda pg=pg, sg=sg: nc.scalar.activation(out=sg[:, 0:32], in_=pg[:, 0:32], func=AF.Sigmoid),
               reads=[pg], writes=[sg])
            op(DVE, lambda pa=pa, sg=sg: nc.vector.tensor_tensor(out=t1[:, 0:32], in0=pa[:, 0:32], in1=sg[:, 0:32], op=ALU.mult),
               reads=[pa, sg], writes=[t1])
            op(DVE, lambda cc=cc: nc.vector.tensor_tensor(out=ypad[:, cc, 0:15], in0=t1[:, 0:15], in1=hmask[:, 0:15], op=ALU.mult),
               reads=[t1, hmask], writes=[ypad])
            op(DVE, lambda cc=cc: nc.vector.tensor_tensor(out=ypad[:, cc, 15 + NOWN:30 + NOWN], in0=t1[:, 15:30],
                                                          in1=hmask[:, 15:30], op=ALU.mult),
               reads=[t1, hmask], writes=[ypad])

        a0 = 83.5
        KT1 = sb(a0, [128, NK], BF16)
        Vaug = [sb(a0 + 8.5 + 8.5 * i, [128, NKT, 128], BF16) for i in range(2)]
        PT = [sb(a0 + 25.5 + 2 * i, [128, 1024], BF16) for i in range(3)]
        rd = sb(a0 + 31.5, [128, 512], F32)
        zc = sb(a0 + 33.5, [128, 4, 512], F32)
        zb = sb(a0 + 41.5, [128, 4, 512], BF16)
        zsq = sb(a0 + 45.5, [128, 4, 512], BF16)
        cmean = sb(a0 + 49.5, [128, 512], F32)
        cmsq = sb(a0 + 51.5, [128, 512], F32)
        crs = sb(a0 + 53.5, [128, 512], F32)
        ctmp = [sb(a0 + 55.5 + 2 * i, [128, 512], F32) for i in range(2)]
        ngl = sb(a0 + 59.5, [128, 8], F32)
        attnT = sb(146, [128, 4, NOWN], BF16)
        convT = sb(162, [128, 4, NOWN], BF16)
        wout_b = sb(178, [128, 8, 1024], BF16)
        bc1 = sb(194, [128, 3, 1024], F32)
        spair = [T(pairs[1][:, :]), T(pairs[2][:, :])]
        att_tiles = [KT1, Vaug[0], Vaug[1], rd, zc, zb, zsq, cmean, cmsq, crs, ctmp[0], ctmp[1], ngl,
                     attnT, convT, wout_b] + PT + spair
        cx.fence(att_tiles)
        cx.alias([bst[0], bst[1]], [xs[2]])

        woch = cx.chan()
        KTb = [KT0, KT1]
        op(POOL, lambda: nc.gpsimd.memset(Vaug[0][:, :, 64:128], 1.0), writes=[Vaug[0]])
        op(POOL, lambda: nc.gpsimd.memset(Vaug[1][:, :, 0:64], 1.0), writes=[Vaug[1]])
        op(POOL, lambda: nc.gpsimd.tensor_copy(out=KT1[64:128, :], in_=KT0[64:128, :]), reads=[KT0], writes=[KT1])
        op(POOL, lambda: nc.gpsimd.tensor_scalar(out=ngl[:, :], in0=cpar[:, 4:12], scalar1=-1.0, scalar2=None, op0=ALU.mult),
           reads=[cpar], writes=[ngl])
        emit_precast()

        kgroups = [(i * 512, 512) for i in range(8)] + [(4096, 256)]
        MISC = [6, 7]

        def prep_k(h, c0, n):
            kt_ = KTb[h % 2]
            p_ = ps(MISC)
            for k in range(2):
                op(PE, lambda k=k: nc.tensor.matmul(
                    p_[0:64, 0:n], lhsT=wuk_b[:, k, h * 64:(h + 1) * 64], rhs=ckvn[:, k, c0:c0 + n],
                    start=(k == 0), stop=(k == 1)), reads=[wuk_b, ckvn], writes=[p_], sig=(k == 1))
            op(DVE, lambda: nc.vector.tensor_copy(out=kt_[0:64, c0:c0 + n], in_=p_[0:64, 0:n]), reads=[p_], writes=[kt_])

        def prep_v(h, t0):
            va_ = Vaug[h % 2]
            voff = 0 if h % 2 == 0 else 64
            nt = min(8, NKT - t0)
            p_ = ps(MISC)
            for j in range(nt):
                kt = t0 + j
                for k in range(2):
                    op(PE, lambda k=k, j=j, kt=kt: nc.tensor.matmul(
                        p_[:, j * 64:(j + 1) * 64], lhsT=ckvn[:, k, kt * 128:(kt + 1) * 128],
                        rhs=wuv_b[:, k, h * 64:(h + 1) * 64], start=(k == 0), stop=(k == 1), skip_group_check=True),
                       reads=[ckvn, wuv_b], writes=[p_], sig=(k == 1 and j == nt - 1))
            op(DVE, lambda: nc.vector.tensor_copy(
                out=va_[:, t0:t0 + nt, voff:voff + 64],
                in_=p_[:, 0:nt * 64].rearrange("p (t d) -> p t d", d=64)), reads=[p_], writes=[va_])

        def prep_units(h):
            us = [lambda c0=c0, n=n: prep_k(h, c0, n) for (c0, n) in kgroups]
            us += [lambda t0=t0: prep_v(h, t0) for t0 in range(0, NKT, 8)]
            return us

        tick = [0]
        defq = []

        seqc = [0]

        def at(delay, fn):
            seqc[0] += 1
            defq.append((tick[0] + delay, seqc[0], fn))

        def run_due(flush=False):
            while True:
                due = [e for e in defq if flush or e[0] <= tick[0]]
                if not due:
                    break
                due.sort()
                e = due[0]
                defq.remove(e)
                e[2]()

        zc2 = sb(178, [128, 4, 512], F32)
        zcb = [zc, zc2]
        cx.fence([zc2])

        def conv_tap(tg, cc, j):
            c0 = tg * 512
            z_ = zcb[tg % 2]
            if j == 0:
                op(DVE, lambda: nc.vector.tensor_scalar(out=z_[:, cc, :], in0=ypad[:, cc, c0:c0 + 512],
                                                        scalar1=wdw[:, cc * 31:cc * 31 + 1], scalar2=cpar[:, cc:cc + 1],
                                                        op0=ALU.mult, op1=ALU.add),
                   reads=[ypad, wdw, cpar], writes=[z_])
            else:
                op(DVE, lambda: nc.vector.scalar_tensor_tensor(
                    out=z_[:, cc, :], in0=ypad[:, cc, c0 + j:c0 + j + 512], scalar=wdw[:, cc * 31 + j:cc * 31 + j + 1],
                    in1=z_[:, cc, :], op0=ALU.mult, op1=ALU.add), reads=[ypad, z_, wdw], writes=[z_])

        def conv_sq(tg, cc):
            z_ = zcb[tg % 2]
            op(DVE, lambda: nc.vector.tensor_copy(out=zb[:, cc, :], in_=z_[:, cc, :]), reads=[z_], writes=[zb])
            op(DVE, lambda: nc.vector.tensor_tensor(out=zsq[:, cc, :], in0=z_[:, cc, :], in1=z_[:, cc, :], op=ALU.mult),
               reads=[z_], writes=[zsq])

        s12 = [None, None]

        def ln_pe():
            s12[0], s12[1] = ps(MISC), ps(MISC)
            for (s_, src) in ((s12[0], zb), (s12[1], zsq)):
                for cc in range(4):
                    op(PE, lambda cc=cc, s_=s_, src=src: nc.tensor.matmul(
                        s_[:, :], lhsT=ones_b[:, :], rhs=src[:, cc, :], start=(cc == 0), stop=(cc == 3)),
                       reads=[ones_b, src], writes=[s_], sig=(cc == 3))

        def ln_stats():
            s1, s2 = s12
            op(DVE, lambda: nc.vector.tensor_scalar(out=cmean[:, :], in0=s1[:, :], scalar1=1.0 / 512, scalar2=None, op0=ALU.mult),
               reads=[s1], writes=[cmean])
            op(DVE, lambda: nc.vector.tensor_tensor(out=cmsq[:, :], in0=cmean[:, :], in1=cmean[:, :], op=ALU.mult),
               reads=[cmean], writes=[cmsq])
            op(DVE, lambda: nc.vector.scalar_tensor_tensor(out=cmsq[:, :], in0=s2[:, :], scalar=1.0 / 512, in1=cmsq[:, :],
                                                           op0=ALU.mult, op1=ALU.subtract), reads=[s2, cmsq], writes=[cmsq])

        def ln_rstd():
            op(ACT, lambda: nc.scalar.activation(out=crs[:, :], in_=cmsq[:, :], func=AF.Ln, bias=eps_ap),
               reads=[cmsq, epsT], writes=[crs])
            op(ACT, lambda: nc.scalar.activation(out=crs[:, :], in_=crs[:, :], func=AF.Exp, scale=-0.5),
               reads=[crs], writes=[crs])

        def ln_norm(tg, cc):
            z_ = zcb[tg % 2]
            op(DVE, lambda: nc.vector.tensor_tensor(out=z_[:, cc, :], in0=z_[:, cc, :], in1=cmean[:, :], op=ALU.subtract),
               reads=[z_, cmean], writes=[z_])
            op(DVE, lambda: nc.vector.tensor_tensor(out=z_[:, cc, :], in0=z_[:, cc, :], in1=crs[:, :], op=ALU.mult),
               reads=[z_, crs], writes=[z_])

        def silu_exp(tg, cc):
            z_ = zcb[tg % 2]
            op(ACT, lambda: nc.scalar.activation(out=ctmp[cc % 2][:, :], in_=z_[:, cc, :], func=AF.Exp,
                                                 scale=ngl[:, cc:cc + 1], bias=ngl[:, 4 + cc:5 + cc]),
               reads=[z_, ngl], writes=[ctmp[cc % 2]])

        def silu_fin(tg, cc):
            c0 = tg * 512
            z_ = zcb[tg % 2]
            e_ = ctmp[cc % 2]
            op(ACT, lambda: nc.scalar.activation(out=e_[:, :], in_=e_[:, :], func=AF.Ln, bias=one_ap), reads=[e_, epsT], writes=[e_])
            op(ACT, lambda: nc.scalar.activation(out=e_[:, :], in_=e_[:, :], func=AF.Exp, scale=-1.0), reads=[e_], writes=[e_])
            op(DVE, lambda: nc.vector.tensor_scalar(out=z_[:, cc, :], in0=z_[:, cc, :], scalar1=cpar[:, 4 + cc:5 + cc],
                                                    scalar2=cpar[:, 8 + cc:9 + cc], op0=ALU.mult, op1=ALU.add),
               reads=[z_, cpar], writes=[z_])
            op(DVE, lambda: nc.vector.tensor_tensor(out=convT[:, cc, c0:c0 + 512], in0=z_[:, cc, :], in1=e_[:, :], op=ALU.mult),
               reads=[z_, e_], writes=[convT])

        conv_list = [(tg, cc, j) for tg in range(4) for cc in range(4) for j in range(31)]
        conv_done = [False]

        def load_wout():
            cx.fence([wout_b])
            dma(SP, woch, wout_b[:, :, :], woutb_d.rearrange("(k p) n -> p k n", p=128), writes=[wout_b],
                extra=list(woutbT.evs.values()))

        def emit_conv(n):
            for _ in range(n):
                if not conv_list:
                    return
                tg, cc, j = conv_list.pop(0)
                conv_tap(tg, cc, j)
                if j == 30:
                    at(12, lambda tg=tg, cc=cc: conv_sq(tg, cc))
                    if cc == 3:
                        at(20, lambda: (ln_pe(), ln_stats()))
                        at(34, ln_rstd)
                        for c2 in range(4):
                            at(40 + 4 * c2, lambda c2=c2, tg=tg: ln_norm(tg, c2))
                            at(60 + 8 * c2, lambda c2=c2, tg=tg: silu_exp(tg, c2))
                            at(66 + 8 * c2, lambda c2=c2, tg=tg: silu_fin(tg, c2))
                        if tg == 3:
                            at(100, load_wout)

        NP = NKT // 2
        it_ = [0]

        def pv(h, qg, p, pt):
            va_ = Vaug[h % 2]
            acc = banks[qg % 2]
            q0 = qg * 512
            for j in range(2):
                k2 = 2 * p + j
                op(PE, lambda j=j, k2=k2: nc.tensor.matmul(
                    acc[:, :], lhsT=va_[:, k2, :], rhs=pt[:, j * 512:(j + 1) * 512], start=(k2 == 0), stop=(k2 == NKT - 1)),
                   reads=[va_, pt], writes=[acc], sig=(j == 1))
            if p != NP - 1:
                return
            o_lo, d_lo = (0, 64) if h % 2 == 0 else (64, 0)

            def norm_act():
                op(ACT, lambda: nc.scalar.activation(out=rd[o_lo:o_lo + 64, :], in_=acc[d_lo:d_lo + 64, :], func=AF.Ln),
                   reads=[acc], writes=[rd])
                op(ACT, lambda: nc.scalar.activation(out=rd[o_lo:o_lo + 64, :], in_=rd[o_lo:o_lo + 64, :], func=AF.Exp, scale=-1.0),
                   reads=[rd], writes=[rd])

            def norm_dve():
                op(DVE, lambda: nc.vector.tensor_tensor(out=attnT[o_lo:o_lo + 64, h // 2, q0:q0 + 512], in0=acc[o_lo:o_lo + 64, :],
                                                        in1=rd[o_lo:o_lo + 64, :], op=ALU.mult), reads=[acc, rd], writes=[attnT])

            at(6, norm_act)
            at(10, norm_dve)

        for u in prep_units(0):
            u()
        steps = [(h, qg, p) for h in range(H) for qg in range(4) for p in range(NP)]
        pend = []
        units = []
        gstep = 0
        for (h, qg, p) in steps:
            if p == 0 and qg == 1 and h + 1 < H:
                assert not units
                units = prep_units(h + 1)
            kt_ = KTb[h % 2]
            q0 = qg * 512
            s_ = spair[p % 2]
            for j in range(2):
                kt = 2 * p + j
                op(PE, lambda j=j, kt=kt: nc.tensor.matmul(
                    s_[:, j * 512:(j + 1) * 512], lhsT=kt_[:, kt * 128:(kt + 1) * 128], rhs=QT[:, h, q0:q0 + 512],
                    start=True, stop=True), reads=[kt_, QT], writes=[s_], sig=(j == 1))
            pt = PT[gstep % 3]
            gstep += 1
            op(ACT, lambda: nc.scalar.activation(out=pt[:, :], in_=s_[:, :], func=AF.Exp, scale=SM_SCALE),
               reads=[s_], writes=[pt])
            pend.append((h, qg, p, pt))
            if len(pend) > 2:
                pv(*pend.pop(0))
            if units and (it_[0] % 3 == 0 or (qg == 3 and NP - p <= len(units))):
                units.pop(0)()
            elif it_[0] % 12 == 6:
                modB_one()
            it_[0] += 1
            emit_conv(2 if it_[0] % 8 == 0 else 1)
            tick[0] += 2
            run_due()
        while pend:
            pv(*pend.pop(0))
        assert not units
        while modB or modB_p:
            modB_one()
        emit_conv(10 ** 6)
        run_due(flush=True)
        cx.alias([bc1], [bst[0], bst[1], rowB[0], rowB[1]])
        cx.fence([banks[2], banks[3], banks[4], banks[5]])

        w1_b = sb(4, [128, 8, 4096], BF16)
        w2_b = sb(68, [128, 32, 1024], BF16)
        xa = [sb(132 + 4 * i, [128, 1024], F32) for i in range(3)]
        tsc = sb(144, [128, 512], F32)
        h2st = [sb(124 + 2 * i, [128, 8, 128], BF16) for i in range(2)]
        cx.fence([w1_b, w2_b, xa[0], xa[1], xa[2], tsc, h2st[0], h2st[1]])
        w1ch = [cx.chan() for _ in range(8)]
        w2ch = [cx.chan() for _ in range(8)]
        w1t = [T() for _ in range(8)]
        w2t = [T() for _ in range(8)]
        w1b_v = w1b_d.rearrange("(k p) n -> p k n", p=128)
        w2b_v = w2b_d.rearrange("(f p) n -> p f n", p=128)
        snap_f = list(cx.snapshot().values())

        def load_mlp_w(c):
            dma(POOL, w1ch[c], w1_b[:, :, c * 512:(c + 1) * 512], w1b_v[:, :, c * 512:(c + 1) * 512],
                writes=[w1t[c]], extra=list(w1bT.evs.values()) + (snap_f if c == 0 else []))
            if c < 7:
                dma(POOL, w2ch[c], w2_b[:, 4 * c:4 * c + 4, :], w2b_v[:, 4 * c:4 * c + 4, :], writes=[w2t[c]],
                    extra=list(w2bT.evs.values()))
            else:
                al = []
                for t_ in h2st:
                    al += ([t_.w] if t_.w else []) + list(t_.r.values())
                dma(SP, w2ch[c], w2_b[:, 4 * c:4 * c + 4, :], w2b_v[:, 4 * c:4 * c + 4, :], writes=[w2t[c]],
                    extra=list(w2bT.evs.values()) + al)

        def bcast_gate(dst_tile, dst_slot, jbase, scratch):
            for c in range(8):
                col = 2 * (jbase + c)
                op(DVE, lambda c=c, col=col: nc.vector.tensor_scalar(out=scratch[1](slice(c * 128, (c + 1) * 128)), in0=ident[:, :],
                                                                     scalar1=modT[:, col:col + 1], scalar2=None, op0=ALU.mult),
                   reads=[ident, modT], writes=[scratch[0]])
            for hf in range(2):
                p_ = ps(range(6, 8))
                op(PE, lambda hf=hf, p_=p_: nc.tensor.matmul(p_[:, :], lhsT=ones_f[:, :], rhs=scratch[1](slice(hf * 512, (hf + 1) * 512)),
                                                              start=True, stop=True), reads=[ones_f, scratch[0]], writes=[p_])
                op(ACT, lambda hf=hf, p_=p_: nc.scalar.activation(out=dst_tile[:, dst_slot, hf * 512:(hf + 1) * 512], in_=p_[:, :],
                                                                  func=AF.Copy), reads=[p_], writes=[dst_tile])

        bch = cx.chan()
        bcast_gate(bc1, 0, 32, (xa[2], lambda sl: xa[2][:, sl]))
        mv = modT[:, 0:96].rearrange("p (j r) -> p j r", r=2)
        op(DVE, lambda: nc.vector.tensor_tensor(out=ab2[:, 0:8], in0=lnT[:, 0:8], in1=mv[:, 24:32, 0], op=ALU.mult),
           reads=[lnT, modT], writes=[ab2])
        op(DVE, lambda: nc.vector.tensor_tensor(out=ab2[:, 8:16], in0=lnT[:, 8:16], in1=mv[:, 24:32, 0], op=ALU.mult),
           reads=[lnT, modT], writes=[ab2])
        op(DVE, lambda: nc.vector.tensor_tensor(out=ab2[:, 8:16], in0=ab2[:, 8:16], in1=mv[:, 16:24, 0], op=ALU.add),
           reads=[ab2, modT], writes=[ab2])

        x1Tt = [T() for _ in range(16)]
        h2Tt = [T() for _ in range(16)]
        xrch = [cx.chan(), cx.chan(), cx.chan()]
        sch = [cx.chan(), cx.chan()]
        hch = [cx.chan(), cx.chan()]
        h2s_v = h2s_d.rearrange("(k p) n -> p k n", p=128)

        def layer_norm(xt, xv, gi_ap, bi_ap, gb_tile, st0, affine=True):
            for hf in range(2):
                op(DVE, lambda hf=hf: nc.vector.bn_stats(out=stat[:, st0 + hf * 6:st0 + hf * 6 + 6],
                                                         in_=xv(slice(hf * 512, (hf + 1) * 512))), reads=[xt], writes=[stat])
            op(DVE, lambda: nc.vector.bn_aggr(out=stat[:, st0 + 12:st0 + 14], in_=stat[:, st0:st0 + 12]), reads=[stat], writes=[stat])
            op(ACT, lambda: nc.scalar.activation(out=stat[:, st0 + 14:st0 + 15], in_=stat[:, st0 + 13:st0 + 14], func=AF.Sqrt,
                                                 bias=eps_ap), reads=[stat, epsT], writes=[stat])
            op(DVE, lambda: nc.vector.reciprocal(out=stat[:, st0 + 15:st0 + 16], in_=stat[:, st0 + 14:st0 + 15]),
               reads=[stat], writes=[stat])
            op(DVE, lambda: nc.vector.tensor_scalar(out=stat[:, st0 + 14:st0 + 15], in0=stat[:, st0 + 12:st0 + 13],
                                                    scalar1=stat[:, st0 + 15:st0 + 16], scalar2=-1.0, op0=ALU.mult, op1=ALU.mult),
               reads=[stat], writes=[stat])
            op(ACT, lambda: nc.scalar.activation(out=xv(slice(0, 1024)), in_=xv(slice(0, 1024)), func=AF.Identity,
                                                 scale=stat[:, st0 + 15:st0 + 16], bias=stat[:, st0 + 14:st0 + 15]),
               reads=[xt, stat], writes=[xt])
            if affine:
                op(DVE, lambda: nc.vector.tensor_tensor(out=xv(slice(0, 1024)), in0=xv(slice(0, 1024)), in1=gi_ap, op=ALU.mult),
                   reads=[xt, gb_tile], writes=[xt])
                op(DVE, lambda: nc.vector.tensor_tensor(out=xv(slice(0, 1024)), in0=xv(slice(0, 1024)), in1=bi_ap, op=ALU.add),
                   reads=[xt, gb_tile], writes=[xt])

        bc2 = sb(146, [128, 5, 1024], F32)
        x1t = [sb(166 + 8 * i, [128, 2, 1024], F32) for i in range(2)]
        h2g = [sb(182 + 4 * i, [128, 8, 256], BF16) for i in range(2)]
        rr = [sb(190 + i, [128, 256], F32) for i in range(3)]
        h1 = [sb(193 + 0.5 * i, [128, 256], BF16) for i in range(3)]
        tsc2 = sb(195, [128, 512], F32)
        astg = sb(197, [128, 2, 1024], F32)
        lch = [cx.chan(), cx.chan()]
        lhch = [cx.chan(), cx.chan()]
        b2ch = cx.chan()

        def load_h(g):
            dma(SP, lhch[g % 2], h2g[g % 2][:, :, :], h2s_v[:, :, g * 256:(g + 1) * 256], reads=[h2Tt[2 * g], h2Tt[2 * g + 1]],
                writes=[h2g[g % 2]])

        def load_x(g):
            for tt in range(2):
                dma(SP, lch[g % 2], x1t[g % 2][:, tt, :], x1s_d[g * 256 + tt * 128:g * 256 + (tt + 1) * 128, :],
                    reads=[x1Tt[2 * g + tt]], writes=[x1t[g % 2]])

        def mlp_prefetch_1():
            cx.alias([bc2, x1t[0], x1t[1], h2g[0], h2g[1]], [attnT, convT, wout_b, zc2])
            dma(SP, b2ch, bc2[:, 1, :], lnp_d[2:3, :].partition_broadcast(128), writes=[bc2])
            dma(SP, b2ch, bc2[:, 2, :], lnp_d[3:4, :].partition_broadcast(128), writes=[bc2])
            dma(SP, b2ch, bc2[:, 3, :], lnp_d[0:1, :].partition_broadcast(128), writes=[bc2])
            dma(SP, b2ch, bc2[:, 4, :], lnp_d[1:2, :].partition_broadcast(128), writes=[bc2])
            load_h(0)
            load_x(0)
            load_x(1)

        def mlp_prefetch_2():
            cx.alias([tsc2, astg] + rr + h1, [bc1, wout_b])
            bcast_gate(bc2, 0, 40, (astg, lambda sl: astg[:, 0, sl]))

        pa_ps = {}

        def stage_A(tt):
            xt = xa[tt % 3]
            r0 = tt * 128
            dma(SP, xrch[tt % 3], xt[:, :], xown_d[r0:r0 + 128, :], writes=[xt])
            pa_ps[tt] = []
            for hf in range(2):
                p_ = banks[2 * (tt % 3) + hf]
                pa_ps[tt].append(p_)
                for k in range(8):
                    src = attnT if k < 4 else convT
                    op(PE, lambda k=k, src=src: nc.tensor.matmul(
                        p_[:, :], lhsT=src[:, k % 4, r0:r0 + 128], rhs=wout_b[:, k, hf * 512:(hf + 1) * 512],
                        start=(k == 0), stop=(k == 7)), reads=[src, wout_b], writes=[p_], sig=(k == 7))

        def stage_B(tt):
            xt = xa[tt % 3]
            r0 = tt * 128
            for hf in range(2):
                p_ = pa_ps[tt][hf]
                op(DVE, lambda: nc.vector.tensor_tensor(out=tsc[:, :], in0=p_[:, :], in1=bc1[:, 0, hf * 512:(hf + 1) * 512],
                                                        op=ALU.mult), reads=[p_, bc1], writes=[tsc])
                op(DVE, lambda: nc.vector.scalar_tensor_tensor(
                    out=xt[:, hf * 512:(hf + 1) * 512], in0=xt[:, hf * 512:(hf + 1) * 512], scalar=ALPHA, in1=tsc[:, :],
                    op0=ALU.mult, op1=ALU.add), reads=[xt, tsc], writes=[xt])
            layer_norm(xt, lambda s_: xt[:, s_], None, None, bc1, 16 * (tt % 2), affine=False)
            dma(SP, sch[tt % 2], x1s_d[r0:r0 + 128, :], xt[:, :], reads=[xt], writes=[x1Tt[tt]])

        def stage_C(tt):
            xt = xa[tt % 3]
            r0 = tt * 128
            hs = h2st[tt % 2]
            for half in range(2):
                p_ = banks[6 + half]
                for kk in range(4):
                    k = 4 * half + kk
                    op(PE, lambda k=k, kk=kk: nc.tensor.transpose(
                        out=p_[:, kk * 128:(kk + 1) * 128], in_=xt[:, k * 128:(k + 1) * 128], identity=ident[:, :]),
                       reads=[xt, ident], writes=[p_], sig=(kk == 3))
                for kk in range(4):
                    k = 4 * half + kk
                    if kk % 2 == 0:
                        op(ACT, lambda k=k, kk=kk: nc.scalar.activation(
                            out=hs[:, k, :], in_=p_[:, kk * 128:(kk + 1) * 128], func=AF.Identity,
                            scale=ab2[:, k:k + 1], bias=ab2[:, 8 + k:9 + k]), reads=[p_, ab2], writes=[hs])
                    else:
                        op(DVE, lambda k=k, kk=kk: nc.vector.tensor_scalar(
                            out=hs[:, k, :], in0=p_[:, kk * 128:(kk + 1) * 128], scalar1=ab2[:, k:k + 1],
                            scalar2=ab2[:, 8 + k:9 + k], op0=ALU.mult, op1=ALU.add), reads=[p_, ab2], writes=[hs])
            dma(SP, hch[tt % 2], h2s_v[:, :, r0:r0 + 128], hs[:, :, :], reads=[hs], writes=[h2Tt[tt]])

        stage_A(0)
        stage_A(1)
        stage_B(0)
        for tt in range(16):
            if tt + 2 < 16:
                stage_A(tt + 2)
            if tt + 1 < 16:
                stage_B(tt + 1)
            stage_C(tt)
            if tt % 2 == 1:
                load_mlp_w(tt // 2)
            if tt == 13:
                mlp_prefetch_1()
            if tt == 14:
                mlp_prefetch_2()

        for tt_ in range(16):
            x1Tt[tt_].w = (sch[tt_ % 2].sem, sch[tt_ % 2].n)
            h2Tt[tt_].w = (hch[tt_ % 2].sem, hch[tt_ % 2].n)
        och = [cx.chan(), cx.chan()]
        outT = T()

        def epilogue_pieces(g):
            xg = x1t[g % 2]
            c0 = g * 256
            pcs = []
            for tt in range(2):
                st0 = 16 * tt
                xv = lambda s_, tt=tt: xg[:, tt, s_]

                def p1(tt=tt, xv=xv, st0=st0):
                    op(DVE, lambda: nc.vector.tensor_tensor(out=xg[:, tt, :], in0=xg[:, tt, :], in1=bc2[:, 3, :], op=ALU.mult),
                       reads=[xg, bc2], writes=[xg])
                    op(DVE, lambda: nc.vector.tensor_tensor(out=xg[:, tt, :], in0=xg[:, tt, :], in1=bc2[:, 4, :], op=ALU.add),
                       reads=[xg, bc2], writes=[xg])
                    op(DVE, lambda: nc.vector.tensor_tensor(out=astg[:, tt, :], in0=astg[:, tt, :], in1=bc2[:, 0, :], op=ALU.mult),
                       reads=[astg, bc2], writes=[astg])
                    op(DVE, lambda: nc.vector.scalar_tensor_tensor(out=xg[:, tt, :], in0=xg[:, tt, :], scalar=ALPHA,
                                                                   in1=astg[:, tt, :], op0=ALU.mult, op1=ALU.add),
                       reads=[xg, astg], writes=[xg])
                    for hf in range(2):
                        op(DVE, lambda hf=hf: nc.vector.bn_stats(out=stat[:, st0 + hf * 6:st0 + hf * 6 + 6],
                                                                 in_=xv(slice(hf * 512, (hf + 1) * 512))), reads=[xg], writes=[stat])
                    op(DVE, lambda: nc.vector.bn_aggr(out=stat[:, st0 + 12:st0 + 14], in_=stat[:, st0:st0 + 12]),
                       reads=[stat], writes=[stat])

                def p2(st0=st0):
                    op(ACT, lambda: nc.scalar.activation(out=stat[:, st0 + 14:st0 + 15], in_=stat[:, st0 + 13:st0 + 14], func=AF.Sqrt,
                                                         bias=eps_ap), reads=[stat, epsT], writes=[stat])
                    op(DVE, lambda: nc.vector.reciprocal(out=stat[:, st0 + 15:st0 + 16], in_=stat[:, st0 + 14:st0 + 15]),
                       reads=[stat], writes=[stat])
                    op(DVE, lambda: nc.vector.tensor_scalar(out=stat[:, st0 + 14:st0 + 15], in0=stat[:, st0 + 12:st0 + 13],
                                                            scalar1=stat[:, st0 + 15:st0 + 16], scalar2=-1.0, op0=ALU.mult,
                                                            op1=ALU.mult), reads=[stat], writes=[stat])

                def p3(tt=tt, xv=xv, st0=st0):
                    op(ACT, lambda: nc.scalar.activation(out=xv(slice(0, 1024)), in_=xv(slice(0, 1024)), func=AF.Identity,
                                                         scale=stat[:, st0 + 15:st0 + 16], bias=stat[:, st0 + 14:st0 + 15]),
                       reads=[xg, stat], writes=[xg])

                def p4(tt=tt, xv=xv):
                    op(DVE, lambda: nc.vector.tensor_tensor(out=xv(slice(0, 1024)), in0=xv(slice(0, 1024)), in1=bc2[:, 1, :],
                                                            op=ALU.mult), reads=[xg, bc2], writes=[xg])
                    op(DVE, lambda: nc.vector.tensor_tensor(out=xv(slice(0, 1024)), in0=xv(slice(0, 1024)), in1=bc2[:, 2, :],
                                                            op=ALU.add), reads=[xg, bc2], writes=[xg])
                    dma(SP, och[g % 2], out_d[c0 + tt * 128:c0 + (tt + 1) * 128, :], xg[:, tt, :], reads=[xg], writes=[outT])

                base = 1 + 10 * tt
                pcs += [(base, p1), (base + 6, p2), (base + 9, p3), (base + 12, p4)]
            if g + 2 < 8:
                pcs.append((26, lambda: load_x(g + 2)))
            pcs.sort(key=lambda e: e[0])
            return pcs

        pend_epi = []
        for g in range(8):
            xg, hg = x1t[g % 2], h2g[g % 2]
            c0 = g * 256
            if g + 1 < 8:
                load_h(g + 1)
            accs = [[banks[2 * tt + hf] for hf in range(2)] for tt in range(2)]

            def down(f):
                hb = h1[f % 3]
                for tt in range(2):
                    for hf in range(2):
                        op(PE, lambda tt=tt, hf=hf: nc.tensor.matmul(
                            accs[tt][hf][:, :], lhsT=hb[:, tt * 128:(tt + 1) * 128], rhs=w2_b[:, f, hf * 512:(hf + 1) * 512],
                            start=(f == 0), stop=(f == 31)), reads=[hb, w2t[f // 4]], writes=[accs[tt][hf]])

            for f in range(32):
                pz = ps(range(4, 8))
                for k in range(8):
                    op(PE, lambda k=k: nc.tensor.matmul(pz[:, 0:256], lhsT=w1_b[:, k, f * 128:(f + 1) * 128], rhs=hg[:, k, :],
                                                        start=(k == 0), stop=(k == 7)),
                       reads=[w1t[f // 4], hg], writes=[pz], sig=(k == 7))
                op(ACT, lambda: nc.scalar.activation(out=rr[f % 3][:, :], in_=pz[:, 0:256], func=AF.Relu),
                   reads=[pz], writes=[rr[f % 3]])
                op(POOL, lambda: nc.gpsimd.tensor_tensor(out=h1[f % 3][:, :], in0=rr[f % 3][:, :], in1=rr[f % 3][:, :], op=ALU.mult),
                   reads=[rr[f % 3]], writes=[h1[f % 3]])
                if f >= 2:
                    down(f - 2)
                while pend_epi and pend_epi[0][0] <= f:
                    pend_epi.pop(0)[1]()
            down(30)
            down(31)
            while pend_epi:
                pend_epi.pop(0)[1]()
            for tt in range(2):
                for hf in range(2):
                    a_ = accs[tt][hf]
                    op(ACT, lambda: nc.scalar.activation(out=astg[:, tt, hf * 512:(hf + 1) * 512], in_=a_[:, :], func=AF.Copy),
                       reads=[a_], writes=[astg])
            pend_epi = epilogue_pieces(g)
        while pend_epi:
            pend_epi.pop(0)[1]()
        for ch in och:
            SP.h.wait_ge(ch.sem, ch.n)
    return nc


def _rope_tables(pos):
    f32 = np.float32
    freqs = (np.float32(10000.0) ** (-(np.arange(0, 16, 2, dtype=f32)) / f32(16))).astype(f32)
    row = (pos // 64).astype(f32)
    col = (pos % 64).astype(f32)
    ar = (row[:, None] * freqs[None, :]).astype(f32)
    ac = (col[:, None] * freqs[None, :]).astype(f32)
    cosr, sinr, cosc, sinc = np.cos(ar), np.sin(ar), np.cos(ac), np.sin(ac)
    cos = np.concatenate([cosr, cosr, cosc, cosc], 1)
    sin = np.concatenate([-sinr, sinr, -sinc, sinc], 1)
    return cos.T.astype(f32), sin.T.astype(f32)


_PERM32 = np.array([(i // 16) * 16 + ((i % 16) + 8) % 16 for i in range(32)])


def kernel(x, c, ctx, c_ctx, ada_w, ada_b, w_in, q_norm_g, kv_norm_g, w_uq, w_uk, w_uv,
           w_dw, b_dw, conv_ln_g, conv_ln_b, w_out, ln1_g, ln1_b, w1, w2, ln2_g, ln2_b):
    f32 = np.float32
    A = lambda a: np.ascontiguousarray(np.asarray(a, dtype=f32))
    x, c, ctx, c_ctx = A(x), A(c), A(ctx), A(c_ctx)
    ada_w, ada_b, w_in = A(ada_w)[0], A(ada_b)[0], A(w_in)[0]
    w_uq, w_uk, w_uv = A(w_uq)[0], A(w_uk)[0], A(w_uv)[0]
    w_dw, w_out, w1, w2 = A(w_dw)[0], A(w_out)[0], A(w1)[0], A(w2)[0]
    kr = w_in[:, 640:672]
    w_in_e = np.concatenate([w_in[:, 0:640], np.zeros((D, 64), f32), kr, kr[:, _PERM32], w_in[:, 672:1696]], 1)
    blocks = []
    for h in range(H):
        qn = w_uq[:, h * 96:h * 96 + 64]
        qr = w_uq[:, h * 96 + 64:h * 96 + 96]
        blocks.append(np.concatenate([qn, qr, qr], 1))
        blocks.append(np.concatenate([np.zeros((384, 64), f32), qr[:, _PERM32], np.zeros((384, 32), f32)], 1))
    w_uq_e = np.ascontiguousarray(np.concatenate(blocks, 1))
    pk = lambda v, k: np.ascontiguousarray(v.reshape(k, 128).T)
    qg = pk(A(q_norm_g)[0], 3)
    kvg = pk(A(kv_norm_g)[0], 2)
    wdw = np.ascontiguousarray(w_dw.T.reshape(4, 128, 31).transpose(1, 0, 2).reshape(128, 124))
    cpar = np.ascontiguousarray(np.concatenate([pk(A(b_dw)[0], 4), pk(A(conv_ln_g)[0], 4), pk(A(conv_ln_b)[0], 4)], 1))
    lnp = np.ascontiguousarray(np.stack([A(ln1_g)[0], A(ln1_b)[0], A(ln2_g)[0], A(ln2_b)[0]], 0))
    adabT = np.ascontiguousarray(ada_b.reshape(48, 128).T)
    i2 = np.eye(2, dtype=f32)
    lnT = np.ascontiguousarray(np.concatenate([pk(A(ln1_g)[0], 8), pk(A(ln1_b)[0], 8)], 1))
    ident = np.eye(128, dtype=f32)
    shared = dict(ada_w=ada_w, adabT=adabT, w_in_e=np.ascontiguousarray(w_in_e), w_uq_e=w_uq_e, w_uk=w_uk, w_uv=w_uv,
                  qg=qg, kvg=kvg, wdw=wdw, cpar=cpar, w_out=w_out, w1=w1, w2=w2, lnp=lnp, i2=i2, ident=ident, lnT=lnT)
    in_maps = []
    for core in range(8):
        b, half = core // 2, core % 2
        own = slice(half * NOWN, (half + 1) * NOWN)
        oth = slice((1 - half) * NOWN, (2 - half) * NOWN)
        xb = x[b]
        xT = np.ascontiguousarray(np.concatenate([xb[own], xb[oth], ctx[b]], 0).T)
        xown = np.ascontiguousarray(xb[own])
        pos = np.concatenate([np.arange(half * NOWN, (half + 1) * NOWN), np.arange((1 - half) * NOWN, (2 - half) * NOWN)])
        cos, sin = _rope_tables(pos)
        ctab = np.zeros((128, SEQ), f32)
        stab = np.zeros((128, SEQ), f32)
        ctab[64:96] = cos
        stab[64:96] = sin
        hal = np.zeros((32, D), f32)
        hm = np.zeros((128, 32), f32)
        s0 = half * NOWN
        for j in range(15):
            pl = s0 - 15 + j
            if 0 <= pl < SEQ:
                hal[j] = xb[pl]
                hm[:, j] = 1.0
            pr_ = s0 + NOWN + j
            if 0 <= pr_ < SEQ:
                hal[15 + j] = xb[pr_]
                hm[:, 15 + j] = 1.0
        cvv = np.zeros((128, 16), f32)
        cvv[:, 0::2] = pk(c[b], 8)
        cvv[:, 1::2] = pk(c_ctx, 8)
        m = dict(shared)
        m.update(xT=xT, xown=xown, xTh=np.ascontiguousarray(hal.T), hmask=hm, cv=cvv, ctab=ctab, stab=stab)
        in_maps.append(m)
    nc = build_program()
    res = run_bass_kernel_spmd(nc, in_maps, core_ids=list(range(8)))
    out = np.zeros((4, SEQ, D), f32)
    for core in range(8):
        b, half = core // 2, core % 2
        out[b, half * NOWN:(half + 1) * NOWN] = res.results[core]["out"]
    return out
```

```python
import numpy as np
import concourse.bass as bass
import concourse.mybir as mybir
from concourse.bass_utils import run_bass_kernel_spmd

F32 = mybir.dt.float32
BF16 = mybir.dt.bfloat16
AF = mybir.ActivationFunctionType
ALU = mybir.AluOpType

D = 1024
SEQ = 4096
NOWN = 2048
NCTX = 256
NK = 4352
NKT = 34
H = 8
EPS = 1e-6
ALPHA = 2.0 ** 0.25
SM_SCALE = 96.0 ** -0.5
KB = 1024
SB_BASE = 16512


class Eng:
    def __init__(self, h, sem):
        self.h = h
        self.sem = sem
        self.n = 0
        self.waited = {}


class T:
    def __init__(self, t=None):
        self.t = t
        self.w = None
        self.r = {}

    def __getitem__(self, k):
        return self.t[k]


class Chan:
    def __init__(self, sem):
        self.sem = sem
        self.n = 0


class Ctx:
    def __init__(self, nc, es):
        self.nc = nc
        self.es = es
        self.nsem = 0
        self.pe = Eng(nc.tensor, self.sem())
        self.act = Eng(nc.scalar, self.sem())
        self.dve = Eng(nc.vector, self.sem())
        self.pool = Eng(nc.gpsimd, self.sem())
        self.sp = Eng(nc.sync, self.sem())
        self.engs = [self.pe, self.act, self.dve, self.pool, self.sp]
        self.chans = []

    def sem(self):
        self.nsem += 1
        return self.es.enter_context(self.nc.semaphore(f"s{self.nsem}"))

    def chan(self):
        c = Chan(self.sem())
        self.chans.append(c)
        return c

    def _waits(self, eng, reads, writes, extra=()):
        need = {}

        def add(ev, same_ok):
            if ev is None:
                return
            sem, val = ev
            if (not same_ok) and sem is eng.sem:
                return
            k = id(sem)
            if k not in need or need[k][1] < val:
                need[k] = (sem, val)

        same = eng.h is not self.nc.tensor
        for t in reads:
            add(t.w, True)
        for t in writes:
            add(t.w, same)
            for ev in t.r.values():
                add(ev, same)
        for ev in extra:
            add(ev, True)
        for k, (sem, val) in need.items():
            if eng.waited.get(k, 0) < val:
                eng.h.wait_ge(sem, val)
                eng.waited[k] = val

    def op(self, eng, fn, reads=(), writes=(), sig=True, extra=()):
        self._waits(eng, reads, writes, extra)
        ins = fn()
        if sig:
            eng.n += 1
            ins.then_inc(eng.sem, 1)
            ev = (eng.sem, eng.n)
        else:
            ev = (eng.sem, eng.n + 1)
        for t in reads:
            t.r[id(eng.sem)] = ev
        for t in writes:
            t.w = ev
            t.r = {}
        return ev

    def dma(self, eng, ch, out, in_, reads=(), writes=(), extra=(), **kw):
        self._waits(eng, reads, writes, extra)
        ins = eng.h.dma_start(out=out, in_=in_, **kw)
        ch.n += 16
        ins.then_inc(ch.sem, 16)
        ev = (ch.sem, ch.n)
        for t in reads:
            t.r[id(ch.sem)] = ev
        for t in writes:
            t.w = ev
            t.r = {}
        return ev

    def snapshot(self):
        evs = {}
        for e in self.engs:
            if e.n > 0:
                evs[id(e.sem)] = (e.sem, e.n)
        for c in self.chans:
            if c.n > 0 and not getattr(c, "nofence", False):
                evs[id(c.sem)] = (c.sem, c.n)
        return evs

    def alias(self, new, old):
        ev = {}
        for t in old:
            for e in ([t.w] if t.w else []) + list(t.r.values()):
                k = id(e[0])
                if k not in ev or ev[k][1] < e[1]:
                    ev[k] = e
        for t in new:
            t.w = None
            t.r = dict(ev)

    def fence(self, tiles):
        snap = self.snapshot()
        for t in tiles:
            t.w = None
            t.r = dict(snap)


def build_program():
    from contextlib import ExitStack

    nc = bass.Bass("TRN2", target_bir_lowering=False)

    def din(name, shape, dt=F32):
        return nc.dram_tensor(name, list(shape), dt, kind="ExternalInput").ap()

    xT_d = din("xT", [D, NK])
    xown_d = din("xown", [NOWN, D])
    xTh_d = din("xTh", [D, 32])
    hmask_d = din("hmask", [128, 32])
    cv_d = din("cv", [128, 16])
    adaw_d = din("ada_w", [D, 6 * D])
    adabT_d = din("adabT", [128, 48])
    win_d = din("w_in_e", [D, 1792])
    wuq_d = din("w_uq_e", [384, 2048])
    wuk_d = din("w_uk", [256, 512])
    wuv_d = din("w_uv", [256, 512])
    qg_d = din("qg", [128, 3])
    kvg_d = din("kvg", [128, 2])
    wdw_d = din("wdw", [128, 4 * 31])
    cpar_d = din("cpar", [128, 12])
    wout_d = din("w_out", [D, D])
    w1_d = din("w1", [D, 4 * D])
    w2_d = din("w2", [4 * D, D])
    lnp_d = din("lnp", [4, D])
    ctab_d = din("ctab", [128, SEQ])
    stab_d = din("stab", [128, SEQ])
    i2_d = din("i2", [2, 2])
    lnT_d = din("lnT", [128, 16])
    ident_d = din("ident", [128, 128])
    out_d = nc.dram_tensor("out", [NOWN, D], F32, kind="ExternalOutput").ap()
    x1s_d = nc.dram_tensor("x1s", [NOWN, D], F32, kind="Internal").ap()
    h2s_d = nc.dram_tensor("h2s", [D, NOWN], BF16, kind="Internal").ap()
    w1b_d = nc.dram_tensor("w1b", [D, 4 * D], BF16, kind="Internal").ap()
    w2b_d = nc.dram_tensor("w2b", [4 * D, D], BF16, kind="Internal").ap()
    woutb_d = nc.dram_tensor("woutb", [D, D], BF16, kind="Internal").ap()

    with ExitStack() as es:
        cx = Ctx(nc, es)
        PE, ACT, DVE, POOL, SP = cx.pe, cx.act, cx.dve, cx.pool, cx.sp
        op, dma = cx.op, cx.dma

        cnt = [0]

        def sb(off_kb, shape, dt):
            cnt[0] += 1
            return T(nc.alloc_sbuf_tensor_at(f"t{cnt[0]}", list(shape), dt, offset=SB_BASE + int(off_kb * KB)))

        pairs = [es.enter_context(nc.psum_tensor(f"pp{i}", [128, 1024], F32)) for i in range(4)]
        banks = [T(pairs[i // 2][:, (i % 2) * 512:(i % 2 + 1) * 512]) for i in range(8)]
        prr = [0]

        def ps(pool=range(8)):
            pool = list(pool)
            b = banks[pool[prr[0] % len(pool)]]
            prr[0] += 1
            return b

        o = 0.0

        def small(shape, dt, nbytes):
            nonlocal o
            t = sb(o, shape, dt)
            o += (-(-nbytes // 32) * 32) / KB
            return t

        cv = small([128, 16], F32, 64)
        scv = small([128, 16], F32, 64)
        modT = small([128, 96], F32, 384)
        wdw = small([128, 124], F32, 496)
        cpar = small([128, 12], F32, 48)
        qg = small([128, 3], F32, 12 + 4)
        kvg = small([128, 2], F32, 8)
        ones_b = small([128, 128], BF16, 256)
        ident = small([128, 128], F32, 512)
        ones_f = small([128, 128], F32, 512)
        dg = small([128, 128], F32, 512)
        i2 = small([2, 2], F32, 8 + 24)
        hmask = small([128, 32], F32, 128)
        adabT = small([128, 48], F32, 192)
        lnT = small([128, 16], F32, 64)
        ab2 = small([128, 16], F32, 64)
        stat = small([128, 32], F32, 128)
        assert o * KB <= 3968, o * KB
        wuk_b = sb(4, [128, 2, 512], BF16)
        wuv_b = sb(6, [128, 2, 512], BF16)
        ckvn = sb(8, [128, 2, NK], BF16)
        QT = sb(25.5, [128, H, NOWN], BF16)
        KT0 = sb(57.5, [128, NK], BF16)
        ypad = sb(66.5, [128, 4, 2080], BF16)
        b0 = 83.5
        win_b = sb(b0, [128, 8, 1792], BF16)
        wuq_b = sb(b0 + 28, [128, 3, 2048], BF16)
        xs = [sb(b0 + 40 + 8 * i, [128, 4, 512], F32) for i in range(2)] + [sb(196, [128, 4, 512], F32)]
        hT = [sb(b0 + 56 + 8 * i, [128, 8, 512], BF16) for i in range(2)]
        craw = sb(b0 + 72, [128, 5, 512], F32)
        sq = sb(b0 + 82, [128, 5, 512], BF16)
        cqn = sb(b0 + 87, [128, 3, 512], BF16)
        ctb = [sb(b0 + 90 + 4 * i, [128, 512], F32) for i in range(2)]
        stb = [sb(b0 + 92 + 4 * i, [128, 512], F32) for i in range(2)]
        t1 = sb(b0 + 98, [128, 512], F32)
        t2 = sb(b0 + 100, [128, 512], F32)
        tB = sb(b0 + 102, [128, 512], F32)
        sig = [sb(b0 + 104 + 2 * i, [128, 512], F32) for i in range(2)]
        rs = sb(b0 + 108, [128, 512], F32)
        sd = sb(b0 + 110, [128, 512], F32)
        adaw_s = [sb(25.5 + 16 * i, [128, 8, 512], F32) for i in range(2)]
        wst = [sb(8 + 7 * i, [128, 1792], F32) for i in range(2)]
        wuq_s = sb(b0 + 72, [128, 3, 2048], F32)
        wukv_s = sb(b0 + 96, [128, 2, 1024], F32)

        cch = cx.chan()
        cl = []

        def cload(t, dst, src, eng=SP, **kw):
            dma(eng, cch, dst, src, writes=[t], **kw)
            cl.append(t)

        cload(cv, cv[:, :], cv_d)
        cload(adabT, adabT[:, :], adabT_d)
        cload(i2, i2[:, :], i2_d)
        cload(wdw, wdw[:, :], wdw_d)
        cload(cpar, cpar[:, :], cpar_d)
        cload(qg, qg[:, :], qg_d)
        cload(kvg, kvg[:, :], kvg_d)
        cload(ident, ident[:, :], ident_d)
        cload(hmask, hmask[:, :], hmask_d)
        cload(lnT, lnT[:, :], lnT_d)
        fin = (cch.sem, cch.n)
        for t in cl:
            t.w = fin

        op(POOL, lambda: nc.gpsimd.memset(ones_b[:, :], 1.0), writes=[ones_b])
        op(POOL, lambda: nc.gpsimd.memset(ones_f[:, :], 1.0), writes=[ones_f])
        op(ACT, lambda: nc.scalar.activation(out=scv[:, :], in_=cv[:, :], func=AF.Silu), reads=[cv], writes=[scv])

        def mod_dest(sc):
            if sc < 16:
                return sc
            if sc < 24:
                return 32 + (sc - 16)
            if sc < 40:
                return 16 + (sc - 24)
            return 40 + (sc - 40)

        def mod_mm(st_, co, sc):
            pm = ps()
            for k in range(8):
                op(PE, lambda k=k: nc.tensor.matmul(pm[0:2, 0:128], lhsT=scv[:, 2 * k:2 * k + 2], rhs=st_[:, k, co:co + 128],
                                                    start=(k == 0), stop=(k == 7)), reads=[scv, st_], writes=[pm], sig=(k == 7))
            return pm

        def mod_fin(pm, sc):
            op(DVE, lambda: nc.vector.tensor_copy(out=dg[0:2, 0:128], in_=pm[0:2, 0:128]), reads=[pm], writes=[dg])
            pt_ = ps()
            op(PE, lambda: nc.tensor.matmul(pt_[:, 0:2], lhsT=dg[0:2, 0:128], rhs=i2[0:2, 0:2], start=True, stop=True),
               reads=[dg, i2], writes=[pt_])
            j = mod_dest(sc)
            one = 1.0 if (8 <= j < 16 or 24 <= j < 32) else 0.0
            op(DVE, lambda: nc.vector.tensor_scalar(out=modT[:, 2 * j:2 * j + 2], in0=pt_[:, 0:2], scalar1=adabT[:, sc:sc + 1],
                                                    scalar2=one, op0=ALU.add, op1=ALU.add), reads=[pt_, adabT], writes=[modT])

        ach = [cx.chan(), cx.chan()]
        wsch = [cx.chan(), cx.chan()]
        xch = [cx.chan(), cx.chan(), cx.chan()]
        tch = [cx.chan(), cx.chan()]
        xT_v = xT_d.rearrange("(k p) n -> p k n", p=128)
        pendA = []

        def modA_fin(pmA, c):
            op(DVE, lambda: nc.vector.tensor_copy(out=rs[0:2, :], in_=pmA[0:2, :]), reads=[pmA], writes=[rs])
            pt_ = ps()
            for q in range(4):
                op(PE, lambda q=q: nc.tensor.matmul(pt_[:, 2 * q:2 * q + 2], lhsT=rs[0:2, q * 128:(q + 1) * 128], rhs=i2[0:2, 0:2],
                                                    start=True, stop=True), reads=[rs, i2], writes=[pt_], sig=(q == 3))
            for q in range(4):
                sc = 4 * c + q
                j = mod_dest(sc)
                one = 1.0 if (8 <= j < 16 or 24 <= j < 32) else 0.0
                op(DVE, lambda q=q, sc=sc, j=j, one=one: nc.vector.tensor_scalar(
                    out=modT[:, 2 * j:2 * j + 2], in0=pt_[:, 2 * q:2 * q + 2], scalar1=adabT[:, sc:sc + 1], scalar2=one,
                    op0=ALU.add, op1=ALU.add), reads=[pt_, adabT], writes=[modT])
        adaw_v = adaw_d.rearrange("(k p) c -> p k c", p=128)
        for c in range(4):
            st_ = adaw_s[c % 2]
            for hh in range(2):
                dma(SP, ach[c % 2], st_[:, 4 * hh:4 * hh + 4, :], adaw_v[:, 4 * hh:4 * hh + 4, c * 512:(c + 1) * 512],
                    writes=[st_])
            if c == 0:
                for hh in range(2):
                    dma(SP, xch[hh], xs[hh][:, :, :], xT_v[:, 4 * hh:4 * hh + 4, 0:512], writes=[xs[hh]])
            for pc in (2 * c, 2 * c + 1):
                ws_ = wst[pc % 2]
                dma(SP, wsch[pc % 2], ws_[:, :], win_d[pc * 128:(pc + 1) * 128, :], writes=[ws_])
                op(DVE, lambda pc=pc, ws_=ws_: nc.vector.tensor_copy(out=win_b[:, pc, :], in_=ws_[:, :]),
                   reads=[ws_], writes=[win_b])
            pmA = ps()
            for k in range(8):
                op(PE, lambda k=k: nc.tensor.matmul(pmA[0:2, :], lhsT=scv[:, 2 * k:2 * k + 2], rhs=st_[:, k, :],
                                                    start=(k == 0), stop=(k == 7)), reads=[scv, st_], writes=[pmA], sig=(k == 7))
            if pendA:
                modA_fin(*pendA.pop(0))
            pendA.append((pmA, c))
        while pendA:
            modA_fin(*pendA.pop(0))
        cload2 = cx.chan()
        dma(SP, cload2, wuq_s[:, :, :], wuq_d.rearrange("(k p) c -> p k c", p=128), writes=[wuq_s])
        dma(SP, cload2, wukv_s[:, :, 0:512], wuk_d.rearrange("(k p) c -> p k c", p=128), writes=[wukv_s])
        dma(SP, cload2, wukv_s[:, :, 512:1024], wuv_d.rearrange("(k p) c -> p k c", p=128), writes=[wukv_s])
        wuq_s.w = wukv_s.w = (cload2.sem, cload2.n)

        for k in range(3):
            op(DVE, lambda k=k: nc.vector.tensor_scalar(out=wuq_b[:, k, :], in0=wuq_s[:, k, :],
                                                         scalar1=qg[:, k:k + 1], scalar2=None, op0=ALU.mult),
               reads=[wuq_s, qg], writes=[wuq_b])
        for k in range(2):
            op(DVE, lambda k=k: nc.vector.tensor_scalar(out=wuk_b[:, k, :], in0=wukv_s[:, k, 0:512],
                                                         scalar1=kvg[:, k:k + 1], scalar2=None, op0=ALU.mult),
               reads=[wukv_s, kvg], writes=[wuk_b])
            op(DVE, lambda k=k: nc.vector.tensor_scalar(out=wuv_b[:, k, :], in0=wukv_s[:, k, 512:1024],
                                                         scalar1=kvg[:, k:k + 1], scalar2=None, op0=ALU.mult),
               reads=[wukv_s, kvg], writes=[wuv_b])

        bst = [sb(196 + 4 * i, [128, 8, 128], F32) for i in range(2)]
        rowB = [sb(204 + 0.5 * i, [2, 128], F32) for i in range(2)]
        bch_ = [cx.chan(), cx.chan()]
        modB = list(range(16, 48))
        modB_pend = []

        modB_p = []

        def modB_one():
            prev = modB_p.pop(0) if modB_p else None
            if modB:
                sc = modB.pop(0)
                st_ = bst[sc % 2]
                dma(SP, bch_[sc % 2], st_[:, :, :], adaw_v[:, :, sc * 128:(sc + 1) * 128], writes=[st_])
                pm = banks[6 + (sc % 2)]
                for k in range(8):
                    op(PE, lambda k=k: nc.tensor.matmul(pm[0:2, 0:128], lhsT=scv[:, 2 * k:2 * k + 2], rhs=st_[:, k, 0:128],
                                                        start=(k == 0), stop=(k == 7)), reads=[scv, st_], writes=[pm], sig=(k == 7))
                op(DVE, lambda: nc.vector.tensor_copy(out=rowB[sc % 2][0:2, 0:128], in_=pm[0:2, 0:128]), reads=[pm], writes=[rowB[sc % 2]])
                modB_p.append(sc)
            if prev is not None:
                sc = prev
                pt_ = banks[6 + (sc % 2)]
                op(PE, lambda: nc.tensor.matmul(pt_[:, 0:2], lhsT=rowB[sc % 2][0:2, 0:128], rhs=i2[0:2, 0:2], start=True, stop=True),
                   reads=[rowB[sc % 2], i2], writes=[pt_])
                j = mod_dest(sc)
                one = 1.0 if (8 <= j < 16 or 24 <= j < 32) else 0.0
                op(DVE, lambda: nc.vector.tensor_scalar(out=modT[:, 2 * j:2 * j + 2], in0=pt_[:, 0:2], scalar1=adabT[:, sc:sc + 1],
                                                        scalar2=one, op0=ALU.add, op1=ALU.add), reads=[pt_, adabT], writes=[modT])

        def modB_batch(n):
            pend = []
            for _ in range(n):
                if not modB:
                    break
                sc = modB.pop(0)
                st_ = bst[sc % 2]
                dma(POOL, bch_[sc % 2], st_[:, :, :], adaw_v[:, :, sc * 128:(sc + 1) * 128], writes=[st_])
                pend.append((mod_mm(st_, 0, sc), sc))
                if len(pend) > 1:
                    mod_fin(*pend.pop(0))
            while pend:
                mod_fin(*pend.pop(0))

        def mod_ap(kind, k, r):
            base = {"sh1": 0, "sc1": 8, "sh2": 16, "sc2": 24}[kind]
            c = 2 * (base + k) + r
            return modT[:, c:c + 1]

        def emit_precast():
            jobs = [(woutb_d[:, :], wout_d[:, :], woutbT)]
            jobs += [(w1b_d[:, c * 512:(c + 1) * 512], w1_d[:, c * 512:(c + 1) * 512], w1bT) for c in range(8)]
            jobs += [(w2b_d[c * 512:(c + 1) * 512, :], w2_d[c * 512:(c + 1) * 512, :], w2bT) for c in range(8)]
            for i, (dst, src, tl) in enumerate(jobs):
                ch = pcw[i % 2]
                prev = [(ch.sem, ch.n)] if ch.n > 0 else []
                ev = dma(POOL, ch, dst, src, extra=prev)
                tl.evs = getattr(tl, "evs", {})
                tl.evs[id(ch.sem)] = ev

        w1bT, w2bT, woutbT = T(), T(), T()
        pcw = [cx.chan(), cx.chan()]
        for c_ in pcw:
            c_.nofence = True

        cx.fence([QT, KT0, ypad, ckvn, craw, sq, cqn, ctb[0], ctb[1], stb[0], stb[1], t1, t2, tB])
        op(POOL, lambda: nc.gpsimd.memset(KT0[64:128, :], 0.0), writes=[KT0])
        op(POOL, lambda: nc.gpsimd.memset(ypad[:, :, :], 0.0), writes=[ypad])


        def modulate(g, ntok, src_v, col0, lat):
            h_ = hT[g % 2]
            r = 0 if lat else 1
            for hh in range(2):
                x_ = xs[hh]
                dma(SP, xch[hh], x_[:, :, 0:ntok], src_v[:, 4 * hh:4 * hh + 4, col0:col0 + ntok], writes=[x_])
                for kk in range(4):
                    k = 4 * hh + kk
                    op(ACT, lambda k=k, kk=kk, x_=x_, h_=h_: nc.scalar.activation(
                        out=h_[:, k, 0:ntok], in_=x_[:, kk, 0:ntok], func=AF.Identity,
                        scale=mod_ap("sc1", k, r), bias=mod_ap("sh1", k, r)),
                       reads=[x_, modT], writes=[h_])
            return h_

        def proj(h_, m, ntok, pool=range(8)):
            p_ = ps(pool)
            for k in range(8):
                op(PE, lambda k=k, p_=p_: nc.tensor.matmul(
                    p_[:, 0:ntok], lhsT=win_b[:, k, m * 128:(m + 1) * 128], rhs=h_[:, k, 0:ntok],
                    start=(k == 0), stop=(k == 7)), reads=[win_b, h_], writes=[p_], sig=(k == 7))
            return p_

        def rms_chain(j0, nj, ntok, width):
            pss = ps()
            for j in range(nj):
                op(PE, lambda j=j: nc.tensor.matmul(pss[:, 0:ntok], lhsT=ones_b[:, :], rhs=sq[:, j0 + j, 0:ntok],
                                                    start=(j == 0), stop=(j == nj - 1)),
                   reads=[ones_b, sq], writes=[pss], sig=(j == nj - 1))
            op(ACT, lambda: nc.scalar.activation(out=sd[:, 0:ntok], in_=pss[:, 0:ntok], func=AF.Ln,
                                                 scale=1.0 / width, bias=eps_ap), reads=[pss, epsT], writes=[sd])
            op(ACT, lambda: nc.scalar.activation(out=rs[:, 0:ntok], in_=sd[:, 0:ntok], func=AF.Exp, scale=-0.5),
               reads=[sd], writes=[rs])

        epsT = T(nc.alloc_sbuf_tensor_at("epsT", [128, 2], F32, offset=SB_BASE + 3968))
        op(POOL, lambda: nc.gpsimd.memset(epsT[:, 0:1], EPS), writes=[epsT])
        op(POOL, lambda: nc.gpsimd.memset(epsT[:, 1:2], 1.0), writes=[epsT])
        eps_ap = epsT[:, 0:1]
        one_ap = epsT[:, 1:2]

        groups = [(g, 512, g * 512, True) for g in range(8)] + [(8, 256, 4096, False)]

        def mod_dma(gi):
            g, ntok, col0, lat = groups[gi]
            if gi > 0:
                for hh in range(2):
                    bi = (2 * gi + hh) % 3
                    x_ = xs[bi]
                    dma(SP, xch[bi], x_[:, :, 0:ntok], xT_v[:, 4 * hh:4 * hh + 4, col0:col0 + ntok], writes=[x_])

        def tab_dma(gi):
            g, ntok, col0, lat = groups[gi]
            if lat:
                dma(SP, tch[g % 2], ctb[g % 2][:, :], ctab_d[:, col0:col0 + 512], writes=[ctb[g % 2]])
                dma(SP, tch[g % 2], stb[g % 2][:, :], stab_d[:, col0:col0 + 512], writes=[stb[g % 2]])
                ctb[g % 2].w = stb[g % 2].w = (tch[g % 2].sem, tch[g % 2].n)

        def mod_act(gi):
            g, ntok, col0, lat = groups[gi]
            h_ = hT[g % 2]
            r = 0 if lat else 1
            for k in range(8):
                x_ = xs[(2 * gi + k // 4) % 3]
                op(ACT, lambda k=k, x_=x_: nc.scalar.activation(
                    out=h_[:, k, 0:ntok], in_=x_[:, k % 4, 0:ntok], func=AF.Identity,
                    scale=mod_ap("sc1", k, r), bias=mod_ap("sh1", k, r)), reads=[x_, modT], writes=[h_])

        def kv_chunks(h_, ntok):
            for j, m in enumerate((3, 4)):
                p_ = proj(h_, m, ntok)
                op(ACT, lambda j=j, p_=p_: nc.scalar.activation(out=craw[:, j, 0:ntok], in_=p_[:, 0:ntok], func=AF.Copy),
                   reads=[p_], writes=[craw])
                op(ACT, lambda j=j, p_=p_: nc.scalar.activation(out=sq[:, j, 0:ntok], in_=p_[:, 0:ntok], func=AF.Square),
                   reads=[p_], writes=[sq])

        def rope_chunk(h_, ntok, col0, lat, ct, st):
            pr = proj(h_, 5, ntok)
            if lat:
                op(ACT, lambda: nc.scalar.activation(out=tB[64:96, :], in_=pr[96:128, :], func=AF.Copy),
                   reads=[pr], writes=[tB])
                op(DVE, lambda: nc.vector.tensor_tensor(out=t1[64:96, :], in0=pr[64:96, :], in1=ct[64:96, :], op=ALU.mult),
                   reads=[pr, ct], writes=[t1])
                op(DVE, lambda: nc.vector.tensor_tensor(out=t2[64:96, :], in0=tB[64:96, :], in1=st[64:96, :], op=ALU.mult),
                   reads=[tB, st], writes=[t2])
                op(DVE, lambda: nc.vector.tensor_tensor(out=KT0[64:96, col0:col0 + 512], in0=t1[64:96, :], in1=t2[64:96, :],
                                                        op=ALU.add), reads=[t1, t2], writes=[KT0])
            else:
                op(ACT, lambda: nc.scalar.activation(out=KT0[96:128, col0:col0 + ntok], in_=pr[64:96, 0:ntok], func=AF.Copy),
                   reads=[pr], writes=[KT0])

        def kv_norm(ntok, col0):
            rms_chain(0, 2, ntok, 256.0)
            for j in range(2):
                op(DVE, lambda j=j: nc.vector.tensor_tensor(out=ckvn[:, j, col0:col0 + ntok], in0=craw[:, j, 0:ntok],
                                                            in1=rs[:, 0:ntok], op=ALU.mult),
                   reads=[craw, rs], writes=[ckvn])

        def q_chunks(h_):
            for j in range(3):
                p_ = proj(h_, j, 512)
                op(ACT, lambda j=j, p_=p_: nc.scalar.activation(out=craw[:, 2 + j, :], in_=p_[:, :], func=AF.Copy),
                   reads=[p_], writes=[craw])
                op(ACT, lambda j=j, p_=p_: nc.scalar.activation(out=sq[:, 2 + j, :], in_=p_[:, :], func=AF.Square),
                   reads=[p_], writes=[sq])

        def glu_chunk(h_, cc, col0):
            pa = proj(h_, 6 + cc, 512)
            pg = proj(h_, 10 + cc, 512)
            sg = sig[cc % 2]
            op(ACT, lambda: nc.scalar.activation(out=sg[:, :], in_=pg[:, :], func=AF.Sigmoid), reads=[pg], writes=[sg])
            op(DVE, lambda: nc.vector.tensor_tensor(out=ypad[:, cc, 15 + col0:15 + col0 + 512], in0=pa[:, :], in1=sg[:, :],
                                                    op=ALU.mult), reads=[pa, sg], writes=[ypad])

        def q_heads(col0, ct, st):
            for h in range(H):
                pA, pB = ps(), ps()
                for (p_, cidx) in ((pA, 2 * h), (pB, 2 * h + 1)):
                    for k in range(3):
                        op(PE, lambda k=k, p_=p_, cidx=cidx: nc.tensor.matmul(
                            p_[:, :], lhsT=wuq_b[:, k, cidx * 128:(cidx + 1) * 128], rhs=cqn[:, k, :],
                            start=(k == 0), stop=(k == 2)), reads=[wuq_b, cqn], writes=[p_], sig=(k == 2))
                op(ACT, lambda: nc.scalar.activation(out=QT[0:64, h, col0:col0 + 512], in_=pA[0:64, :], func=AF.Copy),
                   reads=[pA], writes=[QT])
                op(ACT, lambda: nc.scalar.activation(out=QT[96:128, h, col0:col0 + 512], in_=pA[96:128, :], func=AF.Copy),
                   reads=[pA], writes=[QT])
                op(DVE, lambda: nc.vector.tensor_tensor(out=t1[64:96, :], in0=pA[64:96, :], in1=ct[64:96, :], op=ALU.mult),
                   reads=[pA, ct], writes=[t1])
                op(DVE, lambda: nc.vector.tensor_tensor(out=t2[64:96, :], in0=pB[64:96, :], in1=st[64:96, :], op=ALU.mult),
                   reads=[pB, st], writes=[t2])
                op(DVE, lambda: nc.vector.tensor_tensor(out=QT[64:96, h, col0:col0 + 512], in0=t1[64:96, :], in1=t2[64:96, :],
                                                        op=ALU.add), reads=[t1, t2], writes=[QT])

        mod_dma(0)
        tab_dma(0)
        mod_act(0)
        mod_dma(1)
        for gi, (g, ntok, col0, lat) in enumerate(groups):
            own = g < 4
            h_ = hT[g % 2]
            ct, st = ctb[g % 2], stb[g % 2]
            if gi + 1 < len(groups):
                mod_act(gi + 1)
            if gi + 2 < len(groups):
                mod_dma(gi + 2)
            if gi + 1 < len(groups):
                tab_dma(gi + 1)
            kv_chunks(h_, ntok)
            rope_chunk(h_, ntok, col0, lat, ct, st)
            if own:
                q_chunks(h_)
                glu_chunk(h_, 0, col0)
                glu_chunk(h_, 1, col0)
                kv_norm(ntok, col0)
                glu_chunk(h_, 2, col0)
                rms_chain(2, 3, 512, 384.0)
                for j in range(3):
                    op(DVE, lambda j=j: nc.vector.tensor_tensor(out=cqn[:, j, :], in0=craw[:, 2 + j, :], in1=rs[:, :], op=ALU.mult),
                       reads=[craw, rs], writes=[cqn])
                glu_chunk(h_, 3, col0)
                q_heads(col0, ct, st)
            else:
                kv_norm(ntok, col0)
        xTh_v = xTh_d.rearrange("(k p) n -> p k n", p=128)
        h_ = modulate(9, 32, xTh_v, 0, True)
        for cc in range(4):
            pa = proj(h_, 6 + cc, 32)
            pg = proj(h_, 10 + cc, 32)
            sg = sig[cc % 2]
            op(ACT, lambda pg=pg, sg=sg: nc.scalar.activation(out=sg[:, 0:32], in_=pg[:, 0:32], func=AF.Sigmoid),
               reads=[pg], writes=[sg])
            op(DVE, lambda pa=pa, sg=sg: nc.vector.tensor_tensor(out=t1[:, 0:32], in0=pa[:, 0:32], in1=sg[:, 0:32], op=ALU.mult),
               reads=[pa, sg], writes=[t1])
            op(DVE, lambda cc=cc: nc.vector.tensor_tensor(out=ypad[:, cc, 0:15], in0=t1[:, 0:15], in1=hmask[:, 0:15], op=ALU.mult),
               reads=[t1, hmask], writes=[ypad])
            op(DVE, lambda cc=cc: nc.vector.tensor_tensor(out=ypad[:, cc, 15 + NOWN:30 + NOWN], in0=t1[:, 15:30],
                                                          in1=hmask[:, 15:30], op=ALU.mult),
               reads=[t1, hmask], writes=[ypad])

        a0 = 83.5
        KT1 = sb(a0, [128, NK], BF16)
        Vaug = [sb(a0 + 8.5 + 8.5 * i, [128, NKT, 128], BF16) for i in range(2)]
        PT = [sb(a0 + 25.5 + 2 * i, [128, 1024], BF16) for i in range(3)]
        rd = sb(a0 + 31.5, [128, 512], F32)
        zc = sb(a0 + 33.5, [128, 4, 512], F32)
        zb = sb(a0 + 41.5, [128, 4, 512], BF16)
        zsq = sb(a0 + 45.5, [128, 4, 512], BF16)
        cmean = sb(a0 + 49.5, [128, 512], F32)
        cmsq = sb(a0 + 51.5, [128, 512], F32)
        crs = sb(a0 + 53.5, [128, 512], F32)
        ctmp = [sb(a0 + 55.5 + 2 * i, [128, 512], F32) for i in range(2)]
        ngl = sb(a0 + 59.5, [128, 8], F32)
        attnT = sb(146, [128, 4, NOWN], BF16)
        convT = sb(162, [128, 4, NOWN], BF16)
        wout_b = sb(178, [128, 8, 1024], BF16)
        bc1 = sb(194, [128, 3, 1024], F32)
        spair = [T(pairs[1][:, :]), T(pairs[2][:, :])]
        att_tiles = [KT1, Vaug[0], Vaug[1], rd, zc, zb, zsq, cmean, cmsq, crs, ctmp[0], ctmp[1], ngl,
                     attnT, convT, wout_b] + PT + spair
        cx.fence(att_tiles)
        cx.alias([bst[0], bst[1]], [xs[2]])

        woch = cx.chan()
        KTb = [KT0, KT1]
        op(POOL, lambda: nc.gpsimd.memset(Vaug[0][:, :, 64:128], 1.0), writes=[Vaug[0]])
        op(POOL, lambda: nc.gpsimd.memset(Vaug[1][:, :, 0:64], 1.0), writes=[Vaug[1]])
        op(POOL, lambda: nc.gpsimd.tensor_copy(out=KT1[64:128, :], in_=KT0[64:128, :]), reads=[KT0], writes=[KT1])
        op(POOL, lambda: nc.gpsimd.tensor_scalar(out=ngl[:, :], in0=cpar[:, 4:12], scalar1=-1.0, scalar2=None, op0=ALU.mult),
           reads=[cpar], writes=[ngl])
        emit_precast()

        kgroups = [(i * 512, 512) for i in range(8)] + [(4096, 256)]
        MISC = [6, 7]

        def prep_k(h, c0, n):
            kt_ = KTb[h % 2]
            p_ = ps(MISC)
            for k in range(2):
                op(PE, lambda k=k: nc.tensor.matmul(
                    p_[0:64, 0:n], lhsT=wuk_b[:, k, h * 64:(h + 1) * 64], rhs=ckvn[:, k, c0:c0 + n],
                    start=(k == 0), stop=(k == 1)), reads=[wuk_b, ckvn], writes=[p_], sig=(k == 1))
            op(DVE, lambda: nc.vector.tensor_copy(out=kt_[0:64, c0:c0 + n], in_=p_[0:64, 0:n]), reads=[p_], writes=[kt_])

        def prep_v(h, t0):
            va_ = Vaug[h % 2]
            voff = 0 if h % 2 == 0 else 64
            nt = min(8, NKT - t0)
            p_ = ps(MISC)
            for j in range(nt):
                kt = t0 + j
                for k in range(2):
                    op(PE, lambda k=k, j=j, kt=kt: nc.tensor.matmul(
                        p_[:, j * 64:(j + 1) * 64], lhsT=ckvn[:, k, kt * 128:(kt + 1) * 128],
                        rhs=wuv_b[:, k, h * 64:(h + 1) * 64], start=(k == 0), stop=(k == 1), skip_group_check=True),
                       reads=[ckvn, wuv_b], writes=[p_], sig=(k == 1 and j == nt - 1))
            op(DVE, lambda: nc.vector.tensor_copy(
                out=va_[:, t0:t0 + nt, voff:voff + 64],
                in_=p_[:, 0:nt * 64].rearrange("p (t d) -> p t d", d=64)), reads=[p_], writes=[va_])

        def prep_units(h):
            us = [lambda c0=c0, n=n: prep_k(h, c0, n) for (c0, n) in kgroups]
            us += [lambda t0=t0: prep_v(h, t0) for t0 in range(0, NKT, 8)]
            return us

        tick = [0]
        defq = []

        seqc = [0]

        def at(delay, fn):
            seqc[0] += 1
            defq.append((tick[0] + delay, seqc[0], fn))

        def run_due(flush=False):
            while True:
                due = [e for e in defq if flush or e[0] <= tick[0]]
                if not due:
                    break
                due.sort()
                e = due[0]
                defq.remove(e)
                e[2]()

        zc2 = sb(178, [128, 4, 512], F32)
        zcb = [zc, zc2]
        cx.fence([zc2])

        def conv_tap(tg, cc, j):
            c0 = tg * 512
            z_ = zcb[tg % 2]
            if j == 0:
                op(DVE, lambda: nc.vector.tensor_scalar(out=z_[:, cc, :], in0=ypad[:, cc, c0:c0 + 512],
                                                        scalar1=wdw[:, cc * 31:cc * 31 + 1], scalar2=cpar[:, cc:cc + 1],
                                                        op0=ALU.mult, op1=ALU.add),
                   reads=[ypad, wdw, cpar], writes=[z_])
            else:
                op(DVE, lambda: nc.vector.scalar_tensor_tensor(
                    out=z_[:, cc, :], in0=ypad[:, cc, c0 + j:c0 + j + 512], scalar=wdw[:, cc * 31 + j:cc * 31 + j + 1],
                    in1=z_[:, cc, :], op0=ALU.mult, op1=ALU.add), reads=[ypad, z_, wdw], writes=[z_])

        def conv_sq(tg, cc):
            z_ = zcb[tg % 2]
            op(DVE, lambda: nc.vector.tensor_copy(out=zb[:, cc, :], in_=z_[:, cc, :]), reads=[z_], writes=[zb])
            op(DVE, lambda: nc.vector.tensor_tensor(out=zsq[:, cc, :], in0=z_[:, cc, :], in1=z_[:, cc, :], op=ALU.mult),
               reads=[z_], writes=[zsq])

        s12 = [None, None]

        def ln_pe():
            s12[0], s12[1] = ps(MISC), ps(MISC)
            for (s_, src) in ((s12[0], zb), (s12[1], zsq)):
                for cc in range(4):
                    op(PE, lambda cc=cc, s_=s_, src=src: nc.tensor.matmul(
                        s_[:, :], lhsT=ones_b[:, :], rhs=src[:, cc, :], start=(cc == 0), stop=(cc == 3)),
                       reads=[ones_b, src], writes=[s_], sig=(cc == 3))

        def ln_stats():
            s1, s2 = s12
            op(DVE, lambda: nc.vector.tensor_scalar(out=cmean[:, :], in0=s1[:, :], scalar1=1.0 / 512, scalar2=None, op0=ALU.mult),
               reads=[s1], writes=[cmean])
            op(DVE, lambda: nc.vector.tensor_tensor(out=cmsq[:, :], in0=cmean[:, :], in1=cmean[:, :], op=ALU.mult),
               reads=[cmean], writes=[cmsq])
            op(DVE, lambda: nc.vector.scalar_tensor_tensor(out=cmsq[:, :], in0=s2[:, :], scalar=1.0 / 512, in1=cmsq[:, :],
                                                           op0=ALU.mult, op1=ALU.subtract), reads=[s2, cmsq], writes=[cmsq])

        def ln_rstd():
            op(ACT, lambda: nc.scalar.activation(out=crs[:, :], in_=cmsq[:, :], func=AF.Ln, bias=eps_ap),
               reads=[cmsq, epsT], writes=[crs])
            op(ACT, lambda: nc.scalar.activation(out=crs[:, :], in_=crs[:, :], func=AF.Exp, scale=-0.5),
               reads=[crs], writes=[crs])

        def ln_norm(tg, cc):
            z_ = zcb[tg % 2]
            op(DVE, lambda: nc.vector.tensor_tensor(out=z_[:, cc, :], in0=z_[:, cc, :], in1=cmean[:, :], op=ALU.subtract),
               reads=[z_, cmean], writes=[z_])
            op(DVE, lambda: nc.vector.tensor_tensor(out=z_[:, cc, :], in0=z_[:, cc, :], in1=crs[:, :], op=ALU.mult),
               reads=[z_, crs], writes=[z_])

        def silu_exp(tg, cc):
            z_ = zcb[tg % 2]
            op(ACT, lambda: nc.scalar.activation(out=ctmp[cc % 2][:, :], in_=z_[:, cc, :], func=AF.Exp,
                                                 scale=ngl[:, cc:cc + 1], bias=ngl[:, 4 + cc:5 + cc]),
               reads=[z_, ngl], writes=[ctmp[cc % 2]])

        def silu_fin(tg, cc):
            c0 = tg * 512
            z_ = zcb[tg % 2]
            e_ = ctmp[cc % 2]
            op(ACT, lambda: nc.scalar.activation(out=e_[:, :], in_=e_[:, :], func=AF.Ln, bias=one_ap), reads=[e_, epsT], writes=[e_])
            op(ACT, lambda: nc.scalar.activation(out=e_[:, :], in_=e_[:, :], func=AF.Exp, scale=-1.0), reads=[e_], writes=[e_])
            op(DVE, lambda: nc.vector.tensor_scalar(out=z_[:, cc, :], in0=z_[:, cc, :], scalar1=cpar[:, 4 + cc:5 + cc],
                                                    scalar2=cpar[:, 8 + cc:9 + cc], op0=ALU.mult, op1=ALU.add),
               reads=[z_, cpar], writes=[z_])
            op(DVE, lambda: nc.vector.tensor_tensor(out=convT[:, cc, c0:c0 + 512], in0=z_[:, cc, :], in1=e_[:, :], op=ALU.mult),
               reads=[z_, e_], writes=[convT])

        conv_list = [(tg, cc, j) for tg in range(4) for cc in range(4) for j in range(31)]
        conv_done = [False]

        def load_wout():
            cx.fence([wout_b])
            dma(SP, woch, wout_b[:, :, :], woutb_d.rearrange("(k p) n -> p k n", p=128), writes=[wout_b],
                extra=list(woutbT.evs.values()))

        def emit_conv(n):
            for _ in range(n):
                if not conv_list:
                    return
                tg, cc, j = conv_list.pop(0)
                conv_tap(tg, cc, j)
                if j == 30:
                    at(12, lambda tg=tg, cc=cc: conv_sq(tg, cc))
                    if cc == 3:
                        at(20, lambda: (ln_pe(), ln_stats()))
                        at(34, ln_rstd)
                        for c2 in range(4):
                            at(40 + 4 * c2, lambda c2=c2, tg=tg: ln_norm(tg, c2))
                            at(60 + 8 * c2, lambda c2=c2, tg=tg: silu_exp(tg, c2))
                            at(66 + 8 * c2, lambda c2=c2, tg=tg: silu_fin(tg, c2))
                        if tg == 3:
                            at(100, load_wout)

        NP = NKT // 2
        it_ = [0]

        def pv(h, qg, p, pt):
            va_ = Vaug[h % 2]
            acc = banks[qg % 2]
            q0 = qg * 512
            for j in range(2):
                k2 = 2 * p + j
                op(PE, lambda j=j, k2=k2: nc.tensor.matmul(
                    acc[:, :], lhsT=va_[:, k2, :], rhs=pt[:, j * 512:(j + 1) * 512], start=(k2 == 0), stop=(k2 == NKT - 1)),
                   reads=[va_, pt], writes=[acc], sig=(j == 1))
            if p != NP - 1:
                return
            o_lo, d_lo = (0, 64) if h % 2 == 0 else (64, 0)

            def norm_act():
                op(ACT, lambda: nc.scalar.activation(out=rd[o_lo:o_lo + 64, :], in_=acc[d_lo:d_lo + 64, :], func=AF.Ln),
                   reads=[acc], writes=[rd])
                op(ACT, lambda: nc.scalar.activation(out=rd[o_lo:o_lo + 64, :], in_=rd[o_lo:o_lo + 64, :], func=AF.Exp, scale=-1.0),
                   reads=[rd], writes=[rd])

            def norm_dve():
                op(DVE, lambda: nc.vector.tensor_tensor(out=attnT[o_lo:o_lo + 64, h // 2, q0:q0 + 512], in0=acc[o_lo:o_lo + 64, :],
                                                        in1=rd[o_lo:o_lo + 64, :], op=ALU.mult), reads=[acc, rd], writes=[attnT])

            at(6, norm_act)
            at(10, norm_dve)

        for u in prep_units(0):
            u()
        steps = [(h, qg, p) for h in range(H) for qg in range(4) for p in range(NP)]
        pend = []
        units = []
        gstep = 0
        for (h, qg, p) in steps:
            if p == 0 and qg == 1 and h + 1 < H:
                assert not units
                units = prep_units(h + 1)
            kt_ = KTb[h % 2]
            q0 = qg * 512
            s_ = spair[p % 2]
            for j in range(2):
                kt = 2 * p + j
                op(PE, lambda j=j, kt=kt: nc.tensor.matmul(
                    s_[:, j * 512:(j + 1) * 512], lhsT=kt_[:, kt * 128:(kt + 1) * 128], rhs=QT[:, h, q0:q0 + 512],
                    start=True, stop=True), reads=[kt_, QT], writes=[s_], sig=(j == 1))
            pt = PT[gstep % 3]
            gstep += 1
            op(ACT, lambda: nc.scalar.activation(out=pt[:, :], in_=s_[:, :], func=AF.Exp, scale=SM_SCALE),
               reads=[s_], writes=[pt])
            pend.append((h, qg, p, pt))
            if len(pend) > 2:
                pv(*pend.pop(0))
            if units and (it_[0] % 3 == 0 or (qg == 3 and NP - p <= len(units))):
                units.pop(0)()
            elif it_[0] % 12 == 6:
                modB_one()
            it_[0] += 1
            emit_conv(2 if it_[0] % 8 == 0 else 1)
            tick[0] += 2
            run_due()
        while pend:
            pv(*pend.pop(0))
        assert not units
        while modB or modB_p:
            modB_one()
        emit_conv(10 ** 6)
        run_due(flush=True)
        cx.alias([bc1], [bst[0], bst[1], rowB[0], rowB[1]])
        cx.fence([banks[2], banks[3], banks[4], banks[5]])

        w1_b = sb(4, [128, 8, 4096], BF16)
        w2_b = sb(68, [128, 32, 1024], BF16)
        xa = [sb(132 + 4 * i, [128, 1024], F32) for i in range(3)]
        tsc = sb(144, [128, 512], F32)
        h2st = [sb(124 + 2 * i, [128, 8, 128], BF16) for i in range(2)]
        cx.fence([w1_b, w2_b, xa[0], xa[1], xa[2], tsc, h2st[0], h2st[1]])
        w1ch = [cx.chan() for _ in range(8)]
        w2ch = [cx.chan() for _ in range(8)]
        w1t = [T() for _ in range(8)]
        w2t = [T() for _ in range(8)]
        w1b_v = w1b_d.rearrange("(k p) n -> p k n", p=128)
        w2b_v = w2b_d.rearrange("(f p) n -> p f n", p=128)
        snap_f = list(cx.snapshot().values())

        def load_mlp_w(c):
            dma(POOL, w1ch[c], w1_b[:, :, c * 512:(c + 1) * 512], w1b_v[:, :, c * 512:(c + 1) * 512],
                writes=[w1t[c]], extra=list(w1bT.evs.values()) + (snap_f if c == 0 else []))
            if c < 7:
                dma(POOL, w2ch[c], w2_b[:, 4 * c:4 * c + 4, :], w2b_v[:, 4 * c:4 * c + 4, :], writes=[w2t[c]],
                    extra=list(w2bT.evs.values()))
            else:
                al = []
                for t_ in h2st:
                    al += ([t_.w] if t_.w else []) + list(t_.r.values())
                dma(SP, w2ch[c], w2_b[:, 4 * c:4 * c + 4, :], w2b_v[:, 4 * c:4 * c + 4, :], writes=[w2t[c]],
                    extra=list(w2bT.evs.values()) + al)

        def bcast_gate(dst_tile, dst_slot, jbase, scratch):
            for c in range(8):
                col = 2 * (jbase + c)
                op(DVE, lambda c=c, col=col: nc.vector.tensor_scalar(out=scratch[1](slice(c * 128, (c + 1) * 128)), in0=ident[:, :],
                                                                     scalar1=modT[:, col:col + 1], scalar2=None, op0=ALU.mult),
                   reads=[ident, modT], writes=[scratch[0]])
            for hf in range(2):
                p_ = ps(range(6, 8))
                op(PE, lambda hf=hf, p_=p_: nc.tensor.matmul(p_[:, :], lhsT=ones_f[:, :], rhs=scratch[1](slice(hf * 512, (hf + 1) * 512)),
                                                              start=True, stop=True), reads=[ones_f, scratch[0]], writes=[p_])
                op(ACT, lambda hf=hf, p_=p_: nc.scalar.activation(out=dst_tile[:, dst_slot, hf * 512:(hf + 1) * 512], in_=p_[:, :],
                                                                  func=AF.Copy), reads=[p_], writes=[dst_tile])

        bch = cx.chan()
        bcast_gate(bc1, 0, 32, (xa[2], lambda sl: xa[2][:, sl]))
        mv = modT[:, 0:96].rearrange("p (j r) -> p j r", r=2)
        op(DVE, lambda: nc.vector.tensor_tensor(out=ab2[:, 0:8], in0=lnT[:, 0:8], in1=mv[:, 24:32, 0], op=ALU.mult),
           reads=[lnT, modT], writes=[ab2])
        op(DVE, lambda: nc.vector.tensor_tensor(out=ab2[:, 8:16], in0=lnT[:, 8:16], in1=mv[:, 24:32, 0], op=ALU.mult),
           reads=[lnT, modT], writes=[ab2])
        op(DVE, lambda: nc.vector.tensor_tensor(out=ab2[:, 8:16], in0=ab2[:, 8:16], in1=mv[:, 16:24, 0], op=ALU.add),
           reads=[ab2, modT], writes=[ab2])

        x1Tt = [T() for _ in range(16)]
        h2Tt = [T() for _ in range(16)]
        xrch = [cx.chan(), cx.chan(), cx.chan()]
        sch = [cx.chan(), cx.chan()]
        hch = [cx.chan(), cx.chan()]
        h2s_v = h2s_d.rearrange("(k p) n -> p k n", p=128)

        def layer_norm(xt, xv, gi_ap, bi_ap, gb_tile, st0, affine=True):
            for hf in range(2):
                op(DVE, lambda hf=hf: nc.vector.bn_stats(out=stat[:, st0 + hf * 6:st0 + hf * 6 + 6],
                                                         in_=xv(slice(hf * 512, (hf + 1) * 512))), reads=[xt], writes=[stat])
            op(DVE, lambda: nc.vector.bn_aggr(out=stat[:, st0 + 12:st0 + 14], in_=stat[:, st0:st0 + 12]), reads=[stat], writes=[stat])
            op(ACT, lambda: nc.scalar.activation(out=stat[:, st0 + 14:st0 + 15], in_=stat[:, st0 + 13:st0 + 14], func=AF.Sqrt,
                                                 bias=eps_ap), reads=[stat, epsT], writes=[stat])
            op(DVE, lambda: nc.vector.reciprocal(out=stat[:, st0 + 15:st0 + 16], in_=stat[:, st0 + 14:st0 + 15]),
               reads=[stat], writes=[stat])
            op(DVE, lambda: nc.vector.tensor_scalar(out=stat[:, st0 + 14:st0 + 15], in0=stat[:, st0 + 12:st0 + 13],
                                                    scalar1=stat[:, st0 + 15:st0 + 16], scalar2=-1.0, op0=ALU.mult, op1=ALU.mult),
               reads=[stat], writes=[stat])
            if affine:
                op(ACT, lambda: nc.scalar.activation(out=xv(slice(0, 1024)), in_=xv(slice(0, 1024)), func=AF.Identity,
                                                     scale=stat[:, st0 + 15:st0 + 16], bias=stat[:, st0 + 14:st0 + 15]),
                   reads=[xt, stat], writes=[xt])
            else:
                op(DVE, lambda: nc.vector.tensor_scalar(out=xv(slice(0, 1024)), in0=xv(slice(0, 1024)),
                                                        scalar1=stat[:, st0 + 15:st0 + 16], scalar2=stat[:, st0 + 14:st0 + 15],
                                                        op0=ALU.mult, op1=ALU.add), reads=[xt, stat], writes=[xt])
            if affine:
                op(DVE, lambda: nc.vector.tensor_tensor(out=xv(slice(0, 1024)), in0=xv(slice(0, 1024)), in1=gi_ap, op=ALU.mult),
                   reads=[xt, gb_tile], writes=[xt])
                op(DVE, lambda: nc.vector.tensor_tensor(out=xv(slice(0, 1024)), in0=xv(slice(0, 1024)), in1=bi_ap, op=ALU.add),
                   reads=[xt, gb_tile], writes=[xt])

        bc2 = sb(146, [128, 5, 1024], F32)
        x1t = [sb(166 + 8 * i, [128, 2, 1024], F32) for i in range(2)]
        h2g = [sb(182 + 4 * i, [128, 8, 256], BF16) for i in range(2)]
        rr = [sb(190 + i, [128, 256], F32) for i in range(3)]
        h1 = [sb(193 + 0.5 * i, [128, 256], BF16) for i in range(3)]
        tsc2 = sb(195, [128, 512], F32)
        astg = sb(197, [128, 2, 1024], F32)
        lch = [cx.chan(), cx.chan()]
        lhch = [cx.chan(), cx.chan()]
        b2ch = cx.chan()

        def load_h(g):
            dma(SP, lhch[g % 2], h2g[g % 2][:, :, :], h2s_v[:, :, g * 256:(g + 1) * 256], reads=[h2Tt[2 * g], h2Tt[2 * g + 1]],
                writes=[h2g[g % 2]])

        def load_x(g):
            for tt in range(2):
                dma(SP, lch[g % 2], x1t[g % 2][:, tt, :], x1s_d[g * 256 + tt * 128:g * 256 + (tt + 1) * 128, :],
                    reads=[x1Tt[2 * g + tt]], writes=[x1t[g % 2]])

        def mlp_prefetch_1():
            cx.alias([bc2, x1t[0], x1t[1], h2g[0], h2g[1]], [attnT, convT, wout_b, zc2])
            dma(SP, b2ch, bc2[:, 1, :], lnp_d[2:3, :].partition_broadcast(128), writes=[bc2])
            dma(SP, b2ch, bc2[:, 2, :], lnp_d[3:4, :].partition_broadcast(128), writes=[bc2])
            dma(SP, b2ch, bc2[:, 3, :], lnp_d[0:1, :].partition_broadcast(128), writes=[bc2])
            dma(SP, b2ch, bc2[:, 4, :], lnp_d[1:2, :].partition_broadcast(128), writes=[bc2])
            load_h(0)
            load_x(0)
            load_x(1)

        def mlp_prefetch_2():
            cx.alias([tsc2, astg] + rr + h1, [bc1, wout_b])
            bcast_gate(bc2, 0, 40, (astg, lambda sl: astg[:, 0, sl]))

        pa_ps = {}

        def stage_A(tt):
            xt = xa[tt % 3]
            r0 = tt * 128
            dma(SP, xrch[tt % 3], xt[:, :], xown_d[r0:r0 + 128, :], writes=[xt])
            pa_ps[tt] = []
            for hf in range(2):
                p_ = banks[2 * (tt % 3) + hf]
                pa_ps[tt].append(p_)
                for k in range(8):
                    src = attnT if k < 4 else convT
                    op(PE, lambda k=k, src=src: nc.tensor.matmul(
                        p_[:, :], lhsT=src[:, k % 4, r0:r0 + 128], rhs=wout_b[:, k, hf * 512:(hf + 1) * 512],
                        start=(k == 0), stop=(k == 7)), reads=[src, wout_b], writes=[p_], sig=(k == 7))

        def stage_B(tt):
            xt = xa[tt % 3]
            r0 = tt * 128
            for hf in range(2):
                p_ = pa_ps[tt][hf]
                op(DVE, lambda: nc.vector.tensor_tensor(out=tsc[:, :], in0=p_[:, :], in1=bc1[:, 0, hf * 512:(hf + 1) * 512],
                                                        op=ALU.mult), reads=[p_, bc1], writes=[tsc])
                op(DVE, lambda: nc.vector.scalar_tensor_tensor(
                    out=xt[:, hf * 512:(hf + 1) * 512], in0=xt[:, hf * 512:(hf + 1) * 512], scalar=ALPHA, in1=tsc[:, :],
                    op0=ALU.mult, op1=ALU.add), reads=[xt, tsc], writes=[xt])
            layer_norm(xt, lambda s_: xt[:, s_], None, None, bc1, 16 * (tt % 2), affine=False)
            dma(SP, sch[tt % 2], x1s_d[r0:r0 + 128, :], xt[:, :], reads=[xt], writes=[x1Tt[tt]])

        def stage_C(tt):
            xt = xa[tt % 3]
            r0 = tt * 128
            hs = h2st[tt % 2]
            for half in range(2):
                p_ = banks[6 + half]
                for kk in range(4):
                    k = 4 * half + kk
                    op(PE, lambda k=k, kk=kk: nc.tensor.transpose(
                        out=p_[:, kk * 128:(kk + 1) * 128], in_=xt[:, k * 128:(k + 1) * 128], identity=ident[:, :]),
                       reads=[xt, ident], writes=[p_], sig=(kk == 3))
                for kk in range(4):
                    k = 4 * half + kk
                    op(ACT, lambda k=k, kk=kk: nc.scalar.activation(
                        out=hs[:, k, :], in_=p_[:, kk * 128:(kk + 1) * 128], func=AF.Identity,
                        scale=ab2[:, k:k + 1], bias=ab2[:, 8 + k:9 + k]), reads=[p_, ab2], writes=[hs])
            dma(SP, hch[tt % 2], h2s_v[:, :, r0:r0 + 128], hs[:, :, :], reads=[hs], writes=[h2Tt[tt]])

        stage_A(0)
        stage_A(1)
        stage_B(0)
        for tt in range(16):
            if tt + 2 < 16:
                stage_A(tt + 2)
            if tt + 1 < 16:
                stage_B(tt + 1)
            stage_C(tt)
            if tt % 2 == 1:
                load_mlp_w(tt // 2)
            if tt == 13:
                mlp_prefetch_1()
            if tt == 14:
                mlp_prefetch_2()

        for tt_ in range(16):
            x1Tt[tt_].w = (sch[tt_ % 2].sem, sch[tt_ % 2].n)
            h2Tt[tt_].w = (hch[tt_ % 2].sem, hch[tt_ % 2].n)
        och = [cx.chan(), cx.chan()]
        outT = T()

        def epilogue_pieces(g):
            xg = x1t[g % 2]
            c0 = g * 256
            pcs = []
            for tt in range(2):
                st0 = 16 * tt
                xv = lambda s_, tt=tt: xg[:, tt, s_]

                def p1(tt=tt, xv=xv, st0=st0):
                    op(DVE, lambda: nc.vector.tensor_tensor(out=xg[:, tt, :], in0=xg[:, tt, :], in1=bc2[:, 3, :], op=ALU.mult),
                       reads=[xg, bc2], writes=[xg])
                    op(DVE, lambda: nc.vector.tensor_tensor(out=xg[:, tt, :], in0=xg[:, tt, :], in1=bc2[:, 4, :], op=ALU.add),
                       reads=[xg, bc2], writes=[xg])
                    op(DVE, lambda: nc.vector.tensor_tensor(out=astg[:, tt, :], in0=astg[:, tt, :], in1=bc2[:, 0, :], op=ALU.mult),
                       reads=[astg, bc2], writes=[astg])
                    op(DVE, lambda: nc.vector.scalar_tensor_tensor(out=xg[:, tt, :], in0=xg[:, tt, :], scalar=ALPHA,
                                                                   in1=astg[:, tt, :], op0=ALU.mult, op1=ALU.add),
                       reads=[xg, astg], writes=[xg])
                    for hf in range(2):
                        op(DVE, lambda hf=hf: nc.vector.bn_stats(out=stat[:, st0 + hf * 6:st0 + hf * 6 + 6],
                                                                 in_=xv(slice(hf * 512, (hf + 1) * 512))), reads=[xg], writes=[stat])
                    op(DVE, lambda: nc.vector.bn_aggr(out=stat[:, st0 + 12:st0 + 14], in_=stat[:, st0:st0 + 12]),
                       reads=[stat], writes=[stat])

                def p2(st0=st0):
                    op(ACT, lambda: nc.scalar.activation(out=stat[:, st0 + 14:st0 + 15], in_=stat[:, st0 + 13:st0 + 14], func=AF.Sqrt,
                                                         bias=eps_ap), reads=[stat, epsT], writes=[stat])
                    op(DVE, lambda: nc.vector.reciprocal(out=stat[:, st0 + 15:st0 + 16], in_=stat[:, st0 + 14:st0 + 15]),
                       reads=[stat], writes=[stat])
                    op(DVE, lambda: nc.vector.tensor_scalar(out=stat[:, st0 + 14:st0 + 15], in0=stat[:, st0 + 12:st0 + 13],
                                                            scalar1=stat[:, st0 + 15:st0 + 16], scalar2=-1.0, op0=ALU.mult,
                                                            op1=ALU.mult), reads=[stat], writes=[stat])

                def p3(tt=tt, xv=xv, st0=st0):
                    op(ACT, lambda: nc.scalar.activation(out=xv(slice(0, 1024)), in_=xv(slice(0, 1024)), func=AF.Identity,
                                                         scale=stat[:, st0 + 15:st0 + 16], bias=stat[:, st0 + 14:st0 + 15]),
                       reads=[xg, stat], writes=[xg])

                def p4(tt=tt, xv=xv):
                    op(DVE, lambda: nc.vector.tensor_tensor(out=xv(slice(0, 1024)), in0=xv(slice(0, 1024)), in1=bc2[:, 1, :],
                                                            op=ALU.mult), reads=[xg, bc2], writes=[xg])
                    op(DVE, lambda: nc.vector.tensor_tensor(out=xv(slice(0, 1024)), in0=xv(slice(0, 1024)), in1=bc2[:, 2, :],
                                                            op=ALU.add), reads=[xg, bc2], writes=[xg])
                    dma(SP, och[g % 2], out_d[c0 + tt * 128:c0 + (tt + 1) * 128, :], xg[:, tt, :], reads=[xg], writes=[outT])

                base = 1 + 10 * tt
                pcs += [(base, p1), (base + 6, p2), (base + 9, p3), (base + 12, p4)]
            if g + 2 < 8:
                pcs.append((26, lambda: load_x(g + 2)))
            pcs.sort(key=lambda e: e[0])
            return pcs

        pend_epi = []
        for g in range(8):
            xg, hg = x1t[g % 2], h2g[g % 2]
            c0 = g * 256
            if g + 1 < 8:
                load_h(g + 1)
            accs = [[banks[2 * tt + hf] for hf in range(2)] for tt in range(2)]

            def down(f):
                hb = h1[f % 3]
                for tt in range(2):
                    for hf in range(2):
                        op(PE, lambda tt=tt, hf=hf: nc.tensor.matmul(
                            accs[tt][hf][:, :], lhsT=hb[:, tt * 128:(tt + 1) * 128], rhs=w2_b[:, f, hf * 512:(hf + 1) * 512],
                            start=(f == 0), stop=(f == 31)), reads=[hb, w2t[f // 4]], writes=[accs[tt][hf]])

            for f in range(32):
                pz = ps(range(4, 8))
                for k in range(8):
                    op(PE, lambda k=k: nc.tensor.matmul(pz[:, 0:256], lhsT=w1_b[:, k, f * 128:(f + 1) * 128], rhs=hg[:, k, :],
                                                        start=(k == 0), stop=(k == 7)),
                       reads=[w1t[f // 4], hg], writes=[pz], sig=(k == 7))
                op(ACT, lambda: nc.scalar.activation(out=rr[f % 3][:, :], in_=pz[:, 0:256], func=AF.Relu),
                   reads=[pz], writes=[rr[f % 3]])
                op(POOL, lambda: nc.gpsimd.tensor_tensor(out=h1[f % 3][:, :], in0=rr[f % 3][:, :], in1=rr[f % 3][:, :], op=ALU.mult),
                   reads=[rr[f % 3]], writes=[h1[f % 3]])
                if f >= 2:
                    down(f - 2)
                while pend_epi and pend_epi[0][0] <= f:
                    pend_epi.pop(0)[1]()
            down(30)
            down(31)
            while pend_epi:
                pend_epi.pop(0)[1]()
            for tt in range(2):
                for hf in range(2):
                    a_ = accs[tt][hf]
                    op(ACT, lambda: nc.scalar.activation(out=astg[:, tt, hf * 512:(hf + 1) * 512], in_=a_[:, :], func=AF.Copy),
                       reads=[a_], writes=[astg])
            pend_epi = epilogue_pieces(g)
        while pend_epi:
            pend_epi.pop(0)[1]()
        for ch in och:
            SP.h.wait_ge(ch.sem, ch.n)
    return nc


def _rope_tables(pos):
    f32 = np.float32
    freqs = (np.float32(10000.0) ** (-(np.arange(0, 16, 2, dtype=f32)) / f32(16))).astype(f32)
    row = (pos // 64).astype(f32)
    col = (pos % 64).astype(f32)
    ar = (row[:, None] * freqs[None, :]).astype(f32)
    ac = (col[:, None] * freqs[None, :]).astype(f32)
    cosr, sinr, cosc, sinc = np.cos(ar), np.sin(ar), np.cos(ac), np.sin(ac)
    cos = np.concatenate([cosr, cosr, cosc, cosc], 1)
    sin = np.concatenate([-sinr, sinr, -sinc, sinc], 1)
    return cos.T.astype(f32), sin.T.astype(f32)


_PERM32 = np.array([(i // 16) * 16 + ((i % 16) + 8) % 16 for i in range(32)])


def kernel(x, c, ctx, c_ctx, ada_w, ada_b, w_in, q_norm_g, kv_norm_g, w_uq, w_uk, w_uv,
           w_dw, b_dw, conv_ln_g, conv_ln_b, w_out, ln1_g, ln1_b, w1, w2, ln2_g, ln2_b):
    f32 = np.float32
    A = lambda a: np.ascontiguousarray(np.asarray(a, dtype=f32))
    x, c, ctx, c_ctx = A(x), A(c), A(ctx), A(c_ctx)
    ada_w, ada_b, w_in = A(ada_w)[0], A(ada_b)[0], A(w_in)[0]
    w_uq, w_uk, w_uv = A(w_uq)[0], A(w_uk)[0], A(w_uv)[0]
    w_dw, w_out, w1, w2 = A(w_dw)[0], A(w_out)[0], A(w1)[0], A(w2)[0]
    kr = w_in[:, 640:672]
    w_in_e = np.concatenate([w_in[:, 0:640], np.zeros((D, 64), f32), kr, kr[:, _PERM32], w_in[:, 672:1696]], 1)
    blocks = []
    for h in range(H):
        qn = w_uq[:, h * 96:h * 96 + 64]
        qr = w_uq[:, h * 96 + 64:h * 96 + 96]
        blocks.append(np.concatenate([qn, qr, qr], 1))
        blocks.append(np.concatenate([np.zeros((384, 64), f32), qr[:, _PERM32], np.zeros((384, 32), f32)], 1))
    w_uq_e = np.ascontiguousarray(np.concatenate(blocks, 1))
    pk = lambda v, k: np.ascontiguousarray(v.reshape(k, 128).T)
    qg = pk(A(q_norm_g)[0], 3)
    kvg = pk(A(kv_norm_g)[0], 2)
    wdw = np.ascontiguousarray(w_dw.T.reshape(4, 128, 31).transpose(1, 0, 2).reshape(128, 124))
    cpar = np.ascontiguousarray(np.concatenate([pk(A(b_dw)[0], 4), pk(A(conv_ln_g)[0], 4), pk(A(conv_ln_b)[0], 4)], 1))
    lnp = np.ascontiguousarray(np.stack([A(ln1_g)[0], A(ln1_b)[0], A(ln2_g)[0], A(ln2_b)[0]], 0))
    adabT = np.ascontiguousarray(ada_b.reshape(48, 128).T)
    i2 = np.eye(2, dtype=f32)
    lnT = np.ascontiguousarray(np.concatenate([pk(A(ln1_g)[0], 8), pk(A(ln1_b)[0], 8)], 1))
    ident = np.eye(128, dtype=f32)
    shared = dict(ada_w=ada_w, adabT=adabT, w_in_e=np.ascontiguousarray(w_in_e), w_uq_e=w_uq_e, w_uk=w_uk, w_uv=w_uv,
                  qg=qg, kvg=kvg, wdw=wdw, cpar=cpar, w_out=w_out, w1=w1, w2=w2, lnp=lnp, i2=i2, ident=ident, lnT=lnT)
    in_maps = []
    for core in range(8):
        b, half = core // 2, core % 2
        own = slice(half * NOWN, (half + 1) * NOWN)
        oth = slice((1 - half) * NOWN, (2 - half) * NOWN)
        xb = x[b]
        xT = np.ascontiguousarray(np.concatenate([xb[own], xb[oth], ctx[b]], 0).T)
        xown = np.ascontiguousarray(xb[own])
        pos = np.concatenate([np.arange(half * NOWN, (half + 1) * NOWN), np.arange((1 - half) * NOWN, (2 - half) * NOWN)])
        cos, sin = _rope_tables(pos)
        ctab = np.zeros((128, SEQ), f32)
        stab = np.zeros((128, SEQ), f32)
        ctab[64:96] = cos
        stab[64:96] = sin
        hal = np.zeros((32, D), f32)
        hm = np.zeros((128, 32), f32)
        s0 = half * NOWN
        for j in range(15):
            pl = s0 - 15 + j
            if 0 <= pl < SEQ:
                hal[j] = xb[pl]
                hm[:, j] = 1.0
            pr_ = s0 + NOWN + j
            if 0 <= pr_ < SEQ:
                hal[15 + j] = xb[pr_]
                hm[:, 15 + j] = 1.0
        cvv = np.zeros((128, 16), f32)
        cvv[:, 0::2] = pk(c[b], 8)
        cvv[:, 1::2] = pk(c_ctx, 8)
        m = dict(shared)
        m.update(xT=xT, xown=xown, xTh=np.ascontiguousarray(hal.T), hmask=hm, cv=cvv, ctab=ctab, stab=stab)
        in_maps.append(m)
    nc = build_program()
    res = run_bass_kernel_spmd(nc, in_maps, core_ids=list(range(8)))
    out = np.zeros((4, SEQ, D), f32)
    for core in range(8):
        b, half = core // 2, core % 2
        out[b, half * NOWN:(half + 1) * NOWN] = res.results[core]["out"]
    return out
```

```python
import numpy as np
import concourse.bass as bass
import concourse.mybir as mybir
from concourse.bass_utils import run_bass_kernel_spmd

F32 = mybir.dt.float32
BF16 = mybir.dt.bfloat16
AF = mybir.ActivationFunctionType
ALU = mybir.AluOpType

D = 1024
SEQ = 4096
NOWN = 2048
NCTX = 256
NK = 4352
NKT = 34
H = 8
EPS = 1e-6
ALPHA = 2.0 ** 0.25
SM_SCALE = 96.0 ** -0.5
KB = 1024
SB_BASE = 16512


class Eng:
    def __init__(self, h, sem):
        self.h = h
        self.sem = sem
        self.n = 0
        self.waited = {}


class T:
    def __init__(self, t=None):
        self.t = t
        self.w = None
        self.r = {}

    def __getitem__(self, k):
        return self.t[k]


class Chan:
    def __init__(self, sem):
        self.sem = sem
        self.n = 0


class Ctx:
    def __init__(self, nc, es):
        self.nc = nc
        self.es = es
        self.nsem = 0
        self.pe = Eng(nc.tensor, self.sem())
        self.act = Eng(nc.scalar, self.sem())
        self.dve = Eng(nc.vector, self.sem())
        self.pool = Eng(nc.gpsimd, self.sem())
        self.sp = Eng(nc.sync, self.sem())
        self.engs = [self.pe, self.act, self.dve, self.pool, self.sp]
        self.chans = []

    def sem(self):
        self.nsem += 1
        return self.es.enter_context(self.nc.semaphore(f"s{self.nsem}"))

    def chan(self):
        c = Chan(self.sem())
        self.chans.append(c)
        return c

    def _waits(self, eng, reads, writes, extra=()):
        need = {}

        def add(ev, same_ok):
            if ev is None:
                return
            sem, val = ev
            if (not same_ok) and sem is eng.sem:
                return
            k = id(sem)
            if k not in need or need[k][1] < val:
                need[k] = (sem, val)

        same = eng.h is not self.nc.tensor
        for t in reads:
            add(t.w, True)
        for t in writes:
            add(t.w, same)
            for ev in t.r.values():
                add(ev, same)
        for ev in extra:
            add(ev, True)
        for k, (sem, val) in need.items():
            if eng.waited.get(k, 0) < val:
                eng.h.wait_ge(sem, val)
                eng.waited[k] = val

    def op(self, eng, fn, reads=(), writes=(), sig=True, extra=()):
        self._waits(eng, reads, writes, extra)
        ins = fn()
        if sig:
            eng.n += 1
            ins.then_inc(eng.sem, 1)
            ev = (eng.sem, eng.n)
        else:
            ev = (eng.sem, eng.n + 1)
        for t in reads:
            t.r[id(eng.sem)] = ev
        for t in writes:
            t.w = ev
            t.r = {}
        return ev

    def dma(self, eng, ch, out, in_, reads=(), writes=(), extra=(), **kw):
        self._waits(eng, reads, writes, extra)
        ins = eng.h.dma_start(out=out, in_=in_, **kw)
        ch.n += 16
        ins.then_inc(ch.sem, 16)
        ev = (ch.sem, ch.n)
        for t in reads:
            t.r[id(ch.sem)] = ev
        for t in writes:
            t.w = ev
            t.r = {}
        return ev

    def snapshot(self):
        evs = {}
        for e in self.engs:
            if e.n > 0:
                evs[id(e.sem)] = (e.sem, e.n)
        for c in self.chans:
            if c.n > 0 and not getattr(c, "nofence", False):
                evs[id(c.sem)] = (c.sem, c.n)
        return evs

    def alias(self, new, old):
        ev = {}
        for t in old:
            for e in ([t.w] if t.w else []) + list(t.r.values()):
                k = id(e[0])
                if k not in ev or ev[k][1] < e[1]:
                    ev[k] = e
        for t in new:
            t.w = None
            t.r = dict(ev)

    def fence(self, tiles):
        snap = self.snapshot()
        for t in tiles:
            t.w = None
            t.r = dict(snap)


def build_program():
    from contextlib import ExitStack

    nc = bass.Bass("TRN2", target_bir_lowering=False)

    def din(name, shape, dt=F32):
        return nc.dram_tensor(name, list(shape), dt, kind="ExternalInput").ap()

    xT_d = din("xT", [D, NK])
    xown_d = din("xown", [NOWN, D])
    xTh_d = din("xTh", [D, 32])
    hmask_d = din("hmask", [128, 32])
    cv_d = din("cv", [128, 16])
    adaw_d = din("ada_w", [D, 6 * D])
    adabT_d = din("adabT", [128, 48])
    win_d = din("w_in_e", [D, 1792])
    wuq_d = din("w_uq_e", [384, 2048])
    wuk_d = din("w_uk", [256, 512])
    wuv_d = din("w_uv", [256, 512])
    qg_d = din("qg", [128, 3])
    kvg_d = din("kvg", [128, 2])
    wdw_d = din("wdw", [128, 4 * 31])
    cpar_d = din("cpar", [128, 12])
    wout_d = din("w_out", [D, D])
    w1_d = din("w1", [D, 4 * D])
    w2_d = din("w2", [4 * D, D])
    lnp_d = din("lnp", [4, D])
    ctab_d = din("ctab", [128, SEQ])
    stab_d = din("stab", [128, SEQ])
    i2_d = din("i2", [2, 2])
    lnT_d = din("lnT", [128, 16])
    ident_d = din("ident", [128, 128])
    out_d = nc.dram_tensor("out", [NOWN, D], F32, kind="ExternalOutput").ap()
    x1s_d = nc.dram_tensor("x1s", [NOWN, D], F32, kind="Internal").ap()
    h2s_d = nc.dram_tensor("h2s", [D, NOWN], BF16, kind="Internal").ap()
    w1b_d = nc.dram_tensor("w1b", [D, 4 * D], BF16, kind="Internal").ap()
    w2b_d = nc.dram_tensor("w2b", [4 * D, D], BF16, kind="Internal").ap()
    woutb_d = nc.dram_tensor("woutb", [D, D], BF16, kind="Internal").ap()

    with ExitStack() as es:
        cx = Ctx(nc, es)
        PE, ACT, DVE, POOL, SP = cx.pe, cx.act, cx.dve, cx.pool, cx.sp
        op, dma = cx.op, cx.dma

        cnt = [0]

        def sb(off_kb, shape, dt):
            cnt[0] += 1
            return T(nc.alloc_sbuf_tensor_at(f"t{cnt[0]}", list(shape), dt, offset=SB_BASE + int(off_kb * KB)))

        pairs = [es.enter_context(nc.psum_tensor(f"pp{i}", [128, 1024], F32)) for i in range(4)]
        banks = [T(pairs[i // 2][:, (i % 2) * 512:(i % 2 + 1) * 512]) for i in range(8)]
        prr = [0]

        def ps(pool=range(8)):
            pool = list(pool)
            b = banks[pool[prr[0] % len(pool)]]
            prr[0] += 1
            return b

        o = 0.0

        def small(shape, dt, nbytes):
            nonlocal o
            t = sb(o, shape, dt)
            o += (-(-nbytes // 32) * 32) / KB
            return t

        cv = small([128, 16], F32, 64)
        scv = small([128, 16], F32, 64)
        modT = small([128, 96], F32, 384)
        wdw = small([128, 124], F32, 496)
        cpar = small([128, 12], F32, 48)
        qg = small([128, 3], F32, 12 + 4)
        kvg = small([128, 2], F32, 8)
        ones_b = small([128, 128], BF16, 256)
        ident = small([128, 128], F32, 512)
        ones_f = small([128, 128], F32, 512)
        dg = small([128, 128], F32, 512)
        i2 = small([2, 2], F32, 8 + 24)
        hmask = small([128, 32], F32, 128)
        adabT = small([128, 48], F32, 192)
        lnT = small([128, 16], F32, 64)
        ab2 = small([128, 16], F32, 64)
        stat = small([128, 32], F32, 128)
        assert o * KB <= 3968, o * KB
        wuk_b = sb(4, [128, 2, 512], BF16)
        wuv_b = sb(6, [128, 2, 512], BF16)
        ckvn = sb(8, [128, 2, NK], BF16)
        QT = sb(25.5, [128, H, NOWN], BF16)
        KT0 = sb(57.5, [128, NK], BF16)
        ypad = sb(66.5, [128, 4, 2080], BF16)
        b0 = 83.5
        win_b = sb(b0, [128, 8, 1792], BF16)
        wuq_b = sb(b0 + 28, [128, 3, 2048], BF16)
        xs = [sb(b0 + 40 + 8 * i, [128, 4, 512], F32) for i in range(2)] + [sb(196, [128, 4, 512], F32)]
        hT = [sb(b0 + 56 + 8 * i, [128, 8, 512], BF16) for i in range(2)]
        craw = sb(b0 + 72, [128, 5, 512], F32)
        sq = sb(b0 + 82, [128, 5, 512], BF16)
        cqn = sb(b0 + 87, [128, 3, 512], BF16)
        ctb = [sb(b0 + 90 + 4 * i, [128, 512], F32) for i in range(2)]
        stb = [sb(b0 + 92 + 4 * i, [128, 512], F32) for i in range(2)]
        t1 = sb(b0 + 98, [128, 512], F32)
        t2 = sb(b0 + 100, [128, 512], F32)
        tB = sb(b0 + 102, [128, 512], F32)
        sig = [sb(b0 + 104 + 2 * i, [128, 512], F32) for i in range(2)]
        rs = sb(b0 + 108, [128, 512], F32)
        sd = sb(b0 + 110, [128, 512], F32)
        adaw_s = [sb(25.5 + 16 * i, [128, 8, 512], F32) for i in range(2)]
        wst = [sb(8 + 7 * i, [128, 1792], F32) for i in range(2)]
        wuq_s = sb(b0 + 72, [128, 3, 2048], F32)
        wukv_s = sb(b0 + 96, [128, 2, 1024], F32)

        cch = cx.chan()
        cl = []

        def cload(t, dst, src, eng=SP, **kw):
            dma(eng, cch, dst, src, writes=[t], **kw)
            cl.append(t)

        cvch = cx.chan()
        dma(SP, cvch, cv[:, :], cv_d, writes=[cv])
        ach = [cx.chan(), cx.chan()]
        adaw_v = adaw_d.rearrange("(k p) c -> p k c", p=128)
        for hh in range(2):
            dma(SP, ach[0], adaw_s[0][:, 4 * hh:4 * hh + 4, :], adaw_v[:, 4 * hh:4 * hh + 4, 0:512], writes=[adaw_s[0]])
        cload(adabT, adabT[:, :], adabT_d)
        cload(i2, i2[:, :], i2_d)
        cload(wdw, wdw[:, :], wdw_d)
        cload(cpar, cpar[:, :], cpar_d)
        cload(qg, qg[:, :], qg_d)
        cload(kvg, kvg[:, :], kvg_d)
        cload(ident, ident[:, :], ident_d)
        cload(hmask, hmask[:, :], hmask_d)
        cload(lnT, lnT[:, :], lnT_d)
        fin = (cch.sem, cch.n)
        for t in cl:
            t.w = fin

        op(POOL, lambda: nc.gpsimd.memset(ones_b[:, :], 1.0), writes=[ones_b])
        op(POOL, lambda: nc.gpsimd.memset(ones_f[:, :], 1.0), writes=[ones_f])
        op(ACT, lambda: nc.scalar.activation(out=scv[:, :], in_=cv[:, :], func=AF.Silu), reads=[cv], writes=[scv])

        def mod_dest(sc):
            if sc < 16:
                return sc
            if sc < 24:
                return 32 + (sc - 16)
            if sc < 40:
                return 16 + (sc - 24)
            return 40 + (sc - 40)

        def mod_mm(st_, co, sc):
            pm = ps()
            for k in range(8):
                op(PE, lambda k=k: nc.tensor.matmul(pm[0:2, 0:128], lhsT=scv[:, 2 * k:2 * k + 2], rhs=st_[:, k, co:co + 128],
                                                    start=(k == 0), stop=(k == 7)), reads=[scv, st_], writes=[pm], sig=(k == 7))
            return pm

        def mod_fin(pm, sc):
            op(DVE, lambda: nc.vector.tensor_copy(out=dg[0:2, 0:128], in_=pm[0:2, 0:128]), reads=[pm], writes=[dg])
            pt_ = ps()
            op(PE, lambda: nc.tensor.matmul(pt_[:, 0:2], lhsT=dg[0:2, 0:128], rhs=i2[0:2, 0:2], start=True, stop=True),
               reads=[dg, i2], writes=[pt_])
            j = mod_dest(sc)
            one = 1.0 if (8 <= j < 16 or 24 <= j < 32) else 0.0
            op(DVE, lambda: nc.vector.tensor_scalar(out=modT[:, 2 * j:2 * j + 2], in0=pt_[:, 0:2], scalar1=adabT[:, sc:sc + 1],
                                                    scalar2=one, op0=ALU.add, op1=ALU.add), reads=[pt_, adabT], writes=[modT])

        wsch = [cx.chan(), cx.chan()]
        xch = [cx.chan(), cx.chan(), cx.chan()]
        tch = [cx.chan(), cx.chan()]
        xT_v = xT_d.rearrange("(k p) n -> p k n", p=128)
        pendA = []

        def modA_fin(pmA, c):
            op(DVE, lambda: nc.vector.tensor_copy(out=rs[0:2, :], in_=pmA[0:2, :]), reads=[pmA], writes=[rs])
            pt_ = ps()
            for q in range(4):
                op(PE, lambda q=q: nc.tensor.matmul(pt_[:, 2 * q:2 * q + 2], lhsT=rs[0:2, q * 128:(q + 1) * 128], rhs=i2[0:2, 0:2],
                                                    start=True, stop=True), reads=[rs, i2], writes=[pt_], sig=(q == 3))
            for q in range(4):
                sc = 4 * c + q
                j = mod_dest(sc)
                one = 1.0 if (8 <= j < 16 or 24 <= j < 32) else 0.0
                op(DVE, lambda q=q, sc=sc, j=j, one=one: nc.vector.tensor_scalar(
                    out=modT[:, 2 * j:2 * j + 2], in0=pt_[:, 2 * q:2 * q + 2], scalar1=adabT[:, sc:sc + 1], scalar2=one,
                    op0=ALU.add, op1=ALU.add), reads=[pt_, adabT], writes=[modT])
        for c in range(4):
            st_ = adaw_s[c % 2]
            for hh in range(2):
                if c == 0:
                    continue
                dma(SP, ach[c % 2], st_[:, 4 * hh:4 * hh + 4, :], adaw_v[:, 4 * hh:4 * hh + 4, c * 512:(c + 1) * 512],
                    writes=[st_])
            if c == 0:
                for hh in range(2):
                    dma(SP, xch[hh], xs[hh][:, :, :], xT_v[:, 4 * hh:4 * hh + 4, 0:512], writes=[xs[hh]])
            for pc in (2 * c, 2 * c + 1):
                ws_ = wst[pc % 2]
                dma(SP, wsch[pc % 2], ws_[:, :], win_d[pc * 128:(pc + 1) * 128, :], writes=[ws_])
                op(DVE, lambda pc=pc, ws_=ws_: nc.vector.tensor_copy(out=win_b[:, pc, :], in_=ws_[:, :]),
                   reads=[ws_], writes=[win_b])
            pmA = ps()
            for k in range(8):
                op(PE, lambda k=k: nc.tensor.matmul(pmA[0:2, :], lhsT=scv[:, 2 * k:2 * k + 2], rhs=st_[:, k, :],
                                                    start=(k == 0), stop=(k == 7)), reads=[scv, st_], writes=[pmA], sig=(k == 7))
            if pendA:
                modA_fin(*pendA.pop(0))
            pendA.append((pmA, c))
        while pendA:
            modA_fin(*pendA.pop(0))
        cload2 = cx.chan()
        dma(SP, cload2, wuq_s[:, :, :], wuq_d.rearrange("(k p) c -> p k c", p=128), writes=[wuq_s])
        dma(SP, cload2, wukv_s[:, :, 0:512], wuk_d.rearrange("(k p) c -> p k c", p=128), writes=[wukv_s])
        dma(SP, cload2, wukv_s[:, :, 512:1024], wuv_d.rearrange("(k p) c -> p k c", p=128), writes=[wukv_s])
        wuq_s.w = wukv_s.w = (cload2.sem, cload2.n)

        for k in range(3):
            op(DVE, lambda k=k: nc.vector.tensor_scalar(out=wuq_b[:, k, :], in0=wuq_s[:, k, :],
                                                         scalar1=qg[:, k:k + 1], scalar2=None, op0=ALU.mult),
               reads=[wuq_s, qg], writes=[wuq_b])
        for k in range(2):
            op(DVE, lambda k=k: nc.vector.tensor_scalar(out=wuk_b[:, k, :], in0=wukv_s[:, k, 0:512],
                                                         scalar1=kvg[:, k:k + 1], scalar2=None, op0=ALU.mult),
               reads=[wukv_s, kvg], writes=[wuk_b])
            op(DVE, lambda k=k: nc.vector.tensor_scalar(out=wuv_b[:, k, :], in0=wukv_s[:, k, 512:1024],
                                                         scalar1=kvg[:, k:k + 1], scalar2=None, op0=ALU.mult),
               reads=[wukv_s, kvg], writes=[wuv_b])

        bst = [sb(196 + 4 * i, [128, 8, 128], F32) for i in range(2)]
        rowB = [sb(204 + 0.5 * i, [2, 128], F32) for i in range(2)]
        bch_ = [cx.chan(), cx.chan()]
        modB = list(range(16, 48))
        modB_pend = []

        modB_p = []

        def modB_one():
            prev = modB_p.pop(0) if modB_p else None
            if modB:
                sc = modB.pop(0)
                st_ = bst[sc % 2]
                dma(SP, bch_[sc % 2], st_[:, :, :], adaw_v[:, :, sc * 128:(sc + 1) * 128], writes=[st_])
                pm = banks[6 + (sc % 2)]
                for k in range(8):
                    op(PE, lambda k=k: nc.tensor.matmul(pm[0:2, 0:128], lhsT=scv[:, 2 * k:2 * k + 2], rhs=st_[:, k, 0:128],
                                                        start=(k == 0), stop=(k == 7)), reads=[scv, st_], writes=[pm], sig=(k == 7))
                op(DVE, lambda: nc.vector.tensor_copy(out=rowB[sc % 2][0:2, 0:128], in_=pm[0:2, 0:128]), reads=[pm], writes=[rowB[sc % 2]])
                modB_p.append(sc)
            if prev is not None:
                sc = prev
                pt_ = banks[6 + (sc % 2)]
                op(PE, lambda: nc.tensor.matmul(pt_[:, 0:2], lhsT=rowB[sc % 2][0:2, 0:128], rhs=i2[0:2, 0:2], start=True, stop=True),
                   reads=[rowB[sc % 2], i2], writes=[pt_])
                j = mod_dest(sc)
                one = 1.0 if (8 <= j < 16 or 24 <= j < 32) else 0.0
                op(DVE, lambda: nc.vector.tensor_scalar(out=modT[:, 2 * j:2 * j + 2], in0=pt_[:, 0:2], scalar1=adabT[:, sc:sc + 1],
                                                        scalar2=one, op0=ALU.add, op1=ALU.add), reads=[pt_, adabT], writes=[modT])

        def modB_batch(n):
            pend = []
            for _ in range(n):
                if not modB:
                    break
                sc = modB.pop(0)
                st_ = bst[sc % 2]
                dma(POOL, bch_[sc % 2], st_[:, :, :], adaw_v[:, :, sc * 128:(sc + 1) * 128], writes=[st_])
                pend.append((mod_mm(st_, 0, sc), sc))
                if len(pend) > 1:
                    mod_fin(*pend.pop(0))
            while pend:
                mod_fin(*pend.pop(0))

        def mod_ap(kind, k, r):
            base = {"sh1": 0, "sc1": 8, "sh2": 16, "sc2": 24}[kind]
            c = 2 * (base + k) + r
            return modT[:, c:c + 1]

        def emit_precast():
            jobs = [(woutb_d[:, :], wout_d[:, :], woutbT)]
            jobs += [(w1b_d[:, c * 512:(c + 1) * 512], w1_d[:, c * 512:(c + 1) * 512], w1bT) for c in range(8)]
            jobs += [(w2b_d[c * 512:(c + 1) * 512, :], w2_d[c * 512:(c + 1) * 512, :], w2bT) for c in range(8)]
            for i, (dst, src, tl) in enumerate(jobs):
                ch = pcw[i % 2]
                prev = [(ch.sem, ch.n)] if ch.n > 0 else []
                ev = dma(POOL, ch, dst, src, extra=prev)
                tl.evs = getattr(tl, "evs", {})
                tl.evs[id(ch.sem)] = ev

        w1bT, w2bT, woutbT = T(), T(), T()
        pcw = [cx.chan(), cx.chan()]
        for c_ in pcw:
            c_.nofence = True

        cx.fence([QT, KT0, ypad, ckvn, craw, sq, cqn, ctb[0], ctb[1], stb[0], stb[1], t1, t2, tB])
        op(POOL, lambda: nc.gpsimd.memset(KT0[64:128, :], 0.0), writes=[KT0])
        op(POOL, lambda: nc.gpsimd.memset(ypad[:, :, :], 0.0), writes=[ypad])


        def modulate(g, ntok, src_v, col0, lat):
            h_ = hT[g % 2]
            r = 0 if lat else 1
            for hh in range(2):
                x_ = xs[hh]
                dma(SP, xch[hh], x_[:, :, 0:ntok], src_v[:, 4 * hh:4 * hh + 4, col0:col0 + ntok], writes=[x_])
                for kk in range(4):
                    k = 4 * hh + kk
                    op(ACT, lambda k=k, kk=kk, x_=x_, h_=h_: nc.scalar.activation(
                        out=h_[:, k, 0:ntok], in_=x_[:, kk, 0:ntok], func=AF.Identity,
                        scale=mod_ap("sc1", k, r), bias=mod_ap("sh1", k, r)),
                       reads=[x_, modT], writes=[h_])
            return h_

        def proj(h_, m, ntok, pool=range(8)):
            p_ = ps(pool)
            for k in range(8):
                op(PE, lambda k=k, p_=p_: nc.tensor.matmul(
                    p_[:, 0:ntok], lhsT=win_b[:, k, m * 128:(m + 1) * 128], rhs=h_[:, k, 0:ntok],
                    start=(k == 0), stop=(k == 7)), reads=[win_b, h_], writes=[p_], sig=(k == 7))
            return p_

        def rms_chain(j0, nj, ntok, width):
            pss = ps()
            for j in range(nj):
                op(PE, lambda j=j: nc.tensor.matmul(pss[:, 0:ntok], lhsT=ones_b[:, :], rhs=sq[:, j0 + j, 0:ntok],
                                                    start=(j == 0), stop=(j == nj - 1)),
                   reads=[ones_b, sq], writes=[pss], sig=(j == nj - 1))
            op(ACT, lambda: nc.scalar.activation(out=sd[:, 0:ntok], in_=pss[:, 0:ntok], func=AF.Ln,
                                                 scale=1.0 / width, bias=eps_ap), reads=[pss, epsT], writes=[sd])
            op(ACT, lambda: nc.scalar.activation(out=rs[:, 0:ntok], in_=sd[:, 0:ntok], func=AF.Exp, scale=-0.5),
               reads=[sd], writes=[rs])

        epsT = T(nc.alloc_sbuf_tensor_at("epsT", [128, 2], F32, offset=SB_BASE + 3968))
        op(POOL, lambda: nc.gpsimd.memset(epsT[:, 0:1], EPS), writes=[epsT])
        op(POOL, lambda: nc.gpsimd.memset(epsT[:, 1:2], 1.0), writes=[epsT])
        eps_ap = epsT[:, 0:1]
        one_ap = epsT[:, 1:2]

        groups = [(g, 512, g * 512, True) for g in range(8)] + [(8, 256, 4096, False)]

        def mod_dma(gi):
            g, ntok, col0, lat = groups[gi]
            if gi > 0:
                for hh in range(2):
                    bi = (2 * gi + hh) % 3
                    x_ = xs[bi]
                    dma(SP, xch[bi], x_[:, :, 0:ntok], xT_v[:, 4 * hh:4 * hh + 4, col0:col0 + ntok], writes=[x_])

        def tab_dma(gi):
            g, ntok, col0, lat = groups[gi]
            if lat:
                dma(SP, tch[g % 2], ctb[g % 2][:, :], ctab_d[:, col0:col0 + 512], writes=[ctb[g % 2]])
                dma(SP, tch[g % 2], stb[g % 2][:, :], stab_d[:, col0:col0 + 512], writes=[stb[g % 2]])
                ctb[g % 2].w = stb[g % 2].w = (tch[g % 2].sem, tch[g % 2].n)

        def mod_act(gi):
            g, ntok, col0, lat = groups[gi]
            h_ = hT[g % 2]
            r = 0 if lat else 1
            for k in range(8):
                x_ = xs[(2 * gi + k // 4) % 3]
                op(ACT, lambda k=k, x_=x_: nc.scalar.activation(
                    out=h_[:, k, 0:ntok], in_=x_[:, k % 4, 0:ntok], func=AF.Identity,
                    scale=mod_ap("sc1", k, r), bias=mod_ap("sh1", k, r)), reads=[x_, modT], writes=[h_])

        def kv_chunks(h_, ntok):
            for j, m in enumerate((3, 4)):
                p_ = proj(h_, m, ntok)
                op(ACT, lambda j=j, p_=p_: nc.scalar.activation(out=craw[:, j, 0:ntok], in_=p_[:, 0:ntok], func=AF.Copy),
                   reads=[p_], writes=[craw])
                op(ACT, lambda j=j, p_=p_: nc.scalar.activation(out=sq[:, j, 0:ntok], in_=p_[:, 0:ntok], func=AF.Square),
                   reads=[p_], writes=[sq])

        def rope_chunk(h_, ntok, col0, lat, ct, st):
            pr = proj(h_, 5, ntok)
            if lat:
                op(ACT, lambda: nc.scalar.activation(out=tB[64:96, :], in_=pr[96:128, :], func=AF.Copy),
                   reads=[pr], writes=[tB])
                op(DVE, lambda: nc.vector.tensor_tensor(out=t1[64:96, :], in0=pr[64:96, :], in1=ct[64:96, :], op=ALU.mult),
                   reads=[pr, ct], writes=[t1])
                op(DVE, lambda: nc.vector.tensor_tensor(out=t2[64:96, :], in0=tB[64:96, :], in1=st[64:96, :], op=ALU.mult),
                   reads=[tB, st], writes=[t2])
                op(DVE, lambda: nc.vector.tensor_tensor(out=KT0[64:96, col0:col0 + 512], in0=t1[64:96, :], in1=t2[64:96, :],
                                                        op=ALU.add), reads=[t1, t2], writes=[KT0])
            else:
                op(ACT, lambda: nc.scalar.activation(out=KT0[96:128, col0:col0 + ntok], in_=pr[64:96, 0:ntok], func=AF.Copy),
                   reads=[pr], writes=[KT0])

        def kv_norm(ntok, col0):
            rms_chain(0, 2, ntok, 256.0)
            for j in range(2):
                op(DVE, lambda j=j: nc.vector.tensor_tensor(out=ckvn[:, j, col0:col0 + ntok], in0=craw[:, j, 0:ntok],
                                                            in1=rs[:, 0:ntok], op=ALU.mult),
                   reads=[craw, rs], writes=[ckvn])

        def q_chunks(h_):
            for j in range(3):
                p_ = proj(h_, j, 512)
                op(ACT, lambda j=j, p_=p_: nc.scalar.activation(out=craw[:, 2 + j, :], in_=p_[:, :], func=AF.Copy),
                   reads=[p_], writes=[craw])
                op(ACT, lambda j=j, p_=p_: nc.scalar.activation(out=sq[:, 2 + j, :], in_=p_[:, :], func=AF.Square),
                   reads=[p_], writes=[sq])

        def glu_chunk(h_, cc, col0):
            pa = proj(h_, 6 + cc, 512)
            pg = proj(h_, 10 + cc, 512)
            sg = sig[cc % 2]
            op(ACT, lambda: nc.scalar.activation(out=sg[:, :], in_=pg[:, :], func=AF.Sigmoid), reads=[pg], writes=[sg])
            op(DVE, lambda: nc.vector.tensor_tensor(out=ypad[:, cc, 15 + col0:15 + col0 + 512], in0=pa[:, :], in1=sg[:, :],
                                                    op=ALU.mult), reads=[pa, sg], writes=[ypad])

        def q_heads(col0, ct, st):
            for h in range(H):
                pA, pB = ps(), ps()
                for (p_, cidx) in ((pA, 2 * h), (pB, 2 * h + 1)):
                    for k in range(3):
                        op(PE, lambda k=k, p_=p_, cidx=cidx: nc.tensor.matmul(
                            p_[:, :], lhsT=wuq_b[:, k, cidx * 128:(cidx + 1) * 128], rhs=cqn[:, k, :],
                            start=(k == 0), stop=(k == 2)), reads=[wuq_b, cqn], writes=[p_], sig=(k == 2))
                op(ACT, lambda: nc.scalar.activation(out=QT[0:64, h, col0:col0 + 512], in_=pA[0:64, :], func=AF.Copy),
                   reads=[pA], writes=[QT])
                op(ACT, lambda: nc.scalar.activation(out=QT[96:128, h, col0:col0 + 512], in_=pA[96:128, :], func=AF.Copy),
                   reads=[pA], writes=[QT])
                op(DVE, lambda: nc.vector.tensor_tensor(out=t1[64:96, :], in0=pA[64:96, :], in1=ct[64:96, :], op=ALU.mult),
                   reads=[pA, ct], writes=[t1])
                op(DVE, lambda: nc.vector.tensor_tensor(out=t2[64:96, :], in0=pB[64:96, :], in1=st[64:96, :], op=ALU.mult),
                   reads=[pB, st], writes=[t2])
                op(DVE, lambda: nc.vector.tensor_tensor(out=QT[64:96, h, col0:col0 + 512], in0=t1[64:96, :], in1=t2[64:96, :],
                                                        op=ALU.add), reads=[t1, t2], writes=[QT])

        mod_dma(0)
        tab_dma(0)
        mod_act(0)
        mod_dma(1)
        for gi, (g, ntok, col0, lat) in enumerate(groups):
            own = g < 4
            h_ = hT[g % 2]
            ct, st = ctb[g % 2], stb[g % 2]
            if gi + 1 < len(groups):
                mod_act(gi + 1)
            if gi + 2 < len(groups):
                mod_dma(gi + 2)
            if gi + 1 < len(groups):
                tab_dma(gi + 1)
            kv_chunks(h_, ntok)
            rope_chunk(h_, ntok, col0, lat, ct, st)
            if own:
                q_chunks(h_)
                glu_chunk(h_, 0, col0)
                glu_chunk(h_, 1, col0)
                kv_norm(ntok, col0)
                glu_chunk(h_, 2, col0)
                rms_chain(2, 3, 512, 384.0)
                for j in range(3):
                    op(DVE, lambda j=j: nc.vector.tensor_tensor(out=cqn[:, j, :], in0=craw[:, 2 + j, :], in1=rs[:, :], op=ALU.mult),
                       reads=[craw, rs], writes=[cqn])
                glu_chunk(h_, 3, col0)
                q_heads(col0, ct, st)
            else:
                kv_norm(ntok, col0)
        xTh_v = xTh_d.rearrange("(k p) n -> p k n", p=128)
        h_ = modulate(9, 32, xTh_v, 0, True)
        for cc in range(4):
            pa = proj(h_, 6 + cc, 32)
            pg = proj(h_, 10 + cc, 32)
            sg = sig[cc % 2]
            op(ACT, lambda pg=pg, sg=sg: nc.scalar.activation(out=sg[:, 0:32], in_=pg[:, 0:32], func=AF.Sigmoid),
               reads=[pg], writes=[sg])
            op(DVE, lambda pa=pa, sg=sg: nc.vector.tensor_tensor(out=t1[:, 0:32], in0=pa[:, 0:32], in1=sg[:, 0:32], op=ALU.mult),
               reads=[pa, sg], writes=[t1])
            op(DVE, lambda cc=cc: nc.vector.tensor_tensor(out=ypad[:, cc, 0:15], in0=t1[:, 0:15], in1=hmask[:, 0:15], op=ALU.mult),
               reads=[t1, hmask], writes=[ypad])
            op(DVE, lambda cc=cc: nc.vector.tensor_tensor(out=ypad[:, cc, 15 + NOWN:30 + NOWN], in0=t1[:, 15:30],
                                                          in1=hmask[:, 15:30], op=ALU.mult),
               reads=[t1, hmask], writes=[ypad])

        a0 = 83.5
        KT1 = sb(a0, [128, NK], BF16)
        Vaug = [sb(a0 + 8.5 + 8.5 * i, [128, NKT, 128], BF16) for i in range(2)]
        PT = [sb(a0 + 25.5 + 2 * i, [128, 1024], BF16) for i in range(3)]
        rd = sb(a0 + 31.5, [128, 512], F32)
        zc = sb(a0 + 33.5, [128, 4, 512], F32)
        zb = sb(a0 + 41.5, [128, 4, 512], BF16)
        zsq = sb(a0 + 45.5, [128, 4, 512], BF16)
        cmean = sb(a0 + 49.5, [128, 512], F32)
        cmsq = sb(a0 + 51.5, [128, 512], F32)
        crs = sb(a0 + 53.5, [128, 512], F32)
        ctmp = [sb(a0 + 55.5 + 2 * i, [128, 512], F32) for i in range(2)]
        ngl = sb(a0 + 59.5, [128, 8], F32)
        attnT = sb(146, [128, 4, NOWN], BF16)
        convT = sb(162, [128, 4, NOWN], BF16)
        wout_b = sb(178, [128, 8, 1024], BF16)
        bc1 = sb(194, [128, 3, 1024], F32)
        spair = [T(pairs[1][:, :]), T(pairs[2][:, :])]
        att_tiles = [KT1, Vaug[0], Vaug[1], rd, zc, zb, zsq, cmean, cmsq, crs, ctmp[0], ctmp[1], ngl,
                     attnT, convT, wout_b] + PT + spair
        cx.fence(att_tiles)
        cx.alias([bst[0], bst[1]], [xs[2]])

        woch = cx.chan()
        KTb = [KT0, KT1]
        op(POOL, lambda: nc.gpsimd.memset(Vaug[0][:, :, 64:128], 1.0), writes=[Vaug[0]])
        op(POOL, lambda: nc.gpsimd.memset(Vaug[1][:, :, 0:64], 1.0), writes=[Vaug[1]])
        op(POOL, lambda: nc.gpsimd.tensor_copy(out=KT1[64:128, :], in_=KT0[64:128, :]), reads=[KT0], writes=[KT1])
        op(POOL, lambda: nc.gpsimd.tensor_scalar(out=ngl[:, :], in0=cpar[:, 4:12], scalar1=-1.0, scalar2=None, op0=ALU.mult),
           reads=[cpar], writes=[ngl])
        emit_precast()

        kgroups = [(i * 512, 512) for i in range(8)] + [(4096, 256)]
        MISC = [6, 7]

        def prep_k(h, c0, n):
            kt_ = KTb[h % 2]
            p_ = ps(MISC)
            for k in range(2):
                op(PE, lambda k=k: nc.tensor.matmul(
                    p_[0:64, 0:n], lhsT=wuk_b[:, k, h * 64:(h + 1) * 64], rhs=ckvn[:, k, c0:c0 + n],
                    start=(k == 0), stop=(k == 1)), reads=[wuk_b, ckvn], writes=[p_], sig=(k == 1))
            op(DVE, lambda: nc.vector.tensor_copy(out=kt_[0:64, c0:c0 + n], in_=p_[0:64, 0:n]), reads=[p_], writes=[kt_])

        def prep_v(h, t0):
            va_ = Vaug[h % 2]
            voff = 0 if h % 2 == 0 else 64
            nt = min(8, NKT - t0)
            p_ = ps(MISC)
            for j in range(nt):
                kt = t0 + j
                for k in range(2):
                    op(PE, lambda k=k, j=j, kt=kt: nc.tensor.matmul(
                        p_[:, j * 64:(j + 1) * 64], lhsT=ckvn[:, k, kt * 128:(kt + 1) * 128],
                        rhs=wuv_b[:, k, h * 64:(h + 1) * 64], start=(k == 0), stop=(k == 1), skip_group_check=True),
                       reads=[ckvn, wuv_b], writes=[p_], sig=(k == 1 and j == nt - 1))
            op(DVE, lambda: nc.vector.tensor_copy(
                out=va_[:, t0:t0 + nt, voff:voff + 64],
                in_=p_[:, 0:nt * 64].rearrange("p (t d) -> p t d", d=64)), reads=[p_], writes=[va_])

        def prep_units(h):
            us = [lambda c0=c0, n=n: prep_k(h, c0, n) for (c0, n) in kgroups]
            us += [lambda t0=t0: prep_v(h, t0) for t0 in range(0, NKT, 8)]
            return us

        tick = [0]
        defq = []

        seqc = [0]

        def at(delay, fn):
            seqc[0] += 1
            defq.append((tick[0] + delay, seqc[0], fn))

        def run_due(flush=False):
            while True:
                due = [e for e in defq if flush or e[0] <= tick[0]]
                if not due:
                    break
                due.sort()
                e = due[0]
                defq.remove(e)
                e[2]()

        zc2 = sb(178, [128, 4, 512], F32)
        zcb = [zc, zc2]
        cx.fence([zc2])

        def conv_tap(tg, cc, j):
            c0 = tg * 512
            z_ = zcb[tg % 2]
            if j == 0:
                op(DVE, lambda: nc.vector.tensor_scalar(out=z_[:, cc, :], in0=ypad[:, cc, c0:c0 + 512],
                                                        scalar1=wdw[:, cc * 31:cc * 31 + 1], scalar2=cpar[:, cc:cc + 1],
                                                        op0=ALU.mult, op1=ALU.add),
                   reads=[ypad, wdw, cpar], writes=[z_])
            else:
                op(DVE, lambda: nc.vector.scalar_tensor_tensor(
                    out=z_[:, cc, :], in0=ypad[:, cc, c0 + j:c0 + j + 512], scalar=wdw[:, cc * 31 + j:cc * 31 + j + 1],
                    in1=z_[:, cc, :], op0=ALU.mult, op1=ALU.add), reads=[ypad, z_, wdw], writes=[z_])

        def conv_sq(tg, cc):
            z_ = zcb[tg % 2]
            op(DVE, lambda: nc.vector.tensor_copy(out=zb[:, cc, :], in_=z_[:, cc, :]), reads=[z_], writes=[zb])
            op(DVE, lambda: nc.vector.tensor_tensor(out=zsq[:, cc, :], in0=z_[:, cc, :], in1=z_[:, cc, :], op=ALU.mult),
               reads=[z_], writes=[zsq])

        s12 = [None, None]

        def ln_pe():
            s12[0], s12[1] = ps(MISC), ps(MISC)
            for (s_, src) in ((s12[0], zb), (s12[1], zsq)):
                for cc in range(4):
                    op(PE, lambda cc=cc, s_=s_, src=src: nc.tensor.matmul(
                        s_[:, :], lhsT=ones_b[:, :], rhs=src[:, cc, :], start=(cc == 0), stop=(cc == 3)),
                       reads=[ones_b, src], writes=[s_], sig=(cc == 3))

        def ln_stats():
            s1, s2 = s12
            op(DVE, lambda: nc.vector.tensor_scalar(out=cmean[:, :], in0=s1[:, :], scalar1=1.0 / 512, scalar2=None, op0=ALU.mult),
               reads=[s1], writes=[cmean])
            op(DVE, lambda: nc.vector.tensor_tensor(out=cmsq[:, :], in0=cmean[:, :], in1=cmean[:, :], op=ALU.mult),
               reads=[cmean], writes=[cmsq])
            op(DVE, lambda: nc.vector.scalar_tensor_tensor(out=cmsq[:, :], in0=s2[:, :], scalar=1.0 / 512, in1=cmsq[:, :],
                                                           op0=ALU.mult, op1=ALU.subtract), reads=[s2, cmsq], writes=[cmsq])

        def ln_rstd():
            op(ACT, lambda: nc.scalar.activation(out=crs[:, :], in_=cmsq[:, :], func=AF.Ln, bias=eps_ap),
               reads=[cmsq, epsT], writes=[crs])
            op(ACT, lambda: nc.scalar.activation(out=crs[:, :], in_=crs[:, :], func=AF.Exp, scale=-0.5),
               reads=[crs], writes=[crs])

        def ln_norm(tg, cc):
            z_ = zcb[tg % 2]
            op(DVE, lambda: nc.vector.tensor_tensor(out=z_[:, cc, :], in0=z_[:, cc, :], in1=cmean[:, :], op=ALU.subtract),
               reads=[z_, cmean], writes=[z_])
            op(DVE, lambda: nc.vector.tensor_tensor(out=z_[:, cc, :], in0=z_[:, cc, :], in1=crs[:, :], op=ALU.mult),
               reads=[z_, crs], writes=[z_])

        def silu_exp(tg, cc):
            z_ = zcb[tg % 2]
            op(ACT, lambda: nc.scalar.activation(out=ctmp[cc % 2][:, :], in_=z_[:, cc, :], func=AF.Exp,
                                                 scale=ngl[:, cc:cc + 1], bias=ngl[:, 4 + cc:5 + cc]),
               reads=[z_, ngl], writes=[ctmp[cc % 2]])

        def silu_fin(tg, cc):
            c0 = tg * 512
            z_ = zcb[tg % 2]
            e_ = ctmp[cc % 2]
            op(ACT, lambda: nc.scalar.activation(out=e_[:, :], in_=e_[:, :], func=AF.Ln, bias=one_ap), reads=[e_, epsT], writes=[e_])
            op(ACT, lambda: nc.scalar.activation(out=e_[:, :], in_=e_[:, :], func=AF.Exp, scale=-1.0), reads=[e_], writes=[e_])
            op(DVE, lambda: nc.vector.tensor_scalar(out=z_[:, cc, :], in0=z_[:, cc, :], scalar1=cpar[:, 4 + cc:5 + cc],
                                                    scalar2=cpar[:, 8 + cc:9 + cc], op0=ALU.mult, op1=ALU.add),
               reads=[z_, cpar], writes=[z_])
            op(DVE, lambda: nc.vector.tensor_tensor(out=convT[:, cc, c0:c0 + 512], in0=z_[:, cc, :], in1=e_[:, :], op=ALU.mult),
               reads=[z_, e_], writes=[convT])

        conv_list = [(tg, cc, j) for tg in range(4) for cc in range(4) for j in range(31)]
        conv_done = [False]

        def load_wout():
            cx.fence([wout_b])
            dma(SP, woch, wout_b[:, :, :], woutb_d.rearrange("(k p) n -> p k n", p=128), writes=[wout_b],
                extra=list(woutbT.evs.values()))

        def emit_conv(n):
            for _ in range(n):
                if not conv_list:
                    return
                tg, cc, j = conv_list.pop(0)
                conv_tap(tg, cc, j)
                if j == 30:
                    at(12, lambda tg=tg, cc=cc: conv_sq(tg, cc))
                    if cc == 3:
                        at(20, lambda: (ln_pe(), ln_stats()))
                        at(34, ln_rstd)
                        for c2 in range(4):
                            at(40 + 4 * c2, lambda c2=c2, tg=tg: ln_norm(tg, c2))
                            at(60 + 8 * c2, lambda c2=c2, tg=tg: silu_exp(tg, c2))
                            at(66 + 8 * c2, lambda c2=c2, tg=tg: silu_fin(tg, c2))
                        if tg == 3:
                            at(100, load_wout)

        NP = NKT // 2
        it_ = [0]

        def pv(h, qg, p, pt):
            va_ = Vaug[h % 2]
            acc = banks[qg % 2]
            q0 = qg * 512
            for j in range(2):
                k2 = 2 * p + j
                op(PE, lambda j=j, k2=k2: nc.tensor.matmul(
                    acc[:, :], lhsT=va_[:, k2, :], rhs=pt[:, j * 512:(j + 1) * 512], start=(k2 == 0), stop=(k2 == NKT - 1)),
                   reads=[va_, pt], writes=[acc], sig=(j == 1))
            if p != NP - 1:
                return
            o_lo, d_lo = (0, 64) if h % 2 == 0 else (64, 0)

            def norm_act():
                op(ACT, lambda: nc.scalar.activation(out=rd[o_lo:o_lo + 64, :], in_=acc[d_lo:d_lo + 64, :], func=AF.Ln),
                   reads=[acc], writes=[rd])
                op(ACT, lambda: nc.scalar.activation(out=rd[o_lo:o_lo + 64, :], in_=rd[o_lo:o_lo + 64, :], func=AF.Exp, scale=-1.0),
                   reads=[rd], writes=[rd])

            def norm_dve():
                op(DVE, lambda: nc.vector.tensor_tensor(out=attnT[o_lo:o_lo + 64, h // 2, q0:q0 + 512], in0=acc[o_lo:o_lo + 64, :],
                                                        in1=rd[o_lo:o_lo + 64, :], op=ALU.mult), reads=[acc, rd], writes=[attnT])

            at(6, norm_act)
            at(10, norm_dve)

        for u in prep_units(0):
            u()
        steps = [(h, qg, p) for h in range(H) for qg in range(4) for p in range(NP)]
        pend = []
        units = []
        gstep = 0
        for (h, qg, p) in steps:
            if p == 0 and qg == 1 and h + 1 < H:
                assert not units
                units = prep_units(h + 1)
            kt_ = KTb[h % 2]
            q0 = qg * 512
            s_ = spair[p % 2]
            for j in range(2):
                kt = 2 * p + j
                op(PE, lambda j=j, kt=kt: nc.tensor.matmul(
                    s_[:, j * 512:(j + 1) * 512], lhsT=kt_[:, kt * 128:(kt + 1) * 128], rhs=QT[:, h, q0:q0 + 512],
                    start=True, stop=True), reads=[kt_, QT], writes=[s_], sig=(j == 1))
            pt = PT[gstep % 3]
            gstep += 1
            op(ACT, lambda: nc.scalar.activation(out=pt[:, :], in_=s_[:, :], func=AF.Exp, scale=SM_SCALE),
               reads=[s_], writes=[pt])
            pend.append((h, qg, p, pt))
            if len(pend) > 2:
                pv(*pend.pop(0))
            if units and (it_[0] % 3 == 0 or (qg == 3 and NP - p <= len(units))):
                units.pop(0)()
            elif it_[0] % 12 == 6:
                modB_one()
            it_[0] += 1
            emit_conv(2 if it_[0] % 8 == 0 else 1)
            tick[0] += 2
            run_due()
        while pend:
            pv(*pend.pop(0))
        assert not units
        while modB or modB_p:
            modB_one()
        emit_conv(10 ** 6)
        run_due(flush=True)
        cx.alias([bc1], [bst[0], bst[1], rowB[0], rowB[1]])
        cx.fence([banks[2], banks[3], banks[4], banks[5]])

        w1_b = sb(4, [128, 8, 4096], BF16)
        w2_b = sb(68, [128, 32, 1024], BF16)
        xa = [sb(132 + 4 * i, [128, 1024], F32) for i in range(3)]
        tsc = sb(144, [128, 512], F32)
        h2st = [sb(124 + 2 * i, [128, 8, 128], BF16) for i in range(2)]
        cx.fence([w1_b, w2_b, xa[0], xa[1], xa[2], tsc, h2st[0], h2st[1]])
        w1ch = [cx.chan() for _ in range(8)]
        w2ch = [cx.chan() for _ in range(8)]
        w1t = [T() for _ in range(8)]
        w2t = [T() for _ in range(8)]
        w1b_v = w1b_d.rearrange("(k p) n -> p k n", p=128)
        w2b_v = w2b_d.rearrange("(f p) n -> p f n", p=128)
        snap_f = list(cx.snapshot().values())

        def load_mlp_w(c):
            dma(POOL, w1ch[c], w1_b[:, :, c * 512:(c + 1) * 512], w1b_v[:, :, c * 512:(c + 1) * 512],
                writes=[w1t[c]], extra=list(w1bT.evs.values()) + (snap_f if c == 0 else []))
            if c < 7:
                dma(POOL, w2ch[c], w2_b[:, 4 * c:4 * c + 4, :], w2b_v[:, 4 * c:4 * c + 4, :], writes=[w2t[c]],
                    extra=list(w2bT.evs.values()))
            else:
                al = []
                for t_ in h2st:
                    al += ([t_.w] if t_.w else []) + list(t_.r.values())
                dma(SP, w2ch[c], w2_b[:, 4 * c:4 * c + 4, :], w2b_v[:, 4 * c:4 * c + 4, :], writes=[w2t[c]],
                    extra=list(w2bT.evs.values()) + al)

        def bcast_gate(dst_tile, dst_slot, jbase, scratch):
            for c in range(8):
                col = 2 * (jbase + c)
                op(DVE, lambda c=c, col=col: nc.vector.tensor_scalar(out=scratch[1](slice(c * 128, (c + 1) * 128)), in0=ident[:, :],
                                                                     scalar1=modT[:, col:col + 1], scalar2=None, op0=ALU.mult),
                   reads=[ident, modT], writes=[scratch[0]])
            for hf in range(2):
                p_ = ps(range(6, 8))
                op(PE, lambda hf=hf, p_=p_: nc.tensor.matmul(p_[:, :], lhsT=ones_f[:, :], rhs=scratch[1](slice(hf * 512, (hf + 1) * 512)),
                                                              start=True, stop=True), reads=[ones_f, scratch[0]], writes=[p_])
                op(ACT, lambda hf=hf, p_=p_: nc.scalar.activation(out=dst_tile[:, dst_slot, hf * 512:(hf + 1) * 512], in_=p_[:, :],
                                                                  func=AF.Copy), reads=[p_], writes=[dst_tile])

        bch = cx.chan()
        bcast_gate(bc1, 0, 32, (xa[2], lambda sl: xa[2][:, sl]))
        mv = modT[:, 0:96].rearrange("p (j r) -> p j r", r=2)
        op(DVE, lambda: nc.vector.tensor_tensor(out=ab2[:, 0:8], in0=lnT[:, 0:8], in1=mv[:, 24:32, 0], op=ALU.mult),
           reads=[lnT, modT], writes=[ab2])
        op(DVE, lambda: nc.vector.tensor_tensor(out=ab2[:, 8:16], in0=lnT[:, 8:16], in1=mv[:, 24:32, 0], op=ALU.mult),
           reads=[lnT, modT], writes=[ab2])
        op(DVE, lambda: nc.vector.tensor_tensor(out=ab2[:, 8:16], in0=ab2[:, 8:16], in1=mv[:, 16:24, 0], op=ALU.add),
           reads=[ab2, modT], writes=[ab2])

        x1Tt = [T() for _ in range(16)]
        h2Tt = [T() for _ in range(16)]
        xrch = [cx.chan(), cx.chan(), cx.chan()]
        sch = [cx.chan(), cx.chan()]
        hch = [cx.chan(), cx.chan()]
        h2s_v = h2s_d.rearrange("(k p) n -> p k n", p=128)

        def layer_norm(xt, xv, gi_ap, bi_ap, gb_tile, st0, affine=True):
            for hf in range(2):
                op(DVE, lambda hf=hf: nc.vector.bn_stats(out=stat[:, st0 + hf * 6:st0 + hf * 6 + 6],
                                                         in_=xv(slice(hf * 512, (hf + 1) * 512))), reads=[xt], writes=[stat])
            op(DVE, lambda: nc.vector.bn_aggr(out=stat[:, st0 + 12:st0 + 14], in_=stat[:, st0:st0 + 12]), reads=[stat], writes=[stat])
            op(ACT, lambda: nc.scalar.activation(out=stat[:, st0 + 14:st0 + 15], in_=stat[:, st0 + 13:st0 + 14], func=AF.Sqrt,
                                                 bias=eps_ap), reads=[stat, epsT], writes=[stat])
            op(DVE, lambda: nc.vector.reciprocal(out=stat[:, st0 + 15:st0 + 16], in_=stat[:, st0 + 14:st0 + 15]),
               reads=[stat], writes=[stat])
            op(DVE, lambda: nc.vector.tensor_scalar(out=stat[:, st0 + 14:st0 + 15], in0=stat[:, st0 + 12:st0 + 13],
                                                    scalar1=stat[:, st0 + 15:st0 + 16], scalar2=-1.0, op0=ALU.mult, op1=ALU.mult),
               reads=[stat], writes=[stat])
            if affine:
                op(ACT, lambda: nc.scalar.activation(out=xv(slice(0, 1024)), in_=xv(slice(0, 1024)), func=AF.Identity,
                                                     scale=stat[:, st0 + 15:st0 + 16], bias=stat[:, st0 + 14:st0 + 15]),
                   reads=[xt, stat], writes=[xt])
            else:
                op(DVE, lambda: nc.vector.tensor_scalar(out=xv(slice(0, 1024)), in0=xv(slice(0, 1024)),
                                                        scalar1=stat[:, st0 + 15:st0 + 16], scalar2=stat[:, st0 + 14:st0 + 15],
                                                        op0=ALU.mult, op1=ALU.add), reads=[xt, stat], writes=[xt])
            if affine:
                op(DVE, lambda: nc.vector.tensor_tensor(out=xv(slice(0, 1024)), in0=xv(slice(0, 1024)), in1=gi_ap, op=ALU.mult),
                   reads=[xt, gb_tile], writes=[xt])
                op(DVE, lambda: nc.vector.tensor_tensor(out=xv(slice(0, 1024)), in0=xv(slice(0, 1024)), in1=bi_ap, op=ALU.add),
                   reads=[xt, gb_tile], writes=[xt])

        bc2 = sb(146, [128, 5, 1024], F32)
        x1t = [sb(166 + 8 * i, [128, 2, 1024], F32) for i in range(2)]
        h2g = [sb(182 + 4 * i, [128, 8, 256], BF16) for i in range(2)]
        rr = [sb(190 + i, [128, 256], F32) for i in range(3)]
        h1 = [sb(193 + 0.5 * i, [128, 256], BF16) for i in range(3)]
        tsc2 = sb(195, [128, 512], F32)
        astg = sb(197, [128, 2, 1024], F32)
        lch = [cx.chan(), cx.chan()]
        lhch = [cx.chan(), cx.chan()]
        b2ch = cx.chan()

        def load_h(g):
            dma(SP, lhch[g % 2], h2g[g % 2][:, :, :], h2s_v[:, :, g * 256:(g + 1) * 256], reads=[h2Tt[2 * g], h2Tt[2 * g + 1]],
                writes=[h2g[g % 2]])

        def load_x(g):
            for tt in range(2):
                dma(SP, lch[g % 2], x1t[g % 2][:, tt, :], x1s_d[g * 256 + tt * 128:g * 256 + (tt + 1) * 128, :],
                    reads=[x1Tt[2 * g + tt]], writes=[x1t[g % 2]])

        def mlp_prefetch_1():
            cx.alias([bc2, x1t[0], x1t[1], h2g[0], h2g[1]], [attnT, convT, wout_b, zc2])
            dma(SP, b2ch, bc2[:, 1, :], lnp_d[2:3, :].partition_broadcast(128), writes=[bc2])
            dma(SP, b2ch, bc2[:, 2, :], lnp_d[3:4, :].partition_broadcast(128), writes=[bc2])
            dma(SP, b2ch, bc2[:, 3, :], lnp_d[0:1, :].partition_broadcast(128), writes=[bc2])
            dma(SP, b2ch, bc2[:, 4, :], lnp_d[1:2, :].partition_broadcast(128), writes=[bc2])
            load_h(0)
            load_x(0)
            load_x(1)

        def mlp_prefetch_2():
            cx.alias([tsc2, astg] + rr + h1, [bc1, wout_b])
            bcast_gate(bc2, 0, 40, (astg, lambda sl: astg[:, 0, sl]))

        pa_ps = {}

        def stage_A(tt):
            xt = xa[tt % 3]
            r0 = tt * 128
            dma(SP, xrch[tt % 3], xt[:, :], xown_d[r0:r0 + 128, :], writes=[xt])
            pa_ps[tt] = []
            for hf in range(2):
                p_ = banks[2 * (tt % 3) + hf]
                pa_ps[tt].append(p_)
                for k in range(8):
                    src = attnT if k < 4 else convT
                    op(PE, lambda k=k, src=src: nc.tensor.matmul(
                        p_[:, :], lhsT=src[:, k % 4, r0:r0 + 128], rhs=wout_b[:, k, hf * 512:(hf + 1) * 512],
                        start=(k == 0), stop=(k == 7)), reads=[src, wout_b], writes=[p_], sig=(k == 7))

        def stage_B(tt):
            xt = xa[tt % 3]
            r0 = tt * 128
            for hf in range(2):
                p_ = pa_ps[tt][hf]
                op(DVE, lambda: nc.vector.tensor_tensor(out=tsc[:, :], in0=p_[:, :], in1=bc1[:, 0, hf * 512:(hf + 1) * 512],
                                                        op=ALU.mult), reads=[p_, bc1], writes=[tsc])
                op(DVE, lambda: nc.vector.scalar_tensor_tensor(
                    out=xt[:, hf * 512:(hf + 1) * 512], in0=xt[:, hf * 512:(hf + 1) * 512], scalar=ALPHA, in1=tsc[:, :],
                    op0=ALU.mult, op1=ALU.add), reads=[xt, tsc], writes=[xt])
            layer_norm(xt, lambda s_: xt[:, s_], None, None, bc1, 16 * (tt % 2), affine=False)
            dma(SP, sch[tt % 2], x1s_d[r0:r0 + 128, :], xt[:, :], reads=[xt], writes=[x1Tt[tt]])

        def stage_C(tt):
            xt = xa[tt % 3]
            r0 = tt * 128
            hs = h2st[tt % 2]
            for half in range(2):
                p_ = banks[6 + half]
                for kk in range(4):
                    k = 4 * half + kk
                    op(PE, lambda k=k, kk=kk: nc.tensor.transpose(
                        out=p_[:, kk * 128:(kk + 1) * 128], in_=xt[:, k * 128:(k + 1) * 128], identity=ident[:, :]),
                       reads=[xt, ident], writes=[p_], sig=(kk == 3))
                for kk in range(4):
                    k = 4 * half + kk
                    op(ACT, lambda k=k, kk=kk: nc.scalar.activation(
                        out=hs[:, k, :], in_=p_[:, kk * 128:(kk + 1) * 128], func=AF.Identity,
                        scale=ab2[:, k:k + 1], bias=ab2[:, 8 + k:9 + k]), reads=[p_, ab2], writes=[hs])
            dma(SP, hch[tt % 2], h2s_v[:, :, r0:r0 + 128], hs[:, :, :], reads=[hs], writes=[h2Tt[tt]])

        stage_A(0)
        stage_A(1)
        stage_B(0)
        for tt in range(16):
            if tt + 2 < 16:
                stage_A(tt + 2)
            if tt + 1 < 16:
                stage_B(tt + 1)
            stage_C(tt)
            if tt % 2 == 1:
                load_mlp_w(tt // 2)
            if tt == 13:
                mlp_prefetch_1()
            if tt == 14:
                mlp_prefetch_2()

        for tt_ in range(16):
            x1Tt[tt_].w = (sch[tt_ % 2].sem, sch[tt_ % 2].n)
            h2Tt[tt_].w = (hch[tt_ % 2].sem, hch[tt_ % 2].n)
        och = [cx.chan(), cx.chan()]
        outT = T()

        def epilogue_pieces(g):
            xg = x1t[g % 2]
            c0 = g * 256
            pcs = []
            for tt in range(2):
                st0 = 16 * tt
                xv = lambda s_, tt=tt: xg[:, tt, s_]

                def p1(tt=tt, xv=xv, st0=st0):
                    op(DVE, lambda: nc.vector.tensor_tensor(out=xg[:, tt, :], in0=xg[:, tt, :], in1=bc2[:, 3, :], op=ALU.mult),
                       reads=[xg, bc2], writes=[xg])
                    op(DVE, lambda: nc.vector.tensor_tensor(out=xg[:, tt, :], in0=xg[:, tt, :], in1=bc2[:, 4, :], op=ALU.add),
                       reads=[xg, bc2], writes=[xg])
                    op(DVE, lambda: nc.vector.tensor_tensor(out=astg[:, tt, :], in0=astg[:, tt, :], in1=bc2[:, 0, :], op=ALU.mult),
                       reads=[astg, bc2], writes=[astg])
                    op(DVE, lambda: nc.vector.scalar_tensor_tensor(out=xg[:, tt, :], in0=xg[:, tt, :], scalar=ALPHA,
                                                                   in1=astg[:, tt, :], op0=ALU.mult, op1=ALU.add),
                       reads=[xg, astg], writes=[xg])
                    for hf in range(2):
                        op(DVE, lambda hf=hf: nc.vector.bn_stats(out=stat[:, st0 + hf * 6:st0 + hf * 6 + 6],
                                                                 in_=xv(slice(hf * 512, (hf + 1) * 512))), reads=[xg], writes=[stat])
                    op(DVE, lambda: nc.vector.bn_aggr(out=stat[:, st0 + 12:st0 + 14], in_=stat[:, st0:st0 + 12]),
                       reads=[stat], writes=[stat])

                def p2(st0=st0):
                    op(ACT, lambda: nc.scalar.activation(out=stat[:, st0 + 14:st0 + 15], in_=stat[:, st0 + 13:st0 + 14], func=AF.Sqrt,
                                                         bias=eps_ap), reads=[stat, epsT], writes=[stat])
                    op(DVE, lambda: nc.vector.reciprocal(out=stat[:, st0 + 15:st0 + 16], in_=stat[:, st0 + 14:st0 + 15]),
                       reads=[stat], writes=[stat])
                    op(DVE, lambda: nc.vector.tensor_scalar(out=stat[:, st0 + 14:st0 + 15], in0=stat[:, st0 + 12:st0 + 13],
                                                            scalar1=stat[:, st0 + 15:st0 + 16], scalar2=-1.0, op0=ALU.mult,
                                                            op1=ALU.mult), reads=[stat], writes=[stat])

                def p3(tt=tt, xv=xv, st0=st0):
                    op(ACT, lambda: nc.scalar.activation(out=xv(slice(0, 1024)), in_=xv(slice(0, 1024)), func=AF.Identity,
                                                         scale=stat[:, st0 + 15:st0 + 16], bias=stat[:, st0 + 14:st0 + 15]),
                       reads=[xg, stat], writes=[xg])

                def p4(tt=tt, xv=xv):
                    op(DVE, lambda: nc.vector.tensor_tensor(out=xv(slice(0, 1024)), in0=xv(slice(0, 1024)), in1=bc2[:, 1, :],
                                                            op=ALU.mult), reads=[xg, bc2], writes=[xg])
                    op(DVE, lambda: nc.vector.tensor_tensor(out=xv(slice(0, 1024)), in0=xv(slice(0, 1024)), in1=bc2[:, 2, :],
                                                            op=ALU.add), reads=[xg, bc2], writes=[xg])
                    dma(SP, och[g % 2], out_d[c0 + tt * 128:c0 + (tt + 1) * 128, :], xg[:, tt, :], reads=[xg], writes=[outT])

                base = 1 + 10 * tt
                pcs += [(base, p1), (base + 6, p2), (base + 9, p3), (base + 12, p4)]
            if g + 2 < 8:
                pcs.append((26, lambda: load_x(g + 2)))
            pcs.sort(key=lambda e: e[0])
            return pcs

        pend_epi = []
        for g in range(8):
            xg, hg = x1t[g % 2], h2g[g % 2]
            c0 = g * 256
            if g + 1 < 8:
                load_h(g + 1)
            accs = [[banks[2 * tt + hf] for hf in range(2)] for tt in range(2)]

            def down(f):
                hb = h1[f % 3]
                for tt in range(2):
                    for hf in range(2):
                        op(PE, lambda tt=tt, hf=hf: nc.tensor.matmul(
                            accs[tt][hf][:, :], lhsT=hb[:, tt * 128:(tt + 1) * 128], rhs=w2_b[:, f, hf * 512:(hf + 1) * 512],
                            start=(f == 0), stop=(f == 31)), reads=[hb, w2t[f // 4]], writes=[accs[tt][hf]])

            for f in range(32):
                pz = ps(range(4, 8))
                for k in range(8):
                    op(PE, lambda k=k: nc.tensor.matmul(pz[:, 0:256], lhsT=w1_b[:, k, f * 128:(f + 1) * 128], rhs=hg[:, k, :],
                                                        start=(k == 0), stop=(k == 7)),
                       reads=[w1t[f // 4], hg], writes=[pz], sig=(k == 7))
                op(ACT, lambda: nc.scalar.activation(out=rr[f % 3][:, :], in_=pz[:, 0:256], func=AF.Relu),
                   reads=[pz], writes=[rr[f % 3]])
                op(POOL, lambda: nc.gpsimd.tensor_tensor(out=h1[f % 3][:, :], in0=rr[f % 3][:, :], in1=rr[f % 3][:, :], op=ALU.mult),
                   reads=[rr[f % 3]], writes=[h1[f % 3]])
                if f >= 2:
                    down(f - 2)
                while pend_epi and pend_epi[0][0] <= f:
                    pend_epi.pop(0)[1]()
            down(30)
            down(31)
            while pend_epi:
                pend_epi.pop(0)[1]()
            for tt in range(2):
                for hf in range(2):
                    a_ = accs[tt][hf]
                    op(ACT, lambda: nc.scalar.activation(out=astg[:, tt, hf * 512:(hf + 1) * 512], in_=a_[:, :], func=AF.Copy),
                       reads=[a_], writes=[astg])
            pend_epi = epilogue_pieces(g)
        while pend_epi:
            pend_epi.pop(0)[1]()
        for ch in och:
            SP.h.wait_ge(ch.sem, ch.n)
    return nc


def _rope_tables(pos):
    f32 = np.float32
    freqs = (np.float32(10000.0) ** (-(np.arange(0, 16, 2, dtype=f32)) / f32(16))).astype(f32)
    row = (pos // 64).astype(f32)
    col = (pos % 64).astype(f32)
    ar = (row[:, None] * freqs[None, :]).astype(f32)
    ac = (col[:, None] * freqs[None, :]).astype(f32)
    cosr, sinr, cosc, sinc = np.cos(ar), np.sin(ar), np.cos(ac), np.sin(ac)
    cos = np.concatenate([cosr, cosr, cosc, cosc], 1)
    sin = np.concatenate([-sinr, sinr, -sinc, sinc], 1)
    return cos.T.astype(f32), sin.T.astype(f32)


_PERM32 = np.array([(i // 16) * 16 + ((i % 16) + 8) % 16 for i in range(32)])


def kernel(x, c, ctx, c_ctx, ada_w, ada_b, w_in, q_norm_g, kv_norm_g, w_uq, w_uk, w_uv,
           w_dw, b_dw, conv_ln_g, conv_ln_b, w_out, ln1_g, ln1_b, w1, w2, ln2_g, ln2_b):
    f32 = np.float32
    A = lambda a: np.ascontiguousarray(np.asarray(a, dtype=f32))
    x, c, ctx, c_ctx = A(x), A(c), A(ctx), A(c_ctx)
    ada_w, ada_b, w_in = A(ada_w)[0], A(ada_b)[0], A(w_in)[0]
    w_uq, w_uk, w_uv = A(w_uq)[0], A(w_uk)[0], A(w_uv)[0]
    w_dw, w_out, w1, w2 = A(w_dw)[0], A(w_out)[0], A(w1)[0], A(w2)[0]
    kr = w_in[:, 640:672]
    w_in_e = np.concatenate([w_in[:, 0:640], np.zeros((D, 64), f32), kr, kr[:, _PERM32], w_in[:, 672:1696]], 1)
    blocks = []
    for h in range(H):
        qn = w_uq[:, h * 96:h * 96 + 64]
        qr = w_uq[:, h * 96 + 64:h * 96 + 96]
        blocks.append(np.concatenate([qn, qr, qr], 1))
        blocks.append(np.concatenate([np.zeros((384, 64), f32), qr[:, _PERM32], np.zeros((384, 32), f32)], 1))
    w_uq_e = np.ascontiguousarray(np.concatenate(blocks, 1))
    pk = lambda v, k: np.ascontiguousarray(v.reshape(k, 128).T)
    qg = pk(A(q_norm_g)[0], 3)
    kvg = pk(A(kv_norm_g)[0], 2)
    wdw = np.ascontiguousarray(w_dw.T.reshape(4, 128, 31).transpose(1, 0, 2).reshape(128, 124))
    cpar = np.ascontiguousarray(np.concatenate([pk(A(b_dw)[0], 4), pk(A(conv_ln_g)[0], 4), pk(A(conv_ln_b)[0], 4)], 1))
    lnp = np.ascontiguousarray(np.stack([A(ln1_g)[0], A(ln1_b)[0], A(ln2_g)[0], A(ln2_b)[0]], 0))
    adabT = np.ascontiguousarray(ada_b.reshape(48, 128).T)
    i2 = np.eye(2, dtype=f32)
    lnT = np.ascontiguousarray(np.concatenate([pk(A(ln1_g)[0], 8), pk(A(ln1_b)[0], 8)], 1))
    ident = np.eye(128, dtype=f32)
    shared = dict(ada_w=ada_w, adabT=adabT, w_in_e=np.ascontiguousarray(w_in_e), w_uq_e=w_uq_e, w_uk=w_uk, w_uv=w_uv,
                  qg=qg, kvg=kvg, wdw=wdw, cpar=cpar, w_out=w_out, w1=w1, w2=w2, lnp=lnp, i2=i2, ident=ident, lnT=lnT)
    in_maps = []
    for core in range(8):
        b, half = core // 2, core % 2
        own = slice(half * NOWN, (half + 1) * NOWN)
        oth = slice((1 - half) * NOWN, (2 - half) * NOWN)
        xb = x[b]
        xT = np.ascontiguousarray(np.concatenate([xb[own], xb[oth], ctx[b]], 0).T)
        xown = np.ascontiguousarray(xb[own])
        pos = np.concatenate([np.arange(half * NOWN, (half + 1) * NOWN), np.arange((1 - half) * NOWN, (2 - half) * NOWN)])
        cos, sin = _rope_tables(pos)
        ctab = np.zeros((128, SEQ), f32)
        stab = np.zeros((128, SEQ), f32)
        ctab[64:96] = cos
        stab[64:96] = sin
        hal = np.zeros((32, D), f32)
        hm = np.zeros((128, 32), f32)
        s0 = half * NOWN
        for j in range(15):
            pl = s0 - 15 + j
            if 0 <= pl < SEQ:
                hal[j] = xb[pl]
                hm[:, j] = 1.0
            pr_ = s0 + NOWN + j
            if 0 <= pr_ < SEQ:
                hal[15 + j] = xb[pr_]
                hm[:, 15 + j] = 1.0
        cvv = np.zeros((128, 16), f32)
        cvv[:, 0::2] = pk(c[b], 8)
        cvv[:, 1::2] = pk(c_ctx, 8)
        m = dict(shared)
        m.update(xT=xT, xown=xown, xTh=np.ascontiguousarray(hal.T), hmask=hm, cv=cvv, ctab=ctab, stab=stab)
        in_maps.append(m)
    nc = build_program()
    res = run_bass_kernel_spmd(nc, in_maps, core_ids=list(range(8)))
    out = np.zeros((4, SEQ, D), f32)
    for core in range(8):
        b, half = core // 2, core % 2
        out[b, half * NOWN:(half + 1) * NOWN] = res.results[core]["out"]
    return out
```
